# Optimizing a Trainium2 kernel written in Bass

```python
import math
import jax, jax.numpy as jnp
from jax import lax
import numpy as np

D_MODEL = 1024
BATCH = 8
SEQ = 4096
DEPTH = 1

N_META = 16
BLOCK = 128
MLA_HEADS = 8
QK_NOPE = 64
QK_ROPE = 32
V_HEAD = 64
Q_LORA = 256
KV_LORA = 128
ROPE_THETA = 10000.0
MLA_WIDTH = MLA_HEADS * V_HEAD
CONV_CH = 512
CONV_K = 31
CONV_WIDTH = CONV_CH
MIX_WIDTH = MLA_WIDTH + CONV_WIDTH
IN_COLS = 2 * CONV_CH + Q_LORA + KV_LORA + QK_ROPE
N_KEYS = 128
N_EXPERTS = N_KEYS * N_KEYS
PEER_HEADS = 8
PEER_DK = 128
PEER_DK_HALF = PEER_DK // 2
PEER_TOPK = 16
PEER_CHUNK = 128
NORM_EPS = 1e-6

kernel_name = "hymba_conformer_mla_peer_block"


def rmsnorm(x, g):
    xf = x.astype(jnp.float32)
    y = xf * lax.rsqrt(jnp.mean(xf * xf, axis=-1, keepdims=True) + NORM_EPS)
    return (y * g.astype(jnp.float32)).astype(x.dtype)


def layernorm(x, g, b):
    xf = x.astype(jnp.float32)
    mu = jnp.mean(xf, axis=-1, keepdims=True)
    var = jnp.mean(jnp.square(xf - mu), axis=-1, keepdims=True)
    y = (xf - mu) * lax.rsqrt(var + NORM_EPS)
    return (y * g.astype(jnp.float32) + b.astype(jnp.float32)).astype(x.dtype)


def rope(x, pos):
    half = x.shape[-1] // 2
    freqs = ROPE_THETA ** (-jnp.arange(half, dtype=jnp.float32) / half)
    ang = pos.astype(jnp.float32)[:, None] * freqs[None, :]
    cos = jnp.cos(ang)[None, :, None, :].astype(x.dtype)
    sin = jnp.sin(ang)[None, :, None, :].astype(x.dtype)
    x1, x2 = x[..., :half], x[..., half:]
    return jnp.concatenate([x1 * cos - x2 * sin, x2 * cos + x1 * sin], axis=-1)


def causal_block_attention(q, k, v):
    B, Tp, H, Dq = q.shape
    nb = Tp // BLOCK
    qb = q.reshape(B, nb, BLOCK, H, Dq).transpose(1, 0, 2, 3, 4)
    kpos = jnp.arange(Tp)
    scale = 1.0 / math.sqrt(Dq)

    def one(args):
        qblk, i = args
        s = jnp.einsum('bqhd,bkhd->bhqk', qblk, k).astype(jnp.float32) * scale
        qpos = i * BLOCK + jnp.arange(BLOCK)
        mask = kpos[None, :] <= qpos[:, None]
        s = jnp.where(mask[None, None], s, -jnp.inf)
        p = jax.nn.softmax(s, axis=-1).astype(v.dtype)
        return jnp.einsum('bhqk,bkhd->bqhd', p, v)

    o = lax.map(one, (qb, jnp.arange(nb)))
    return o.transpose(1, 0, 2, 3, 4).reshape(B, Tp, H, v.shape[-1])


def mla_group(c_q, c_kv, k_rope_in, pos, g_q, w_uq, g_kv, w_ukv):
    B, Tp, _ = c_q.shape
    q = (rmsnorm(c_q, g_q) @ w_uq).reshape(B, Tp, MLA_HEADS, QK_NOPE + QK_ROPE)
    q = jnp.concatenate([q[..., :QK_NOPE], rope(q[..., QK_NOPE:], pos)], axis=-1)
    kv = (rmsnorm(c_kv, g_kv) @ w_ukv).reshape(B, Tp, MLA_HEADS, QK_NOPE + V_HEAD)
    k_nope, v = kv[..., :QK_NOPE], kv[..., QK_NOPE:]
    k_r = rope(k_rope_in[:, :, None, :], pos)
    k = jnp.concatenate([k_nope, jnp.broadcast_to(k_r, (B, Tp, MLA_HEADS, QK_ROPE))], axis=-1)
    o = causal_block_attention(q, k, v)
    return o.reshape(B, Tp, MLA_WIDTH)


def conv_group(a, b, conv_w, conv_b, g_ln, b_ln):
    h = a * jax.nn.sigmoid(b)
    h = lax.conv_general_dilated(h, conv_w[:, None, :].astype(h.dtype), window_strides=(1,),
                                 padding=[(CONV_K - 1, 0)],
                                 dimension_numbers=('NWC', 'WIO', 'NWC'),
                                 feature_group_count=CONV_CH) + conv_b
    h = layernorm(h, g_ln, b_ln)
    return jax.nn.silu(h)


def peer_ffn(xn, wq, keys, u_tab, v_tab):
    B, Tp, D = xn.shape
    xc = xn.reshape(-1, PEER_CHUNK, D)
    K = PEER_TOPK

    def one(xb):
        C = xb.shape[0]
        q = (xb @ wq).reshape(C, PEER_HEADS, 2, PEER_DK_HALF)
        s = jnp.einsum('chpd,hpnd->chpn', q, keys).astype(jnp.float32)
        sv, si = lax.top_k(s, K)
        cand = sv[:, :, 0, :, None] + sv[:, :, 1, None, :]
        cidx = si[:, :, 0, :, None] * N_KEYS + si[:, :, 1, None, :]
        top_s, sel = lax.top_k(cand.reshape(C, PEER_HEADS, K * K), K)
        eidx = jnp.take_along_axis(cidx.reshape(C, PEER_HEADS, K * K), sel, axis=-1)
        g = jax.nn.softmax(top_s, axis=-1)
        u = u_tab[eidx]
        act = jax.nn.gelu(jnp.einsum('chkd,cd->chk', u, xb).astype(jnp.float32), approximate=False)
        w = (g * act).astype(xb.dtype)
        return jnp.einsum('chk,chkd->cd', w, v_tab[eidx])

    return lax.map(one, xc).reshape(B, Tp, D)


def setup_inputs(seed: int = 0) -> dict:
    key = jax.random.key(seed)
    ks = jax.random.split(key, 24)
    f32 = jnp.float32
    L, D = DEPTH, D_MODEL

    def nrm(k, shape, scale):
        return jax.random.normal(k, shape, f32) * scale

    def gain(k, shape):
        return 1.0 + 0.02 * jax.random.normal(k, shape, f32)

    return {
        "x": jax.random.normal(ks[0], (BATCH, SEQ, D), f32),
        "meta": nrm(ks[1], (N_META, D), 1.0),
        "g_mix_norm": gain(ks[2], (L, D)),
        "w_in": nrm(ks[3], (L, D, IN_COLS), D ** -0.5),
        "g_q": gain(ks[4], (L, Q_LORA)),
        "w_uq": nrm(ks[5], (L, Q_LORA, MLA_HEADS * (QK_NOPE + QK_ROPE)), Q_LORA ** -0.5),
        "g_kv": gain(ks[6], (L, KV_LORA)),
        "w_ukv": nrm(ks[7], (L, KV_LORA, MLA_HEADS * (QK_NOPE + V_HEAD)), KV_LORA ** -0.5),
        "conv_w": nrm(ks[8], (L, CONV_K, CONV_CH), CONV_K ** -0.5),
        "conv_b": nrm(ks[9], (L, CONV_CH), 0.02),
        "g_conv_ln": gain(ks[10], (L, CONV_CH)),
        "b_conv_ln": nrm(ks[11], (L, CONV_CH), 0.02),
        "g_out_attn": gain(ks[12], (L, MLA_WIDTH)),
        "g_out_conv": gain(ks[13], (L, CONV_WIDTH)),
        "w_out": nrm(ks[14], (L, MIX_WIDTH, D), MIX_WIDTH ** -0.5),
        "g_ffn_norm": gain(ks[15], (L, D)),
        "peer_wq": nrm(ks[16], (L, D, PEER_HEADS * PEER_DK), D ** -0.5),
        "peer_keys": nrm(ks[17], (L, PEER_HEADS, 2, N_KEYS, PEER_DK_HALF), PEER_DK_HALF ** -0.5),
        "peer_u": nrm(ks[18], (L, N_EXPERTS, D), D ** -0.5),
        "peer_v": nrm(ks[19], (L, N_EXPERTS, D), (PEER_HEADS * PEER_TOPK) ** -0.5),
        "g_final": gain(ks[20], (D,)),
    }


def reference(x, meta, g_mix_norm, w_in, g_q, w_uq, g_kv, w_ukv, conv_w, conv_b, g_conv_ln, b_conv_ln,
              g_out_attn, g_out_conv, w_out, g_ffn_norm, peer_wq, peer_keys, peer_u, peer_v, g_final):
    B, S, D = x.shape
    T = N_META + S
    Tp = ((T + BLOCK - 1) // BLOCK) * BLOCK
    h = jnp.concatenate([jnp.broadcast_to(meta.astype(x.dtype)[None], (B, N_META, D)), x,
                         jnp.zeros((B, Tp - T, D), x.dtype)], axis=1)
    pos = jnp.arange(Tp, dtype=jnp.int32)

    c0 = 2 * CONV_CH
    c1 = c0 + Q_LORA
    c2 = c1 + KV_LORA
    for l in range(DEPTH):
        hn = rmsnorm(h, g_mix_norm[l])
        z = hn @ w_in[l]
        conv_out = conv_group(z[..., :CONV_CH], z[..., CONV_CH:c0], conv_w[l], conv_b[l],
                              g_conv_ln[l], b_conv_ln[l])
        attn_out = mla_group(z[..., c0:c1], z[..., c1:c2], z[..., c2:], pos,
                             g_q[l], w_uq[l], g_kv[l], w_ukv[l])
        mixed = jnp.concatenate([rmsnorm(attn_out, g_out_attn[l]),
                                 rmsnorm(conv_out, g_out_conv[l])], axis=-1)
        h = h + mixed @ w_out[l]
        h = h + peer_ffn(rmsnorm(h, g_ffn_norm[l]), peer_wq[l], peer_keys[l], peer_u[l], peer_v[l])

    h = rmsnorm(h, g_final)
    return h[:, N_META:N_META + S, :]
```

```python
import os
import math
import numpy as np
import concourse.bass as bass
import concourse.mybir as mybir
from concourse.bass_utils import run_bass_kernel_spmd
from contextlib import ExitStack

F32 = mybir.dt.float32
BF16 = mybir.dt.bfloat16
I32 = mybir.dt.int32
U32 = mybir.dt.uint32
ALU = mybir.AluOpType
AF = mybir.ActivationFunctionType
AX = mybir.AxisListType

D = 1024
SEQ = 4096
NMETA = 16
TP = 4224
NT_FULL = 33
EPS = 1e-6
NEXP = 16384


class Buf:
    __slots__ = ("name", "last_w", "readers", "excl")

    def __init__(self, name, excl=False):
        self.name = name
        self.last_w = None
        self.readers = []
        self.excl = excl


class Op:
    __slots__ = ("eng", "fns", "deps", "signal", "ord", "is_dma", "dsem", "dval", "dprev", "batch")

    def __init__(self, eng, fns, is_dma):
        self.batch = 0
        self.eng = eng
        self.fns = fns
        self.deps = []
        self.signal = False
        self.ord = 0
        self.is_dma = is_dma
        self.dsem = None
        self.dval = 0
        self.dprev = 0


class Prog:
    ENGS = ("pe", "act", "dve", "pool", "sp")

    def __init__(self, nc, stack, dma_sems=None):
        self.nc = nc
        self.ops = []
        self.eng_ops = {e: [] for e in self.ENGS}
        self.sems = {e: stack.enter_context(nc.semaphore("sem_" + e)) for e in self.ENGS}
        self.batch = 0
        self.ordc = {e: 0 for e in self.ENGS}
        self.waited = {e: {} for e in self.ENGS}
        dma_sems = dma_sems or {"sp": 12, "act": 4, "pool": 40}
        self.dsems, self.dcount, self.dnext = {}, {}, {}
        for e, n in dma_sems.items():
            self.dsems[e] = [stack.enter_context(nc.semaphore("dsem_%s%d" % (e, i))) for i in range(n)]
            self.dcount[e] = [0] * n
            self.dnext[e] = 0

    def add(self, eng, fns, reads=(), writes=(), dma=False):
        if callable(fns):
            fns = [fns]
        op = Op(eng, list(fns), dma)
        deps, seen = [], set()

        def dep(d):
            if d is not None and id(d) not in seen:
                seen.add(id(d))
                deps.append(d)

        wr = list(writes) + [b for b in reads if b.excl]
        for b in reads:
            if not b.excl:
                dep(b.last_w)
        for b in wr:
            dep(b.last_w)
            for r in b.readers:
                dep(r)
        for b in reads:
            if not b.excl:
                b.readers.append(op)
        for b in wr:
            b.last_w = op
            b.readers = []
        op.deps = deps
        op.batch = self.batch
        if dma:
            k = self.dnext[eng]
            self.dnext[eng] = (k + 1) % len(self.dsems[eng])
            op.dsem = self.dsems[eng][k]
            op.dprev = self.dcount[eng][k]
            self.dcount[eng][k] += 16
            op.dval = self.dcount[eng][k]
        self.ops.append(op)
        self.eng_ops[eng].append(op)
        return op

    def emit(self):
        nc = self.nc
        cur = self.batch
        for op in self.ops:
            for d in op.deps:
                if not d.is_dma and d.batch == cur:
                    d.signal = True
        for e in self.ENGS:
            n = self.ordc[e]
            for op in self.eng_ops[e]:
                if op.signal and not op.is_dma:
                    n += 1
                    op.ord = n
            self.ordc[e] = n
        prog = self

        def run(ename, engobj):
            waited = prog.waited[ename]

            def wait(sem, val):
                if val <= 0 or waited.get(sem.num, 0) >= val:
                    return
                waited[sem.num] = val
                engobj.wait_ge(sem, val)

            for op in prog.eng_ops[ename]:
                for d in op.deps:
                    if d.is_dma:
                        wait(d.dsem, d.dval)
                    else:
                        if d.batch != cur:
                            continue
                        if d.eng == ename and ename == "pe":
                            continue
                        wait(prog.sems[d.eng], d.ord)
                if op.is_dma:
                    wait(op.dsem, op.dprev)
                last = None
                for fn in op.fns:
                    last = fn(engobj)
                if last is not None:
                    if op.is_dma:
                        last.then_inc(op.dsem, 16)
                    elif op.signal:
                        last.then_inc(prog.sems[ename], 1)
                else:
                    assert not op.signal

        with nc.Block() as block:
            @block.tensor
            def _(eng):
                run("pe", eng)

            @block.scalar
            def _(eng):
                run("act", eng)

            @block.vector
            def _(eng):
                run("dve", eng)

            @block.gpsimd
            def _(eng):
                run("pool", eng)

            @block.sync
            def _(eng):
                run("sp", eng)
        self.batch += 1
        self.ops = []
        self.eng_ops = {e: [] for e in self.ENGS}


def _interleave(main, side, n_main_hint, n_side_hint):
    if side is None:
        for _ in main:
            pass
        return
    ratio = max(1e-9, n_side_hint / max(1, n_main_hint))
    credit = 0.0
    side_done = False
    for _ in main:
        credit += ratio
        while credit >= 1.0 and not side_done:
            credit -= 1.0
            try:
                next(side)
            except StopIteration:
                side_done = True
    if not side_done:
        for _ in side:
            pass


def build_program(NT=NT_FULL, NB=20, phases=(0, 1, 2), dbg_h2=False):
    nc = bass.Bass("TRN2", target_bir_lowering=False)

    def din(name, shape, dt=F32):
        return nc.dram_tensor(name, list(shape), dt, kind="ExternalInput").ap()

    h0 = din("h0", [TP, D])
    w_in = din("w_in", [D, 1440])
    w_uq = din("w_uq", [256, 768])
    w_ukv = din("w_ukv", [128, 1024])
    conv_w = din("conv_w", [31, 512])
    w_out = din("w_out", [D, D])
    peer_wq = din("peer_wq", [D, D])
    peer_keys = din("peer_keys", [8, 2, 128, 64])
    peer_u = din("peer_u", [NEXP, D])
    peer_v = din("peer_v", [NEXP, D])
    vecs = din("vecs", [31, 128])
    g_ffn = din("g_ffn", [1, D])
    g_fin = din("g_fin", [1, D])
    ident_d = din("ident", [128, 128])
    rope_d = din("rope", [NT_FULL, 32, 256])
    mask_d = din("mask", [128, 128])
    iota_d = din("iota16", [128, 16])
    n_out_rows = min(SEQ, NT * 128 - NMETA)
    out = nc.dram_tensor("out", [SEQ, D], F32, kind="ExternalOutput").ap()
    h2s = nc.dram_tensor("h2s", [TP, D], F32, kind="ExternalOutput" if dbg_h2 else "Internal").ap()

    with ExitStack() as st:
        P = Prog(nc, st)
        psum = [st.enter_context(nc.psum_tensor("ps%d" % i, [128, 512], F32)) for i in range(8)]
        pbuf = [Buf("ps%d" % i, excl=True) for i in range(8)]
        b_out = Buf("out")
        uv16 = nc.dram_tensor("uv16", [NEXP, 2 * D], BF16, kind="Internal").ap()
        b_uv16 = Buf("uv16")
        b_h2s = [Buf("h2s%d" % i) for i in range(NT_FULL)]

        def cp_any(eng, o, i):
            if eng == "act":
                return lambda e: e.copy(o, i)
            return lambda e: e.tensor_copy(o, i)

        def phase1():
            with ExitStack() as s1:
                def sb(name, shape, dt):
                    return s1.enter_context(nc.sbuf_tensor("a_" + name, list(shape), dt))

                gen_cycle = [0, 1, 4, 5, 6, 7]
                gen_pos = [0]

                def gbank():
                    b = gen_cycle[gen_pos[0] % len(gen_cycle)]
                    gen_pos[0] += 1
                    return b

                ident_f = sb("ident_f", [128, 128], F32)
                ident_b = sb("ident_b", [128, 128], BF16)
                onesb = sb("onesb", [128, 128], BF16)
                maskf = sb("maskf", [128, 128], F32)
                maskb = sb("maskb", [128, 128], BF16)
                vst = sb("vst", [31, 128], F32)
                cols = sb("cols", [128, 31], F32)
                cw = sb("cw", [128, 4, 31], F32)
                Wb = sb("Wb", [128, 8, 1440], BF16)
                wkrot = sb("wkrot", [128, 8, 96], BF16)
                Wq = sb("Wq", [128, 2, 8, 96], BF16)
                Wqrot = sb("Wqrot", [128, 2, 8, 96], BF16)
                Wk = sb("Wk", [128, 8, 64], BF16)
                Wv = sb("Wv", [128, 8, 64], BF16)
                Wo = sb("Wo", [128, 8, 1024], BF16)
                KT = sb("KT", [128, 8, NT * 128], BF16)
                VP = sb("VP", [128, NT, 8, 65], BF16)
                Gs = [sb("G%d" % i, [128, 4, 158], F32) for i in range(2)]
                b_G = [Buf("G0"), Buf("G1")]
                cwst = Gs[1][0:31, :, :].rearrange("p c t -> p (c t)")[:, 0:512]
                xt = [sb("xt%d" % i, [128, D], F32) for i in range(2)]
                h2 = [sb("h2_0", [128, D], F32)] * 2
                b_xt = [Buf("xt0"), Buf("xt1")]
                b_h2 = [Buf("h2_0")] * 2
                stage = xt + h2[:1]
                b_stage = b_xt + b_h2[:1]
                B = {n: Buf(n) for n in ("ident_f ident_b onesb maskf maskb vst cols cwst cw Wb wkrot Wq Wqrot Wk Wv Wo "
                                         "KT VP G").split()}
                gm, gq, gkv = cols[:, 0:8], cols[:, 8:10], cols[:, 10:11]
                cb, gln, bln = cols[:, 11:15], cols[:, 15:19], cols[:, 19:23]
                goa, goc = cols[:, 23:27], cols[:, 27:31]

                P.add("sp", lambda e: e.dma_start(out=ident_f[:], in_=ident_d), writes=[B["ident_f"]], dma=True)
                P.add("sp", lambda e: e.dma_start(out=maskf[:], in_=mask_d), writes=[B["maskf"]], dma=True)
                P.add("sp", lambda e: e.dma_start(out=vst[:], in_=vecs), writes=[B["vst"]], dma=True)
                P.add("sp", lambda e: e.dma_start(out=cwst, in_=conv_w), writes=[b_G[1]], dma=True)
                P.add("dve", lambda e: e.tensor_copy(ident_b[:], ident_f[:]), reads=[B["ident_f"]], writes=[B["ident_b"]])
                P.add("dve", lambda e: e.tensor_copy(maskb[:], maskf[:]), reads=[B["maskf"]], writes=[B["maskb"]])
                P.add("dve", lambda e: e.memset(onesb[:], 1.0), writes=[B["onesb"]])
                P.add("dve", lambda e: e.memset(Gs[0][:], 0.0), writes=[b_G[0]])
                P.add("pool", lambda e: e.memset(VP[:], 1.0), writes=[B["VP"]])
                P.add("pool", lambda e: e.memset(KT[:], 0.0), writes=[B["KT"]])
                bk = gbank()
                P.add("pe", lambda e, bk=bk: e.matmul(psum[bk][:, 0:31], vst[0:31, :], ident_f[0:31, 0:31], start=True, stop=True),
                      reads=[B["vst"], B["ident_f"]], writes=[pbuf[bk]])
                P.add("dve", lambda e, bk=bk: e.tensor_copy(cols[:], psum[bk][:, 0:31]), reads=[pbuf[bk]], writes=[B["cols"]])
                bk = gbank()
                fns = []
                for c in range(4):
                    fns.append(lambda e, bk=bk, c=c: e.matmul(psum[bk][:, c * 31:(c + 1) * 31], cwst[:, c * 128:(c + 1) * 128],
                                                              ident_f[0:31, 0:31], start=(c == 0), stop=(c == 3)))
                P.add("pe", fns, reads=[b_G[1], B["ident_f"]], writes=[pbuf[bk]])
                P.add("dve", lambda e, bk=bk: e.tensor_copy(cw[:].rearrange("p c k -> p (c k)"), psum[bk][:, 0:124]),
                      reads=[pbuf[bk]], writes=[B["cw"]])
                sidx = [0]

                def stage_load(src_ap, ncol):
                    k = sidx[0] % 3
                    sidx[0] += 1
                    P.add("sp", lambda e, k=k: e.dma_start(out=stage[k][:, 0:ncol], in_=src_ap), writes=[b_stage[k]], dma=True)
                    return k

                for c in range(8):
                    for (c0, c1) in ((0, 1024), (1024, 1440)):
                        k = stage_load(w_in[c * 128:(c + 1) * 128, c0:c1], c1 - c0)
                        P.add("dve" if c % 2 == 0 else "pool",
                              lambda e, k=k, c=c, c0=c0, c1=c1: e.tensor_scalar(out=Wb[:, c, c0:c1], in0=stage[k][:, 0:c1 - c0],
                                                                                scalar1=gm[:, c:c + 1], scalar2=None, op0=ALU.mult),
                              reads=[b_stage[k], B["cols"]], writes=[B["Wb"]])
                P.add("dve", lambda e: e.tensor_copy(wkrot[:, :, 0:64], Wb[:, :, 1344:1408]), reads=[B["Wb"]], writes=[B["wkrot"]])
                P.add("dve", lambda e: e.tensor_copy(wkrot[:, :, 64:80], Wb[:, :, 1424:1440]), reads=[B["Wb"]], writes=[B["wkrot"]])
                P.add("dve", lambda e: e.tensor_copy(wkrot[:, :, 80:96], Wb[:, :, 1408:1424]), reads=[B["Wb"]], writes=[B["wkrot"]])
                for c in range(2):
                    k = stage_load(w_uq[c * 128:(c + 1) * 128, :], 768)
                    P.add("dve", lambda e, k=k, c=c: e.tensor_scalar(out=Wq[:, c, :, :].rearrange("p h d -> p (h d)"),
                                                                     in0=stage[k][:, 0:768], scalar1=gq[:, c:c + 1],
                                                                     scalar2=None, op0=ALU.mult),
                          reads=[b_stage[k], B["cols"]], writes=[B["Wq"]])
                P.add("dve", lambda e: e.tensor_copy(Wqrot[:, :, :, 0:64], Wq[:, :, :, 0:64]), reads=[B["Wq"]], writes=[B["Wqrot"]])
                P.add("dve", lambda e: e.tensor_copy(Wqrot[:, :, :, 64:80], Wq[:, :, :, 80:96]), reads=[B["Wq"]], writes=[B["Wqrot"]])
                P.add("dve", lambda e: e.tensor_copy(Wqrot[:, :, :, 80:96], Wq[:, :, :, 64:80]), reads=[B["Wq"]], writes=[B["Wqrot"]])
                k = stage_load(w_ukv, 1024)
                stv = stage[k][:, 0:1024].rearrange("p (h x) -> p h x", h=8)
                P.add("dve", lambda e, stv=stv: e.tensor_scalar(out=Wk[:], in0=stv[:, :, 0:64], scalar1=gkv[:, 0:1], scalar2=None,
                                                                op0=ALU.mult), reads=[b_stage[k], B["cols"]], writes=[B["Wk"]])
                P.add("dve", lambda e, stv=stv: e.tensor_scalar(out=Wv[:], in0=stv[:, :, 64:128], scalar1=gkv[:, 0:1], scalar2=None,
                                                                op0=ALU.mult), reads=[b_stage[k], B["cols"]], writes=[B["Wv"]])
                for c in range(8):
                    k = stage_load(w_out[c * 128:(c + 1) * 128, :], 1024)
                    sc = goa[:, c:c + 1] if c < 4 else goc[:, c - 4:c - 3]
                    P.add("dve" if c % 2 == 0 else "pool",
                          lambda e, k=k, c=c, sc=sc: e.tensor_scalar(out=Wo[:, c, :], in0=stage[k][:, 0:1024], scalar1=sc,
                                                                     scalar2=None, op0=ALU.mult),
                          reads=[b_stage[k], B["cols"]], writes=[B["Wo"]])

                rp = [sb("rp%d" % i, [128, 2, 128], F32) for i in range(2)]
                st1 = sb("st1", [128, 8], F32)
                hs = sb("hs", [128, D], BF16)
                junkb = hs
                hT = sb("hT", [128, 8, 128], BF16)
                sig = sb("sig", [128, 512], F32)
                cv = sb("cv", [128, 4, 128], F32)
                cbf = sb("cbf", [128, 512], BF16)
                c2b = sb("c2b", [128, 512], BF16)
                mean = sb("mean", [128, 128], F32)
                m2 = sb("m2", [128, 128], F32)
                var = sb("var", [128, 128], F32)
                rs = sb("rs", [128, 128], F32)
                tt = sb("tt", [128, 4, 128], F32)
                t2 = sb("t2", [128, 4, 128], F32)
                sl = sb("sl", [128, 4, 128], F32)
                s2b = c2b
                rs2 = sb("rs2", [128, 128], F32)
                MTs = [sb("mixT%d" % i, [128, 8, 128], BF16) for i in range(2)]
                b_mTa = [Buf("mTa0"), Buf("mTa1")]
                b_mTc = [Buf("mTc0"), Buf("mTc1")]
                cq = sb("cq", [128, 2, 128], F32)
                cq2 = sb("cq2", [128, 256], BF16)
                rsq = sb("rsq", [128, 128], F32)
                cqn = sb("cqn", [128, 2, 128], BF16)
                ckv = sb("ckv", [128, 128], F32)
                ckv2 = sb("ckv2", [128, 128], BF16)
                rskv = sb("rskv", [128, 128], F32)
                ckvn = sb("ckvn", [128, 128], BF16)
                kr1 = sb("kr1", [128, 128], F32)
                kr2 = sb("kr2", [128, 128], F32)
                qT = sb("qT", [128, 8, 128], BF16)
                qr1 = tt
                qr2 = t2
                PT = [sb("PT%d" % i, [128, 4, 128], BF16) for i in range(4)]
                rec = sb("rec", [128, 8, 1], F32)
                ao = sig[:].rearrange("p (h x) -> p h x", h=8)
                aob = sb("aob", [128, 512], BF16)
                W = {n: Buf(n) for n in ("junkb st1 hs hT sig cbf c2b mean m2 var rs tt t2 sl s2b rs2 mixTa mixTc cq cq2 "
                                         "rsq cqn ckv ckv2 rskv ckvn kr1 kr2 qT qr1 qr2 rec ao aob ssa rsa").split()}
                b_rp = [Buf("rp0"), Buf("rp1")]
                b_cv = [Buf("cv%d" % c) for c in range(4)]
                b_PT = [Buf("PT%d" % i) for i in range(4)]
                W["junkb"] = W["hs"]
                W["s2b"] = W["c2b"]
                W["qr1"] = W["tt"]
                W["qr2"] = W["t2"]
                W["ao"] = W["sig"]
                ssa = sb("ssa", [128, 4], F32)
                SCALE = 1.0 / math.sqrt(96.0)

                Obank = (2, 3)

                def segA1(i):
                    X, bX = xt[i % 2], b_xt[i % 2]
                    R, bR = rp[i % 2], b_rp[i % 2]
                    tc = slice(i * 128, (i + 1) * 128)
                    G, bG = Gs[i % 2], b_G[i % 2]
                    P.add("sp", lambda e, X=X, i=i: e.dma_start(out=X[:], in_=h0[i * 128:(i + 1) * 128, :]), writes=[bX], dma=True)
                    P.add("sp", lambda e, R=R, i=i: e.dma_start(out=R[64:96, :, :].rearrange("p a t -> p (a t)"), in_=rope_d[i]),
                          writes=[bR], dma=True)
                    P.add("act", lambda e, X=X: e.activation(out=junkb[:], in_=X[:], func=AF.Square, accum_out=st1[:, 0:1]),
                          reads=[bX], writes=[W["junkb"], W["st1"]])
                    P.add("act", lambda e: e.activation(out=st1[:, 1:2], in_=st1[:, 0:1], func=AF.Sqrt, bias=EPS, scale=1.0 / D),
                          reads=[W["st1"]], writes=[W["st1"]])
                    P.add("dve", lambda e: e.reciprocal(st1[:, 2:3], st1[:, 1:2]), reads=[W["st1"]], writes=[W["st1"]])
                    P.add("act", lambda e, X=X: e.activation(out=hs[:], in_=X[:], func=AF.Copy, scale=st1[:, 2:3]),
                          reads=[bX, W["st1"]], writes=[W["hs"]])
                    bk = gbank()
                    psT = psum[bk][:].bitcast(BF16)
                    P.add("pe", [lambda e, c=c, psT=psT: e.transpose(psT[:, c * 128:(c + 1) * 128], hs[:, c * 128:(c + 1) * 128], ident_b[:])
                                 for c in range(8)], reads=[W["hs"], B["ident_b"]], writes=[pbuf[bk]])
                    P.add("dve", lambda e, psT=psT: e.tensor_copy(hT[:].rearrange("p c t -> p (c t)"), psT[:, 0:1024]),
                          reads=[pbuf[bk]], writes=[W["hT"]])
                    bA, bB, bC, bD = gbank(), gbank(), gbank(), gbank()
                    for (bk, col0) in ((bA, 0), (bB, 512)):
                        fns = []
                        for cc in range(4):
                            for c in range(8):
                                fns.append(lambda e, bk=bk, cc=cc, c=c, col0=col0: e.matmul(
                                    psum[bk][:, cc * 128:(cc + 1) * 128], Wb[:, c, col0 + cc * 128:col0 + (cc + 1) * 128], hT[:, c, :],
                                    start=(cc == 0 and c == 0), stop=(cc == 3 and c == 7)))
                        P.add("pe", fns, reads=[B["Wb"], W["hT"]], writes=[pbuf[bk]])
                    fns = []
                    for cc in range(3):
                        for c in range(8):
                            fns.append(lambda e, cc=cc, c=c: e.matmul(
                                psum[bC][:, cc * 128:(cc + 1) * 128], Wb[:, c, 1024 + cc * 128:1024 + (cc + 1) * 128], hT[:, c, :],
                                start=(cc == 0 and c == 0), stop=False))
                    for c in range(8):
                        fns.append(lambda e, c=c: e.matmul(psum[bC][0:96, 384:512], Wb[:, c, 1344:1440], hT[:, c, :],
                                                           start=False, stop=(c == 7)))
                    P.add("pe", fns, reads=[B["Wb"], W["hT"]], writes=[pbuf[bC]])
                    P.add("pe", [lambda e, c=c: e.matmul(psum[bD][0:96, 0:128], wkrot[:, c, :], hT[:, c, :], start=(c == 0), stop=(c == 7))
                                 for c in range(8)], reads=[B["wkrot"], W["hT"]], writes=[pbuf[bD]])
                    P.add("act", lambda e: e.activation(out=sig[:], in_=psum[bB][:, :], func=AF.Sigmoid), reads=[pbuf[bB]], writes=[W["sig"]])
                    P.add("dve", lambda e: e.tensor_tensor(out=G[:, :, 30:158], in0=psum[bA][:, :].rearrange("p (c t) -> p c t", c=4),
                                                           in1=sig[:].rearrange("p (c t) -> p c t", c=4), op=ALU.mult),
                          reads=[pbuf[bA], W["sig"]], writes=[bG])
                    P.add("act", lambda e: e.copy(cq[:].rearrange("p c t -> p (c t)"), psum[bC][:, 0:256]), reads=[pbuf[bC]], writes=[W["cq"]])
                    P.add("act", lambda e: e.activation(out=cq2[:], in_=psum[bC][:, 0:256], func=AF.Square), reads=[pbuf[bC]], writes=[W["cq2"]])
                    P.add("act", lambda e: e.copy(ckv[:], psum[bC][:, 256:384]), reads=[pbuf[bC]], writes=[W["ckv"]])
                    P.add("act", lambda e: e.activation(out=ckv2[:], in_=psum[bC][:, 256:384], func=AF.Square), reads=[pbuf[bC]], writes=[W["ckv2"]])
                    P.add("dve", lambda e, R=R: e.tensor_tensor(out=kr1[64:96, :], in0=psum[bC][64:96, 384:512], in1=R[64:96, 0, :], op=ALU.mult),
                          reads=[pbuf[bC], bR], writes=[W["kr1"]])
                    P.add("dve", lambda e, R=R: e.tensor_tensor(out=kr2[64:96, :], in0=psum[bD][64:96, 0:128], in1=R[64:96, 1, :], op=ALU.mult),
                          reads=[pbuf[bD], bR], writes=[W["kr2"]])
                    P.add("dve", lambda e, tc=tc: e.tensor_tensor(out=KT[64:96, 0, tc], in0=kr1[64:96, :], in1=kr2[64:96, :], op=ALU.add),
                          reads=[W["kr1"], W["kr2"]], writes=[B["KT"]])
                    P.add("dve", lambda e, tc=tc: e.tensor_copy(KT[64:96, 1:8, tc], KT[64:96, 0:1, tc].broadcast_to([32, 7, 128])),
                          reads=[B["KT"]], writes=[B["KT"]])
                    bE = gbank()
                    P.add("pe", [lambda e, c=c: e.matmul(psum[bE][:, 0:128], onesb[:], cq2[:, c * 128:(c + 1) * 128], start=(c == 0), stop=False)
                                 for c in range(2)] +
                          [lambda e: e.matmul(psum[bE][:, 128:256], onesb[:], ckv2[:], start=False, stop=True)],
                          reads=[B["onesb"], W["cq2"], W["ckv2"]], writes=[pbuf[bE]])
                    P.add("act", lambda e: e.activation(out=rsq[:], in_=psum[bE][:, 0:128], func=AF.Sqrt, bias=EPS, scale=1.0 / 256),
                          reads=[pbuf[bE]], writes=[W["rsq"]])
                    P.add("act", lambda e: e.activation(out=rskv[:], in_=psum[bE][:, 128:256], func=AF.Sqrt, bias=EPS, scale=1.0 / 128),
                          reads=[pbuf[bE]], writes=[W["rskv"]])
                    P.add("dve", lambda e: e.reciprocal(rsq[:], rsq[:]), reads=[W["rsq"]], writes=[W["rsq"]])
                    P.add("dve", lambda e: e.reciprocal(rskv[:], rskv[:]), reads=[W["rskv"]], writes=[W["rskv"]])
                    P.add("dve", lambda e: e.tensor_tensor(out=cqn[:], in0=cq[:], in1=rsq[:].unsqueeze(1).broadcast_to([128, 2, 128]), op=ALU.mult),
                          reads=[W["cq"], W["rsq"]], writes=[W["cqn"]])
                    P.add("dve", lambda e: e.tensor_tensor(out=ckvn[:], in0=ckv[:], in1=rskv[:], op=ALU.mult),
                          reads=[W["ckv"], W["rskv"]], writes=[W["ckvn"]])
                    bq = [gbank(), gbank()]
                    for half in range(2):
                        fns = []
                        for hh in range(4):
                            h = half * 4 + hh
                            for c in range(2):
                                fns.append(lambda e, half=half, hh=hh, h=h, c=c: e.matmul(
                                    psum[bq[half]][0:96, hh * 128:(hh + 1) * 128], Wq[:, c, h, :], cqn[:, c, :],
                                    start=(hh == 0 and c == 0), stop=(hh == 3 and c == 1)))
                        P.add("pe", fns, reads=[B["Wq"], W["cqn"]], writes=[pbuf[bq[half]]])
                    bkk = [gbank(), gbank()]
                    for half in range(2):
                        P.add("pe", [lambda e, half=half, hh=hh: e.matmul(psum[bkk[half]][0:64, hh * 128:(hh + 1) * 128], Wk[:, half * 4 + hh, :],
                                                                          ckvn[:], start=(hh == 0), stop=(hh == 3)) for hh in range(4)],
                              reads=[B["Wk"], W["ckvn"]], writes=[pbuf[bkk[half]]])
                    for half in range(2):
                        P.add("act", lambda e, half=half: e.copy(qT[0:64, half * 4:(half + 1) * 4, :],
                                                                 psum[bq[half]][0:64, :].rearrange("p (h t) -> p h t", h=4)),
                              reads=[pbuf[bq[half]]], writes=[W["qT"]])
                        P.add("dve", lambda e, half=half, R=R: e.tensor_tensor(
                            out=qr1[64:96, :, :], in0=psum[bq[half]][64:96, :].rearrange("p (h t) -> p h t", h=4),
                            in1=R[64:96, 0:1, :].broadcast_to([32, 4, 128]), op=ALU.mult),
                            reads=[pbuf[bq[half]], bR], writes=[W["qr1"]])
                        brot = gbank()
                        fns = []
                        for hh in range(4):
                            h = half * 4 + hh
                            for c in range(2):
                                fns.append(lambda e, brot=brot, hh=hh, h=h, c=c: e.matmul(
                                    psum[brot][0:96, hh * 128:(hh + 1) * 128], Wqrot[:, c, h, :], cqn[:, c, :],
                                    start=(hh == 0 and c == 0), stop=(hh == 3 and c == 1)))
                        P.add("pe", fns, reads=[B["Wqrot"], W["cqn"]], writes=[pbuf[brot]])
                        P.add("dve", lambda e, brot=brot, R=R: e.tensor_tensor(
                            out=qr2[64:96, :, :], in0=psum[brot][64:96, :].rearrange("p (h t) -> p h t", h=4),
                            in1=R[64:96, 1:2, :].broadcast_to([32, 4, 128]), op=ALU.mult),
                            reads=[pbuf[brot], bR], writes=[W["qr2"]])
                        P.add("dve", lambda e, half=half: e.tensor_tensor(out=qT[64:96, half * 4:(half + 1) * 4, :], in0=qr1[64:96, :, :],
                                                                          in1=qr2[64:96, :, :], op=ALU.add),
                              reads=[W["qr1"], W["qr2"]], writes=[W["qT"]])
                        P.add("act", lambda e, half=half, tc=tc: e.copy(KT[0:64, half * 4:(half + 1) * 4, tc],
                                                                        psum[bkk[half]][0:64, :].rearrange("p (h t) -> p h t", h=4)),
                              reads=[pbuf[bkk[half]]], writes=[B["KT"]])
                    bv = gbank()
                    P.add("pe", lambda e: e.matmul(psum[bv][:, :], ckvn[:], Wv[:].rearrange("p h x -> p (h x)"), start=True, stop=True),
                          reads=[W["ckvn"], B["Wv"]], writes=[pbuf[bv]])
                    P.add("act", lambda e, i=i: e.copy(VP[:, i, :, 0:64], psum[bv][:, :].rearrange("p (h x) -> p h x", h=8)),
                          reads=[pbuf[bv]], writes=[B["VP"]])
                    pend = None
                    nPT = [0]

                    def issue_pv(j, pts):
                        for half in range(2):
                            k_pt = pts[half]
                            P.add("pe", [lambda e, half=half, hh=hh, k_pt=k_pt, j=j: e.matmul(
                                psum[Obank[half]][:, hh * 65:(hh + 1) * 65], PT[k_pt][:, hh, :], VP[:, j, half * 4 + hh, :],
                                start=(j == 0 and hh == 0), stop=(j == i and hh == 3)) for hh in range(4)],
                                reads=[b_PT[k_pt], B["VP"]], writes=[pbuf[Obank[half]]])

                    for j in range(i + 1):
                        pts = []
                        for half in range(2):
                            sbk = (4 + half) if (j % 2 == 0) else (6 + half)
                            P.add("pe", [lambda e, half=half, hh=hh, sbk=sbk, j=j: e.matmul(
                                psum[sbk][:, hh * 128:(hh + 1) * 128], KT[0:96, half * 4 + hh, j * 128:(j + 1) * 128],
                                qT[0:96, half * 4 + hh, :], start=(hh == 0), stop=(hh == 3)) for hh in range(4)],
                                reads=[B["KT"], W["qT"]], writes=[pbuf[sbk]])
                            k_pt = nPT[0] % 4
                            nPT[0] += 1
                            P.add("act", lambda e, sbk=sbk, k_pt=k_pt: e.activation(out=PT[k_pt][:].rearrange("p h t -> p (h t)"),
                                                                                    in_=psum[sbk][:, :], func=AF.Exp, scale=SCALE),
                                  reads=[pbuf[sbk]], writes=[b_PT[k_pt]])
                            if j == i:
                                P.add("pool", lambda e, k_pt=k_pt: e.tensor_tensor(out=PT[k_pt][:], in0=PT[k_pt][:],
                                                                                   in1=maskb[:].unsqueeze(1).broadcast_to([128, 4, 128]),
                                                                                   op=ALU.mult),
                                      reads=[b_PT[k_pt], B["maskb"]], writes=[b_PT[k_pt]])
                            pts.append(k_pt)
                        if pend is not None:
                            issue_pv(*pend)
                        pend = (j, pts)
                    issue_pv(*pend)
                def segB1(i):
                    G, bG = Gs[i % 2], b_G[i % 2]
                    Gn, bGn = Gs[(i + 1) % 2], b_G[(i + 1) % 2]
                    mixT = MTs[i % 2]
                    for k in range(31):
                        for c in range(4):
                            if k == 0:
                                P.add("dve", lambda e, c=c: e.tensor_scalar(out=cv[:, c, :], in0=G[:, c, 0:128], scalar1=cw[:, c, 0:1],
                                                                            scalar2=cb[:, c:c + 1], op0=ALU.mult, op1=ALU.add),
                                      reads=[bG, B["cw"], B["cols"]], writes=[b_cv[c]])
                            else:
                                P.add("dve", lambda e, c=c, k=k: e.scalar_tensor_tensor(out=cv[:, c, :], in0=G[:, c, k:k + 128],
                                                                                        scalar=cw[:, c, k:k + 1], in1=cv[:, c, :],
                                                                                        op0=ALU.mult, op1=ALU.add),
                                      reads=[bG, B["cw"], b_cv[c]], writes=[b_cv[c]])
                    P.add("dve", lambda e: e.tensor_copy(Gn[:, :, 0:30], G[:, :, 128:158]), reads=[bG], writes=[bGn])
                    cvf = cv[:].rearrange("p c t -> p (c t)")
                    P.add("act", lambda e: e.copy(cbf[:], cvf), reads=b_cv, writes=[W["cbf"]])
                    P.add("act", lambda e: e.activation(out=c2b[:], in_=cvf, func=AF.Square), reads=b_cv, writes=[W["c2b"]])
                    bS = gbank()
                    P.add("pe", [lambda e, c=c: e.matmul(psum[bS][:, 0:128], onesb[:], cbf[:, c * 128:(c + 1) * 128], start=(c == 0), stop=False)
                                 for c in range(4)] +
                          [lambda e, c=c: e.matmul(psum[bS][:, 128:256], onesb[:], c2b[:, c * 128:(c + 1) * 128], start=False, stop=(c == 3))
                           for c in range(4)], reads=[B["onesb"], W["cbf"], W["c2b"]], writes=[pbuf[bS]])
                    P.add("dve", lambda e: e.tensor_scalar(out=mean[:], in0=psum[bS][:, 0:128], scalar1=1.0 / 512, scalar2=None, op0=ALU.mult),
                          reads=[pbuf[bS]], writes=[W["mean"]])
                    P.add("dve", lambda e: e.tensor_tensor(out=m2[:], in0=mean[:], in1=mean[:], op=ALU.mult), reads=[W["mean"]], writes=[W["m2"]])
                    P.add("dve", lambda e: e.scalar_tensor_tensor(out=var[:], in0=psum[bS][:, 128:256], scalar=1.0 / 512, in1=m2[:],
                                                                  op0=ALU.mult, op1=ALU.subtract),
                          reads=[pbuf[bS], W["m2"]], writes=[W["var"]])
                    P.add("act", lambda e: e.activation(out=rs[:], in_=var[:], func=AF.Sqrt, bias=EPS, scale=1.0), reads=[W["var"]], writes=[W["rs"]])
                    P.add("dve", lambda e: e.reciprocal(rs[:], rs[:]), reads=[W["rs"]], writes=[W["rs"]])
                    P.add("dve", lambda e: e.tensor_tensor(out=tt[:], in0=cv[:], in1=mean[:].unsqueeze(1).broadcast_to([128, 4, 128]), op=ALU.subtract),
                          reads=b_cv + [W["mean"]], writes=[W["tt"]])
                    P.add("dve", lambda e: e.tensor_tensor(out=t2[:], in0=tt[:], in1=rs[:].unsqueeze(1).broadcast_to([128, 4, 128]), op=ALU.mult),
                          reads=[W["tt"], W["rs"]], writes=[W["t2"]])
                    for c in range(4):
                        P.add("act", lambda e, c=c: e.activation(out=sl[:, c, :], in_=t2[:, c, :], func=AF.Silu, bias=bln[:, c:c + 1],
                                                                 scale=gln[:, c:c + 1]),
                              reads=[W["t2"], B["cols"]], writes=[W["sl"]])
                    P.add("act", lambda e: e.activation(out=s2b[:], in_=sl[:].rearrange("p c t -> p (c t)"), func=AF.Square),
                          reads=[W["sl"]], writes=[W["s2b"]])
                    bS2 = gbank()
                    P.add("pe", [lambda e, c=c: e.matmul(psum[bS2][:, 0:128], onesb[:], s2b[:, c * 128:(c + 1) * 128], start=(c == 0), stop=(c == 3))
                                 for c in range(4)], reads=[B["onesb"], W["s2b"]], writes=[pbuf[bS2]])
                    P.add("act", lambda e: e.activation(out=rs2[:], in_=psum[bS2][:, 0:128], func=AF.Sqrt, bias=EPS, scale=1.0 / 512),
                          reads=[pbuf[bS2]], writes=[W["rs2"]])
                    P.add("dve", lambda e: e.reciprocal(rs2[:], rs2[:]), reads=[W["rs2"]], writes=[W["rs2"]])
                    P.add("dve", lambda e: e.tensor_tensor(out=mixT[:, 4:8, :], in0=sl[:], in1=rs2[:].unsqueeze(1).broadcast_to([128, 4, 128]),
                                                           op=ALU.mult), reads=[W["sl"], W["rs2"]], writes=[b_mTc[i % 2]])
                def segA2(i):
                    mixT = MTs[i % 2]
                    for half in range(2):
                        ov = psum[Obank[half]][:, 0:260].rearrange("p (h x) -> p h x", h=4)
                        P.add("dve", lambda e, ov=ov, half=half: e.reciprocal(rec[:, half * 4:(half + 1) * 4, :], ov[:, :, 64:65]),
                              reads=[pbuf[Obank[half]]], writes=[W["rec"]])
                        P.add("dve", lambda e, ov=ov, half=half: e.tensor_tensor(
                            out=ao[:, half * 4:(half + 1) * 4, :], in0=ov[:, :, 0:64],
                            in1=rec[:, half * 4:(half + 1) * 4, :].broadcast_to([128, 4, 64]), op=ALU.mult),
                            reads=[pbuf[Obank[half]], W["rec"]], writes=[W["ao"]])
                    aof = sig[:]
                    P.add("act", lambda e: e.activation(out=junkb[:, 0:512], in_=aof, func=AF.Square, accum_out=ssa[:, 0:1]),
                          reads=[W["ao"]], writes=[W["junkb"], W["ssa"]])
                    P.add("act", lambda e: e.activation(out=ssa[:, 1:2], in_=ssa[:, 0:1], func=AF.Sqrt, bias=EPS, scale=1.0 / 512),
                          reads=[W["ssa"]], writes=[W["ssa"]])
                    P.add("dve", lambda e: e.reciprocal(ssa[:, 2:3], ssa[:, 1:2]), reads=[W["ssa"]], writes=[W["ssa"]])
                    P.add("act", lambda e: e.activation(out=aob[:], in_=aof, func=AF.Copy, scale=ssa[:, 2:3]), reads=[W["ao"], W["ssa"]],
                          writes=[W["aob"]])
                    bk = gbank()
                    psT = psum[bk][:].bitcast(BF16)
                    P.add("pe", [lambda e, c=c, psT=psT: e.transpose(psT[:, c * 128:(c + 1) * 128], aob[:, c * 128:(c + 1) * 128], ident_b[:])
                                 for c in range(4)], reads=[W["aob"], B["ident_b"]], writes=[pbuf[bk]])
                    P.add("dve", lambda e, psT=psT: e.tensor_copy(mixT[:, 0:4, :].rearrange("p c t -> p (c t)"), psT[:, 0:512]),
                          reads=[pbuf[bk]], writes=[b_mTa[i % 2]])

                def segB2(i):
                    X, bX = xt[i % 2], b_xt[i % 2]
                    mixT = MTs[i % 2]
                    H2, bH2 = h2[i % 2], b_h2[i % 2]
                    for half in range(2):
                        bk = gbank()
                        P.add("pe", [lambda e, bk=bk, c=c, half=half: e.matmul(psum[bk][:, :], mixT[:, c, :], Wo[:, c, half * 512:(half + 1) * 512],
                                                                               start=(c == 0), stop=(c == 7)) for c in range(8)],
                              reads=[b_mTa[i % 2], b_mTc[i % 2], B["Wo"]], writes=[pbuf[bk]])
                        P.add("dve", lambda e, bk=bk, half=half, X=X, H2=H2: e.tensor_tensor(
                            out=H2[:, half * 512:(half + 1) * 512], in0=psum[bk][:, :], in1=X[:, half * 512:(half + 1) * 512], op=ALU.add),
                            reads=[pbuf[bk], bX], writes=[bH2])
                    P.add("sp", lambda e, H2=H2, i=i: e.dma_start(out=h2s[i * 128:(i + 1) * 128, :], in_=H2[:]), reads=[bH2],
                          writes=[b_h2s[i]], dma=True)
                segA1(0)
                segA2(0)
                for i in range(NT):
                    if i + 1 < NT:
                        segA1(i + 1)
                    segB1(i)
                    segB2(i)
                    if i + 1 < NT:
                        segA2(i + 1)
                P.add("sp", [], reads=[b for b in b_h2s[:NT]])
                P.emit()

        def phase0():
            with ExitStack() as s0:
                def sb(name, shape, dt):
                    return s0.enter_context(nc.sbuf_tensor("z_" + name, list(shape), dt))
                R = 4
                NCH = 128 // R
                uin = [sb("uin%d" % i, [128, R, D], F32) for i in range(2)]
                vin = [sb("vin%d" % i, [128, R, D], F32) for i in range(2)]
                uvo = [sb("uvo%d" % i, [128, R, 2 * D], BF16) for i in range(2)]
                b_uin = [Buf("uin0"), Buf("uin1")]
                b_vin = [Buf("vin0"), Buf("vin1")]
                b_uvo = [Buf("uvo0"), Buf("uvo1")]
                uview = peer_u.rearrange("(p r) d -> p r d", p=128)
                vview = peer_v.rearrange("(p r) d -> p r d", p=128)
                oview = uv16.rearrange("(p r) d -> p r d", p=128)
                for c in range(NCH):
                    k = c % 2
                    P.add("sp", lambda e, k=k, c=c: e.dma_start(out=uin[k][:], in_=uview[:, c * R:(c + 1) * R, :]), writes=[b_uin[k]], dma=True)
                    P.add("act", lambda e, k=k, c=c: e.dma_start(out=vin[k][:], in_=vview[:, c * R:(c + 1) * R, :]), writes=[b_vin[k]], dma=True)
                    P.add("dve", lambda e, k=k: e.tensor_copy(uvo[k][:, :, 0:D], uin[k][:]), reads=[b_uin[k]], writes=[b_uvo[k]])
                    P.add("act", lambda e, k=k: e.copy(uvo[k][:, :, D:2 * D], vin[k][:]), reads=[b_vin[k]], writes=[b_uvo[k]])
                    P.add("sp", lambda e, k=k, c=c: e.dma_start(out=oview[:, c * R:(c + 1) * R, :], in_=uvo[k][:]), reads=[b_uvo[k]],
                          writes=[b_uv16], dma=True)
                P.add("sp", [], reads=[b_uv16])
                P.emit()

        def phase2():
            NT2 = 32 if NT >= NT_FULL else NT - 1
            GS = 4
            with ExitStack() as s2:
                tot = [0]

                def sb(name, shape, dt):
                    n = 1
                    for x in shape[1:]:
                        n *= x
                    tot[0] += n * (4 if dt in (F32, I32, U32) else 2)
                    if os.environ.get("KERNEL_SBDBG"):
                        print("sbuf p2", name, tot[0])
                    return s2.enter_context(nc.sbuf_tensor("b_" + name, list(shape), dt))

                gcyc = [4, 5, 6, 7]
                gpos = [0]

                def gbank():
                    b = gcyc[gpos[0] % 4]
                    gpos[0] += 1
                    return b

                ident_f = sb("ident_f", [128, 128], F32)
                ident_b = sb("ident_b", [128, 128], BF16)
                iota16 = sb("iota16", [128, 16], F32)
                gffn = sb("gffn", [128, D], F32)
                gfin = sb("gfin", [128, D], F32)
                Wpq = sb("Wpq", [128, 8, D], BF16)
                KB = sb("KB", [128, 8, 256], BF16)
                kst = sb("kst", [128, 128], F32)
                ot = [sb("ot%d" % i, [128, D], F32) for i in range(2)]
                b_ot = [Buf("ot0"), Buf("ot1")]
                stage = ot
                b_stage = b_ot
                C = {n: Buf(n) for n in "ident_f ident_b iota16 gffn gfin Wpq KB kst".split()}
                P.add("sp", lambda e: e.dma_start(out=ident_f[:], in_=ident_d), writes=[C["ident_f"]], dma=True)
                P.add("sp", lambda e: e.dma_start(out=iota16[:], in_=iota_d), writes=[C["iota16"]], dma=True)
                P.add("sp", lambda e: e.dma_start(out=gffn[:], in_=g_ffn.partition_broadcast(128)), writes=[C["gffn"]], dma=True)
                P.add("sp", lambda e: e.dma_start(out=gfin[:], in_=g_fin.partition_broadcast(128)), writes=[C["gfin"]], dma=True)
                P.add("dve", lambda e: e.tensor_copy(ident_b[:], ident_f[:]), reads=[C["ident_f"]], writes=[C["ident_b"]])
                P.add("dve", lambda e: e.memset(KB[:], 0.0), writes=[C["KB"]])
                for c in range(8):
                    k = c % 2
                    P.add("sp", lambda e, k=k, c=c: e.dma_start(out=stage[k][:], in_=peer_wq[c * 128:(c + 1) * 128, :]),
                          writes=[b_stage[k]], dma=True)
                    P.add("dve" if c % 2 == 0 else "act", cp_any("dve" if c % 2 == 0 else "act", Wpq[:, c, :], stage[k][:]),
                          reads=[b_stage[k]], writes=[C["Wpq"]])
                for h in range(8):
                    P.add("sp", lambda e, h=h: e.dma_start(out=kst[:].rearrange("n (p d) -> n p d", p=2),
                                                           in_=peer_keys[h].rearrange("p n d -> n p d")),
                          writes=[C["kst"]], dma=True)
                    bk = gbank()
                    P.add("pe", lambda e, bk=bk: e.transpose(psum[bk][:, 0:128], kst[:], ident_f[:]), reads=[C["kst"], C["ident_f"]],
                          writes=[pbuf[bk]])
                    P.add("dve", lambda e, bk=bk, h=h: e.tensor_copy(KB[0:64, h, 0:128], psum[bk][0:64, 0:128]), reads=[pbuf[bk]],
                          writes=[C["KB"]])
                    P.add("dve", lambda e, bk=bk, h=h: e.tensor_copy(KB[64:128, h, 128:256], psum[bk][64:128, 0:128]), reads=[pbuf[bk]],
                          writes=[C["KB"]])

                h2t = [sb("h2t%d" % i, [128, D], F32) for i in range(2)]
                xn = sb("xn", [128, D], F32)
                xnb = [sb("xnb%d" % i, [128, D], BF16) for i in range(2)]
                idx = [sb("idx%d" % i, [128, 128], I32) for i in range(2)]
                gate = [sb("gate%d" % i, [128, 8, 16], F32) for i in range(2)]
                st2 = sb("st2", [128, 8], F32)
                junkb = sb("junkb", [128, D], BF16)
                xnT = sb("xnT", [128, 8, 128], BF16)
                qpT = sb("qpT", [128, 8, 128], BF16)
                S = sb("S", [128, 16, 128], F32)
                S2 = sb("S2", [128, 16, 128], F32)
                sv = sb("sv", [128, 8, 2, 16], F32)
                si = sb("si", [128, 8, 2, 16], U32)
                sif = sb("sif", [128, 8, 2, 16], F32)
                cand = S[:].rearrange("p (h q) n -> p h (q n)", q=2)
                cand2 = S2[:].rearrange("p (h q) n -> p h (q n)", q=2)
                tv = sb("tv", [128, 8, 16], F32)
                tp = sb("tp", [128, 8, 16], U32)
                tpf = sb("tpf", [128, 8, 16], F32)
                cmp = cand2.rearrange("p h (a b) -> p h a b", a=16)
                abf = sb("abf", [128, 8, 2, 16], F32)
                oh = S2[:].rearrange("p g n -> p (g n)").bitcast(BF16).rearrange("p (g k x) -> p g k x", g=16, k=16)
                oh2 = S[:].rearrange("p g n -> p (g n)").bitcast(BF16).rearrange("p (g k x) -> p g k x", g=16, k=16)
                sel = sb("sel", [128, 8, 2, 16], F32)
                eidf = sb("eidf", [128, 8, 16], F32)
                dd = sb("dd", [128, 8, 16], F32)
                ee = sb("ee", [128, 8, 16], F32)
                zz = sb("zz", [128, 8], F32)
                act = sb("act", [128, 128], F32)
                ga = sb("ga", [128, 128], F32)
                wgt = sb("wgt", [128, 128], F32)
                junk = junkb
                junk2 = junkb
                NPROD = 8
                prod = [sb("prod%d" % i, [128, D], BF16) for i in range(NPROD)]
                b_prod = [Buf("prod%d" % i) for i in range(NPROD)]
                NDG = 8
                dg = [sb("dg%d" % i, [128, 128], BF16) for i in range(NDG)]
                b_dg = [Buf("dg%d" % i) for i in range(NDG)]
                yy = sb("yy", [128, D], F32)
                gb = [sb("gb%d" % i, [128, 2 * D], BF16) for i in range(NB)]
                b_gb = [Buf("gb%d" % i) for i in range(NB)]
                b_h2t = [Buf("h2t0"), Buf("h2t1")]
                b_xnb = [Buf("xnb0"), Buf("xnb1")]
                b_idx = [Buf("idx0"), Buf("idx1")]
                b_gate = [Buf("gate0"), Buf("gate1")]
                b_act = [Buf("act%d" % s) for s in range(128 // GS)]
                b_ga = [Buf("ga%d" % s) for s in range(128 // GS)]
                b_w = [Buf("w%d" % s) for s in range(128 // GS)]
                T = {n: Buf(n) for n in ("st2 junkb xn xnT qpT S S2 sv si sif cand cand2 tv tp tpf abf sel eidf dd ee zz yy").split()}
                T["cand"] = T["S"]
                T["cand2"] = T["S2"]
                T["cmp"] = T["S2"]
                T["oh"] = T["S2"]
                T["oh2"] = T["S"]
                thr16 = sb("thr16", [128, 16], F32)
                C["thr16"] = Buf("thr16")
                P.add("dve", lambda e: e.tensor_scalar(out=thr16[:], in0=iota16[:], scalar1=16.0, scalar2=None, op0=ALU.mult),
                      reads=[C["iota16"]], writes=[C["thr16"]])

                def prep(t):
                    H, bH = h2t[t % 2], b_h2t[t % 2]
                    XNB, bXNB = xnb[t % 2], b_xnb[t % 2]
                    IDX, bIDX = idx[t % 2], b_idx[t % 2]
                    GT, bGT = gate[t % 2], b_gate[t % 2]
                    r0 = NMETA + t * 128
                    P.add("sp", lambda e: e.dma_start(out=H[:], in_=h2s[r0:r0 + 128, :]), reads=[b_h2s[t], b_h2s[min(t + 1, NT_FULL - 1)]],
                          writes=[bH], dma=True)
                    yield
                    P.add("act", lambda e: e.activation(out=junkb[:], in_=H[:], func=AF.Square, accum_out=st2[:, 0:1]),
                          reads=[bH], writes=[T["junkb"], T["st2"]])
                    P.add("act", lambda e: e.activation(out=st2[:, 1:2], in_=st2[:, 0:1], func=AF.Sqrt, bias=EPS, scale=1.0 / D),
                          reads=[T["st2"]], writes=[T["st2"]])
                    yield
                    P.add("dve", lambda e: e.reciprocal(st2[:, 2:3], st2[:, 1:2]), reads=[T["st2"]], writes=[T["st2"]])
                    yield
                    P.add("dve", lambda e: e.scalar_tensor_tensor(out=xn[:], in0=H[:], scalar=st2[:, 2:3], in1=gffn[:], op0=ALU.mult,
                                                                  op1=ALU.mult), reads=[bH, T["st2"], C["gffn"]], writes=[T["xn"]])
                    yield
                    P.add("act", lambda e: e.copy(XNB[:], xn[:]), reads=[T["xn"]], writes=[bXNB])
                    bk = gbank()
                    psT = psum[bk][:].bitcast(BF16)
                    P.add("pe", [lambda e, c=c, psT=psT: e.transpose(psT[:, c * 128:(c + 1) * 128], XNB[:, c * 128:(c + 1) * 128], ident_b[:])
                                 for c in range(8)], reads=[bXNB, C["ident_b"]], writes=[pbuf[bk]])
                    P.add("act", lambda e, psT=psT: e.copy(xnT[:].rearrange("p c t -> p (c t)"), psT[:, 0:1024]), reads=[pbuf[bk]],
                          writes=[T["xnT"]])
                    yield
                    for half in range(2):
                        bk = gbank()
                        fns = []
                        for hh in range(4):
                            f = half * 4 + hh
                            for c in range(8):
                                fns.append(lambda e, bk=bk, hh=hh, f=f, c=c: e.matmul(
                                    psum[bk][:, hh * 128:(hh + 1) * 128], Wpq[:, c, f * 128:(f + 1) * 128], xnT[:, c, :],
                                    start=(hh == 0 and c == 0), stop=(hh == 3 and c == 7)))
                        P.add("pe", fns, reads=[C["Wpq"], T["xnT"]], writes=[pbuf[bk]])
                        P.add("act", lambda e, bk=bk, half=half: e.copy(qpT[:, half * 4:(half + 1) * 4, :].rearrange("p h t -> p (h t)"),
                                                                        psum[bk][:, :]), reads=[pbuf[bk]], writes=[T["qpT"]])
                        yield
                    for q4 in range(4):
                        bk = gbank()
                        P.add("pe", [lambda e, bk=bk, q4=q4, u=u: e.matmul(psum[bk][:, u * 256:(u + 1) * 256], qpT[:, q4 * 2 + u, :],
                                                                          KB[:, q4 * 2 + u, :], start=(u == 0), stop=(u == 1))
                                     for u in range(2)], reads=[T["qpT"], C["KB"]], writes=[pbuf[bk]])
                        P.add("act", lambda e, bk=bk, q4=q4: e.copy(S[:, q4 * 4:(q4 + 1) * 4, :].rearrange("p g n -> p (g n)"), psum[bk][:, :]),
                              reads=[pbuf[bk]], writes=[T["S"]])
                        yield
                    for g in range(16):
                        h, p = g // 2, g % 2
                        P.add("dve", lambda e, g=g, h=h, p=p: e.max(out=sv[:, h, p, 0:8], in_=S[:, g, :]), reads=[T["S"]], writes=[T["sv"]])
                        P.add("dve", lambda e, g=g, h=h, p=p: e.max_index(out=si[:, h, p, 0:8], in_max=sv[:, h, p, 0:8], in_values=S[:, g, :]),
                              reads=[T["S"], T["sv"]], writes=[T["si"]])
                        yield
                        P.add("dve", lambda e, g=g, h=h, p=p: e.match_replace(out=S2[:, g, :], in_to_replace=sv[:, h, p, 0:8],
                                                                              in_values=S[:, g, :], imm_value=-1e30),
                              reads=[T["S"], T["sv"]], writes=[T["S2"]])
                        P.add("dve", lambda e, g=g, h=h, p=p: e.max(out=sv[:, h, p, 8:16], in_=S2[:, g, :]), reads=[T["S2"]], writes=[T["sv"]])
                        yield
                        P.add("dve", lambda e, g=g, h=h, p=p: e.max_index(out=si[:, h, p, 8:16], in_max=sv[:, h, p, 8:16], in_values=S2[:, g, :]),
                              reads=[T["S2"], T["sv"]], writes=[T["si"]])
                        yield
                    P.add("dve", lambda e: e.tensor_tensor(out=cand.rearrange("p h (a b) -> p h a b", a=16),
                                                           in0=sv[:, :, 0, :].unsqueeze(3).broadcast_to([128, 8, 16, 16]),
                                                           in1=sv[:, :, 1, :].unsqueeze(2).broadcast_to([128, 8, 16, 16]), op=ALU.add),
                          reads=[T["sv"]], writes=[T["cand"]])
                    yield
                    for h in range(8):
                        P.add("dve", lambda e, h=h: e.max(out=tv[:, h, 0:8], in_=cand[:, h, :]), reads=[T["cand"]], writes=[T["tv"]])
                        P.add("dve", lambda e, h=h: e.max_index(out=tp[:, h, 0:8], in_max=tv[:, h, 0:8], in_values=cand[:, h, :]),
                              reads=[T["cand"], T["tv"]], writes=[T["tp"]])
                        yield
                        P.add("dve", lambda e, h=h: e.match_replace(out=cand2[:, h, :], in_to_replace=tv[:, h, 0:8], in_values=cand[:, h, :],
                                                                    imm_value=-1e30), reads=[T["cand"], T["tv"]], writes=[T["cand2"]])
                        P.add("dve", lambda e, h=h: e.max(out=tv[:, h, 8:16], in_=cand2[:, h, :]), reads=[T["cand2"]], writes=[T["tv"]])
                        yield
                        P.add("dve", lambda e, h=h: e.max_index(out=tp[:, h, 8:16], in_max=tv[:, h, 8:16], in_values=cand2[:, h, :]),
                              reads=[T["cand2"], T["tv"]], writes=[T["tp"]])
                        yield
                    P.add("dve", lambda e: e.tensor_tensor(out=dd[:], in0=tv[:], in1=tv[:, :, 0:1].broadcast_to([128, 8, 16]), op=ALU.subtract),
                          reads=[T["tv"]], writes=[T["dd"]])
                    P.add("act", lambda e: e.activation(out=ee[:], in_=dd[:], func=AF.Exp), reads=[T["dd"]], writes=[T["ee"]])
                    yield
                    P.add("dve", lambda e: e.tensor_reduce(out=zz[:], in_=ee[:], axis=AX.X, op=ALU.add), reads=[T["ee"]], writes=[T["zz"]])
                    P.add("dve", lambda e: e.reciprocal(zz[:], zz[:]), reads=[T["zz"]], writes=[T["zz"]])
                    yield
                    P.add("dve", lambda e: e.tensor_tensor(out=GT[:], in0=ee[:], in1=zz[:].unsqueeze(2).broadcast_to([128, 8, 16]), op=ALU.mult),
                          reads=[T["ee"], T["zz"]], writes=[bGT])
                    yield
                    P.add("dve", lambda e: e.tensor_copy(tpf[:], tp[:]), reads=[T["tp"]], writes=[T["tpf"]])
                    P.add("dve", lambda e: e.tensor_copy(sif[:], si[:]), reads=[T["si"]], writes=[T["sif"]])
                    yield
                    P.add("dve", lambda e: e.tensor_tensor(out=cmp, in0=tpf[:].unsqueeze(3).broadcast_to([128, 8, 16, 16]),
                                                           in1=thr16[:, :].unsqueeze(1).unsqueeze(1).broadcast_to([128, 8, 16, 16]),
                                                           op=ALU.is_ge), reads=[T["tpf"], C["thr16"]], writes=[T["cmp"]])
                    yield
                    P.add("dve", lambda e: e.tensor_reduce(out=abf[:, :, 0, :], in_=cmp, axis=AX.X, op=ALU.add), reads=[T["cmp"]],
                          writes=[T["abf"]])
                    yield
                    P.add("dve", lambda e: e.tensor_scalar(out=abf[:, :, 0, :], in0=abf[:, :, 0, :], scalar1=-1.0, scalar2=None, op0=ALU.add),
                          reads=[T["abf"]], writes=[T["abf"]])
                    P.add("dve", lambda e: e.scalar_tensor_tensor(out=abf[:, :, 1, :], in0=abf[:, :, 0, :], scalar=-16.0, in1=tpf[:],
                                                                  op0=ALU.mult, op1=ALU.add), reads=[T["abf"], T["tpf"]], writes=[T["abf"]])
                    yield
                    P.add("dve", lambda e: e.tensor_tensor(out=oh, in0=iota16[:, :].unsqueeze(1).unsqueeze(1).broadcast_to([128, 16, 16, 16]),
                                                           in1=abf[:].rearrange("p h q k -> p (h q) k").unsqueeze(3).broadcast_to([128, 16, 16, 16]),
                                                           op=ALU.is_equal), reads=[C["iota16"], T["abf"]], writes=[T["oh"]])
                    yield
                    P.add("dve", lambda e: e.tensor_tensor(out=oh2, in0=oh,
                                                           in1=sif[:].rearrange("p h q x -> p (h q) x").unsqueeze(2).broadcast_to([128, 16, 16, 16]),
                                                           op=ALU.mult), reads=[T["oh"], T["sif"]], writes=[T["oh2"]])
                    yield
                    P.add("dve", lambda e: e.tensor_reduce(out=sel[:].rearrange("p h q k -> p (h q) k"), in_=oh2, axis=AX.X, op=ALU.add),
                          reads=[T["oh2"]], writes=[T["sel"]])
                    yield
                    P.add("dve", lambda e: e.scalar_tensor_tensor(out=eidf[:], in0=sel[:, :, 0, :], scalar=128.0, in1=sel[:, :, 1, :],
                                                                  op0=ALU.mult, op1=ALU.add), reads=[T["sel"]], writes=[T["eidf"]])
                    P.add("dve", lambda e: e.tensor_copy(IDX[:].rearrange("p (h k) -> p h k", h=8), eidf[:]), reads=[T["eidf"]], writes=[bIDX])
                    yield

                jobs = [(t, s) for t in range(NT2) for s in range(128)]
                nprod = [0]
                ndg = [0]

                prep_gen = {}

                def gather(jn):
                    t, s = jobs[jn]
                    if t in prep_gen:
                        for _ in prep_gen.pop(t):
                            pass
                    k = jn % NB
                    IDX, bIDX = idx[t % 2], b_idx[t % 2]
                    P.add("pool", lambda e: e.indirect_dma_start(out=gb[k][:, :], out_offset=None, in_=uv16,
                                                                 in_offset=bass.IndirectOffsetOnAxis(ap=IDX[:, s:s + 1], axis=0)),
                          reads=[bIDX, b_uv16], writes=[b_gb[k]], dma=True)

                def final_ops(t):
                    H, bH = h2t[t % 2], b_h2t[t % 2]
                    OT, bOT = ot[t % 2], b_ot[t % 2]
                    bankA, bankB = (0, 1) if t % 2 == 0 else (2, 3)
                    P.add("dve", lambda e: e.tensor_tensor(out=yy[:, 0:512], in0=psum[bankA][:, :], in1=H[:, 0:512], op=ALU.add),
                          reads=[pbuf[bankA], bH], writes=[T["yy"]])
                    P.add("dve", lambda e: e.tensor_tensor(out=yy[:, 512:1024], in0=psum[bankB][:, :], in1=H[:, 512:1024], op=ALU.add),
                          reads=[pbuf[bankB], bH], writes=[T["yy"]])
                    P.add("act", lambda e: e.activation(out=junkb[:], in_=yy[:], func=AF.Square, accum_out=st2[:, 4:5]),
                          reads=[T["yy"]], writes=[T["junkb"], T["st2"]])
                    P.add("act", lambda e: e.activation(out=st2[:, 5:6], in_=st2[:, 4:5], func=AF.Sqrt, bias=EPS, scale=1.0 / D),
                          reads=[T["st2"]], writes=[T["st2"]])
                    P.add("dve", lambda e: e.reciprocal(st2[:, 6:7], st2[:, 5:6]), reads=[T["st2"]], writes=[T["st2"]])
                    P.add("dve", lambda e: e.scalar_tensor_tensor(out=OT[:], in0=yy[:], scalar=st2[:, 6:7], in1=gfin[:], op0=ALU.mult,
                                                                  op1=ALU.mult), reads=[T["yy"], T["st2"], C["gfin"]], writes=[bOT])
                    P.add("sp", lambda e: e.dma_start(out=out[t * 128:(t + 1) * 128, :], in_=OT[:]), reads=[bOT], writes=[b_out], dma=True)


                def consume_tile(t):
                    XNB, bXNB = xnb[t % 2], b_xnb[t % 2]
                    GT, bGT = gate[t % 2], b_gate[t % 2]
                    H, bH = h2t[t % 2], b_h2t[t % 2]
                    OT, bOT = ot[t % 2], b_ot[t % 2]
                    bankA, bankB = (0, 1) if t % 2 == 0 else (2, 3)
                    base = t * 128
                    NG = 128 // GS

                    def dots_mul(g):
                        qs = []
                        for s in range(g * GS, (g + 1) * GS):
                            k = (base + s) % NB
                            q = nprod[0] % NPROD
                            nprod[0] += 1
                            qs.append(q)
                            P.add("dve", lambda e, k=k, q=q: e.tensor_tensor(out=prod[q][:], in0=gb[k][:, 0:D], in1=XNB[:], op=ALU.mult),
                                  reads=[b_gb[k], bXNB], writes=[b_prod[q]])
                        return qs

                    def dots_acc(g, qs):
                        for i_, s in enumerate(range(g * GS, (g + 1) * GS)):
                            q = qs[i_]
                            P.add("act", lambda e, s=s, q=q: e.activation(out=junk2[:], in_=prod[q][:], func=AF.Copy,
                                                                          accum_out=act[:, s:s + 1]),
                                  reads=[b_prod[q]], writes=[b_act[g]])

                    qs0 = dots_mul(0)
                    dots_acc(0, qs0)
                    if t == 0:
                        yield
                    for g in range(NG):
                        s0 = g * GS
                        if g == 2 and t > 0:
                            final_ops(t - 1)
                        qs = dots_mul(g + 1) if g + 1 < NG else None
                        P.add("act", lambda e, s0=s0: e.activation(out=ga[:, s0:s0 + GS], in_=act[:, s0:s0 + GS], func=AF.Gelu),
                              reads=[b_act[g]], writes=[b_ga[g]])
                        if qs is not None:
                            dots_acc(g + 1, qs)
                        if t == 0 or g >= 2:
                            yield
                        P.add("dve", lambda e, s0=s0: e.tensor_tensor(out=wgt[:, s0:s0 + GS], in0=ga[:, s0:s0 + GS],
                                                                      in1=GT[:].rearrange("p h k -> p (h k)")[:, s0:s0 + GS], op=ALU.mult),
                              reads=[b_ga[g], bGT], writes=[b_w[g]])
                        for s in range(s0, s0 + GS):
                            k = (base + s) % NB
                            r = ndg[0] % NDG
                            ndg[0] += 1
                            P.add("dve", lambda e, s=s, r=r: e.tensor_tensor(out=dg[r][:], in0=ident_b[:],
                                                                             in1=wgt[:, s:s + 1].broadcast_to([128, 128]), op=ALU.mult),
                                  reads=[b_w[g], C["ident_b"]], writes=[b_dg[r]])
                            P.add("pe", [lambda e, s=s, r=r, k=k: e.matmul(psum[bankA][:, :], dg[r][:], gb[k][:, D:D + 512],
                                                                           start=(s == 0), stop=(s == 127)),
                                         lambda e, s=s, r=r, k=k: e.matmul(psum[bankB][:, :], dg[r][:], gb[k][:, D + 512:2 * D],
                                                                           start=(s == 0), stop=(s == 127))],
                                  reads=[b_dg[r], b_gb[k]], writes=[pbuf[bankA], pbuf[bankB]])
                            jn = base + s
                            if jn + NB < len(jobs):
                                gather(jn + NB)
                        if t == 0 or g >= 2:
                            yield
                for _ in prep(0):
                    pass
                for jn in range(min(NB, len(jobs))):
                    gather(jn)
                for t in range(NT2):
                    side = None
                    if t + 1 < NT2:
                        side = prep(t + 1)
                        prep_gen[t + 1] = side
                    _interleave(consume_tile(t), side, 58, 100)
                final_ops(NT2 - 1)
                P.add("sp", [], reads=[b_out])
                P.emit()

        if 0 in phases:
            phase0()
        if 1 in phases:
            phase1()
        if 2 in phases:
            phase2()
    return nc


def _consts():
    half = 16
    freqs = (10000.0 ** (-np.arange(half, dtype=np.float32) / half)).astype(np.float32)
    pos = np.arange(TP, dtype=np.float32)
    ang = pos[:, None] * freqs[None, :]
    cos = np.cos(ang).astype(np.float32).T
    sin = np.sin(ang).astype(np.float32).T
    cos32 = np.concatenate([cos, cos], axis=0)
    sin32 = np.concatenate([-sin, sin], axis=0)
    rope = np.zeros((NT_FULL, 32, 256), np.float32)
    for i in range(NT_FULL):
        rope[i, :, 0:128] = cos32[:, i * 128:(i + 1) * 128]
        rope[i, :, 128:256] = sin32[:, i * 128:(i + 1) * 128]
    kk = np.arange(128)
    mask = (kk[:, None] <= kk[None, :]).astype(np.float32)
    iota16 = np.tile(np.arange(16, dtype=np.float32)[None, :], (128, 1))
    return rope, mask, iota16, np.eye(128, dtype=np.float32)


def _in_maps(inputs, n_cores=8):
    f = lambda a: np.ascontiguousarray(np.asarray(a, dtype=np.float32))
    x = f(inputs["x"])
    meta = f(inputs["meta"])
    rope, mask, iota16, ident = _consts()
    vecs = np.concatenate([
        f(inputs["g_mix_norm"])[0].reshape(8, 128), f(inputs["g_q"])[0].reshape(2, 128), f(inputs["g_kv"])[0].reshape(1, 128),
        f(inputs["conv_b"])[0].reshape(4, 128), f(inputs["g_conv_ln"])[0].reshape(4, 128), f(inputs["b_conv_ln"])[0].reshape(4, 128),
        f(inputs["g_out_attn"])[0].reshape(4, 128), f(inputs["g_out_conv"])[0].reshape(4, 128)], axis=0)
    shared = {
        "w_in": f(inputs["w_in"])[0], "w_uq": f(inputs["w_uq"])[0], "w_ukv": f(inputs["w_ukv"])[0],
        "conv_w": f(inputs["conv_w"])[0], "w_out": f(inputs["w_out"])[0], "peer_wq": f(inputs["peer_wq"])[0],
        "peer_keys": f(inputs["peer_keys"])[0], "peer_u": f(inputs["peer_u"])[0], "peer_v": f(inputs["peer_v"])[0],
        "vecs": np.ascontiguousarray(vecs), "g_ffn": f(inputs["g_ffn_norm"])[0].reshape(1, D), "g_fin": f(inputs["g_final"]).reshape(1, D),
        "ident": ident, "rope": rope, "mask": mask, "iota16": iota16,
    }
    maps = []
    for b in range(n_cores):
        h0 = np.zeros((TP, D), np.float32)
        h0[:NMETA] = meta
        h0[NMETA:NMETA + SEQ] = x[b]
        m = dict(shared)
        m["h0"] = h0
        maps.append(m)
    return maps


def kernel(**inputs):
    nt = int(os.environ.get("KERNEL_NT", NT_FULL))
    nc = build_program(NT=nt)
    maps = _in_maps(inputs)
    res = run_bass_kernel_spmd(nc, maps, core_ids=list(range(8)))
    return np.stack([np.asarray(r["out"], dtype=np.float32) for r in res.results], axis=0)
```

```python
import os
import math
import numpy as np
import concourse.bass as bass
import concourse.mybir as mybir
from concourse.bass_utils import run_bass_kernel_spmd
from contextlib import ExitStack

F32 = mybir.dt.float32
BF16 = mybir.dt.bfloat16
I32 = mybir.dt.int32
U32 = mybir.dt.uint32
ALU = mybir.AluOpType
AF = mybir.ActivationFunctionType
AX = mybir.AxisListType

D = 1024
SEQ = 4096
NMETA = 16
TP = 4224
NT_FULL = 33
EPS = 1e-6
NEXP = 16384


class Buf:
    __slots__ = ("name", "last_w", "readers", "excl")

    def __init__(self, name, excl=False):
        self.name = name
        self.last_w = None
        self.readers = []
        self.excl = excl


class Op:
    __slots__ = ("eng", "fns", "deps", "signal", "ord", "is_dma", "dsem", "dval", "dprev", "batch")

    def __init__(self, eng, fns, is_dma):
        self.batch = 0
        self.eng = eng
        self.fns = fns
        self.deps = []
        self.signal = False
        self.ord = 0
        self.is_dma = is_dma
        self.dsem = None
        self.dval = 0
        self.dprev = 0


class Prog:
    ENGS = ("pe", "act", "dve", "pool", "sp")

    def __init__(self, nc, stack, dma_sems=None):
        self.nc = nc
        self.ops = []
        self.eng_ops = {e: [] for e in self.ENGS}
        self.sems = {e: stack.enter_context(nc.semaphore("sem_" + e)) for e in self.ENGS}
        self.batch = 0
        self.ordc = {e: 0 for e in self.ENGS}
        self.waited = {e: {} for e in self.ENGS}
        dma_sems = dma_sems or {"sp": 12, "act": 4, "pool": 40}
        self.dsems, self.dcount, self.dnext = {}, {}, {}
        for e, n in dma_sems.items():
            self.dsems[e] = [stack.enter_context(nc.semaphore("dsem_%s%d" % (e, i))) for i in range(n)]
            self.dcount[e] = [0] * n
            self.dnext[e] = 0

    def add(self, eng, fns, reads=(), writes=(), dma=False):
        if callable(fns):
            fns = [fns]
        op = Op(eng, list(fns), dma)
        deps, seen = [], set()

        def dep(d):
            if d is not None and id(d) not in seen:
                seen.add(id(d))
                deps.append(d)

        wr = list(writes) + [b for b in reads if b.excl]
        for b in reads:
            if not b.excl:
                dep(b.last_w)
        for b in wr:
            dep(b.last_w)
            for r in b.readers:
                dep(r)
        for b in reads:
            if not b.excl:
                b.readers.append(op)
        for b in wr:
            b.last_w = op
            b.readers = []
        op.deps = deps
        op.batch = self.batch
        if dma:
            k = self.dnext[eng]
            self.dnext[eng] = (k + 1) % len(self.dsems[eng])
            op.dsem = self.dsems[eng][k]
            op.dprev = self.dcount[eng][k]
            self.dcount[eng][k] += 16
            op.dval = self.dcount[eng][k]
        self.ops.append(op)
        self.eng_ops[eng].append(op)
        return op

    def emit(self):
        nc = self.nc
        cur = self.batch
        for op in self.ops:
            for d in op.deps:
                if not d.is_dma and d.batch == cur:
                    d.signal = True
        for e in self.ENGS:
            n = self.ordc[e]
            for op in self.eng_ops[e]:
                if op.signal and not op.is_dma:
                    n += 1
                    op.ord = n
            self.ordc[e] = n
        prog = self

        def run(ename, engobj):
            waited = prog.waited[ename]

            def wait(sem, val):
                if val <= 0 or waited.get(sem.num, 0) >= val:
                    return
                waited[sem.num] = val
                engobj.wait_ge(sem, val)

            for op in prog.eng_ops[ename]:
                for d in op.deps:
                    if d.is_dma:
                        wait(d.dsem, d.dval)
                    else:
                        if d.batch != cur:
                            continue
                        if d.eng == ename and ename == "pe":
                            continue
                        wait(prog.sems[d.eng], d.ord)
                if op.is_dma:
                    wait(op.dsem, op.dprev)
                last = None
                for fn in op.fns:
                    last = fn(engobj)
                if last is not None:
                    if op.is_dma:
                        last.then_inc(op.dsem, 16)
                    elif op.signal:
                        last.then_inc(prog.sems[ename], 1)
                else:
                    assert not op.signal

        with nc.Block() as block:
            @block.tensor
            def _(eng):
                run("pe", eng)

            @block.scalar
            def _(eng):
                run("act", eng)

            @block.vector
            def _(eng):
                run("dve", eng)

            @block.gpsimd
            def _(eng):
                run("pool", eng)

            @block.sync
            def _(eng):
                run("sp", eng)
        self.batch += 1
        self.ops = []
        self.eng_ops = {e: [] for e in self.ENGS}


def _interleave(main, side, n_main_hint, n_side_hint):
    if side is None:
        for _ in main:
            pass
        return
    ratio = max(1e-9, n_side_hint / max(1, n_main_hint))
    credit = 0.0
    side_done = False
    for _ in main:
        credit += ratio
        while credit >= 1.0 and not side_done:
            credit -= 1.0
            try:
                next(side)
            except StopIteration:
                side_done = True
    if not side_done:
        for _ in side:
            pass


def build_program(NT=NT_FULL, NB=20, phases=(0, 1, 2), dbg_h2=False):
    nc = bass.Bass("TRN2", target_bir_lowering=False)

    def din(name, shape, dt=F32):
        return nc.dram_tensor(name, list(shape), dt, kind="ExternalInput").ap()

    h0 = din("h0", [TP, D])
    w_in = din("w_in", [D, 1440])
    w_uq = din("w_uq", [256, 768])
    w_ukv = din("w_ukv", [128, 1024])
    conv_w = din("conv_w", [31, 512])
    w_out = din("w_out", [D, D])
    peer_wq = din("peer_wq", [D, D])
    peer_keys = din("peer_keys", [8, 2, 128, 64])
    peer_u = din("peer_u", [NEXP, D])
    peer_v = din("peer_v", [NEXP, D])
    vecs = din("vecs", [31, 128])
    g_ffn = din("g_ffn", [1, D])
    g_fin = din("g_fin", [1, D])
    ident_d = din("ident", [128, 128])
    rope_d = din("rope", [NT_FULL, 32, 256])
    mask_d = din("mask", [128, 128])
    iota_d = din("iota16", [128, 16])
    n_out_rows = min(SEQ, NT * 128 - NMETA)
    out = nc.dram_tensor("out", [SEQ, D], F32, kind="ExternalOutput").ap()
    h2s = nc.dram_tensor("h2s", [TP, D], F32, kind="ExternalOutput" if dbg_h2 else "Internal").ap()

    with ExitStack() as st:
        P = Prog(nc, st)
        psum = [st.enter_context(nc.psum_tensor("ps%d" % i, [128, 512], F32)) for i in range(8)]
        pbuf = [Buf("ps%d" % i, excl=True) for i in range(8)]
        b_out = Buf("out")
        uv16 = nc.dram_tensor("uv16", [NEXP, 2 * D], BF16, kind="Internal").ap()
        b_uv16 = Buf("uv16")
        b_h2s = [Buf("h2s%d" % i) for i in range(NT_FULL)]

        def cp_any(eng, o, i):
            if eng == "act":
                return lambda e: e.copy(o, i)
            return lambda e: e.tensor_copy(o, i)

        def phase1():
            with ExitStack() as s1:
                def sb(name, shape, dt):
                    return s1.enter_context(nc.sbuf_tensor("a_" + name, list(shape), dt))

                gen_cycle = [0, 1, 4, 5, 6, 7]
                gen_pos = [0]

                def gbank():
                    b = gen_cycle[gen_pos[0] % len(gen_cycle)]
                    gen_pos[0] += 1
                    return b

                ident_f = sb("ident_f", [128, 128], F32)
                ident_b = sb("ident_b", [128, 128], BF16)
                onesb = sb("onesb", [128, 128], BF16)
                maskf = sb("maskf", [128, 128], F32)
                maskb = sb("maskb", [128, 128], BF16)
                vst = sb("vst", [31, 128], F32)
                cols = sb("cols", [128, 31], F32)
                cw = sb("cw", [128, 4, 31], F32)
                Wb = sb("Wb", [128, 8, 1440], BF16)
                wkrot = sb("wkrot", [128, 8, 96], BF16)
                Wq = sb("Wq", [128, 2, 8, 96], BF16)
                Wqrot = sb("Wqrot", [128, 2, 8, 96], BF16)
                Wk = sb("Wk", [128, 8, 64], BF16)
                Wv = sb("Wv", [128, 8, 64], BF16)
                Wo = sb("Wo", [128, 8, 1024], BF16)
                KT = sb("KT", [128, 8, NT * 128], BF16)
                VP = sb("VP", [128, NT, 8, 65], BF16)
                Gs = [sb("G%d" % i, [128, 4, 158], F32) for i in range(2)]
                b_G = [Buf("G0"), Buf("G1")]
                cwst = Gs[1][0:31, :, :].rearrange("p c t -> p (c t)")[:, 0:512]
                xt = [sb("xt%d" % i, [128, D], F32) for i in range(2)]
                h2 = [sb("h2_0", [128, D], F32)] * 2
                b_xt = [Buf("xt0"), Buf("xt1")]
                b_h2 = [Buf("h2_0")] * 2
                stage = xt + h2[:1]
                b_stage = b_xt + b_h2[:1]
                B = {n: Buf(n) for n in ("ident_f ident_b onesb maskf maskb vst cols cwst cw Wb wkrot Wq Wqrot Wk Wv Wo "
                                         "KT VP G").split()}
                gm, gq, gkv = cols[:, 0:8], cols[:, 8:10], cols[:, 10:11]
                cb, gln, bln = cols[:, 11:15], cols[:, 15:19], cols[:, 19:23]
                goa, goc = cols[:, 23:27], cols[:, 27:31]

                P.add("sp", lambda e: e.dma_start(out=ident_f[:], in_=ident_d), writes=[B["ident_f"]], dma=True)
                P.add("sp", lambda e: e.dma_start(out=maskf[:], in_=mask_d), writes=[B["maskf"]], dma=True)
                P.add("sp", lambda e: e.dma_start(out=vst[:], in_=vecs), writes=[B["vst"]], dma=True)
                P.add("sp", lambda e: e.dma_start(out=cwst, in_=conv_w), writes=[b_G[1]], dma=True)
                P.add("dve", lambda e: e.tensor_copy(ident_b[:], ident_f[:]), reads=[B["ident_f"]], writes=[B["ident_b"]])
                P.add("dve", lambda e: e.tensor_scalar(out=maskb[:], in0=maskf[:], scalar1=-1.0, scalar2=30000.0, op0=ALU.add, op1=ALU.mult),
                      reads=[B["maskf"]], writes=[B["maskb"]])
                P.add("dve", lambda e: e.memset(onesb[:], 1.0), writes=[B["onesb"]])
                P.add("dve", lambda e: e.memset(Gs[0][:], 0.0), writes=[b_G[0]])
                P.add("pool", lambda e: e.memset(VP[:], 1.0), writes=[B["VP"]])
                P.add("pool", lambda e: e.memset(KT[:], 0.0), writes=[B["KT"]])
                bk = gbank()
                P.add("pe", lambda e, bk=bk: e.matmul(psum[bk][:, 0:31], vst[0:31, :], ident_f[0:31, 0:31], start=True, stop=True),
                      reads=[B["vst"], B["ident_f"]], writes=[pbuf[bk]])
                P.add("dve", lambda e, bk=bk: e.tensor_copy(cols[:], psum[bk][:, 0:31]), reads=[pbuf[bk]], writes=[B["cols"]])
                bk = gbank()
                fns = []
                for c in range(4):
                    fns.append(lambda e, bk=bk, c=c: e.matmul(psum[bk][:, c * 31:(c + 1) * 31], cwst[:, c * 128:(c + 1) * 128],
                                                              ident_f[0:31, 0:31], start=(c == 0), stop=(c == 3)))
                P.add("pe", fns, reads=[b_G[1], B["ident_f"]], writes=[pbuf[bk]])
                P.add("dve", lambda e, bk=bk: e.tensor_copy(cw[:].rearrange("p c k -> p (c k)"), psum[bk][:, 0:124]),
                      reads=[pbuf[bk]], writes=[B["cw"]])
                sidx = [0]

                def stage_load(src_ap, ncol):
                    k = sidx[0] % 3
                    sidx[0] += 1
                    P.add("sp", lambda e, k=k: e.dma_start(out=stage[k][:, 0:ncol], in_=src_ap), writes=[b_stage[k]], dma=True)
                    return k

                for c in range(8):
                    for (c0, c1) in ((0, 1024), (1024, 1440)):
                        k = stage_load(w_in[c * 128:(c + 1) * 128, c0:c1], c1 - c0)
                        if c % 2 == 0:
                            P.add("dve", lambda e, k=k, c=c, c0=c0, c1=c1: e.tensor_scalar(out=Wb[:, c, c0:c1], in0=stage[k][:, 0:c1 - c0],
                                                                                           scalar1=gm[:, c:c + 1], scalar2=None, op0=ALU.mult),
                                  reads=[b_stage[k], B["cols"]], writes=[B["Wb"]])
                        else:
                            P.add("act", lambda e, k=k, c=c, c0=c0, c1=c1: e.activation(out=Wb[:, c, c0:c1], in_=stage[k][:, 0:c1 - c0],
                                                                                        func=AF.Copy, scale=gm[:, c:c + 1]),
                                  reads=[b_stage[k], B["cols"]], writes=[B["Wb"]])
                P.add("dve", lambda e: e.tensor_copy(wkrot[:, :, 0:64], Wb[:, :, 1344:1408]), reads=[B["Wb"]], writes=[B["wkrot"]])
                P.add("dve", lambda e: e.tensor_copy(wkrot[:, :, 64:80], Wb[:, :, 1424:1440]), reads=[B["Wb"]], writes=[B["wkrot"]])
                P.add("dve", lambda e: e.tensor_copy(wkrot[:, :, 80:96], Wb[:, :, 1408:1424]), reads=[B["Wb"]], writes=[B["wkrot"]])
                for c in range(2):
                    k = stage_load(w_uq[c * 128:(c + 1) * 128, :], 768)
                    P.add("dve", lambda e, k=k, c=c: e.tensor_scalar(out=Wq[:, c, :, :].rearrange("p h d -> p (h d)"),
                                                                     in0=stage[k][:, 0:768], scalar1=gq[:, c:c + 1],
                                                                     scalar2=None, op0=ALU.mult),
                          reads=[b_stage[k], B["cols"]], writes=[B["Wq"]])
                P.add("dve", lambda e: e.tensor_copy(Wqrot[:, :, :, 0:64], Wq[:, :, :, 0:64]), reads=[B["Wq"]], writes=[B["Wqrot"]])
                P.add("dve", lambda e: e.tensor_copy(Wqrot[:, :, :, 64:80], Wq[:, :, :, 80:96]), reads=[B["Wq"]], writes=[B["Wqrot"]])
                P.add("dve", lambda e: e.tensor_copy(Wqrot[:, :, :, 80:96], Wq[:, :, :, 64:80]), reads=[B["Wq"]], writes=[B["Wqrot"]])
                k = stage_load(w_ukv, 1024)
                stv = stage[k][:, 0:1024].rearrange("p (h x) -> p h x", h=8)
                P.add("dve", lambda e, stv=stv: e.tensor_scalar(out=Wk[:], in0=stv[:, :, 0:64], scalar1=gkv[:, 0:1], scalar2=None,
                                                                op0=ALU.mult), reads=[b_stage[k], B["cols"]], writes=[B["Wk"]])
                P.add("dve", lambda e, stv=stv: e.tensor_scalar(out=Wv[:], in0=stv[:, :, 64:128], scalar1=gkv[:, 0:1], scalar2=None,
                                                                op0=ALU.mult), reads=[b_stage[k], B["cols"]], writes=[B["Wv"]])
                for c in range(8):
                    k = stage_load(w_out[c * 128:(c + 1) * 128, :], 1024)
                    sc = goa[:, c:c + 1] if c < 4 else goc[:, c - 4:c - 3]
                    if c % 2 == 0:
                        P.add("dve", lambda e, k=k, c=c, sc=sc: e.tensor_scalar(out=Wo[:, c, :], in0=stage[k][:, 0:1024], scalar1=sc,
                                                                                scalar2=None, op0=ALU.mult),
                              reads=[b_stage[k], B["cols"]], writes=[B["Wo"]])
                    else:
                        P.add("act", lambda e, k=k, c=c, sc=sc: e.activation(out=Wo[:, c, :], in_=stage[k][:, 0:1024], func=AF.Copy, scale=sc),
                              reads=[b_stage[k], B["cols"]], writes=[B["Wo"]])

                rp = [sb("rp%d" % i, [128, 2, 128], F32) for i in range(2)]
                st1 = sb("st1", [128, 8], F32)
                hs = sb("hs", [128, D], BF16)
                junkb = hs
                hT = sb("hT", [128, 8, 128], BF16)
                sig = sb("sig", [128, 512], F32)
                cv = sb("cv", [128, 4, 128], F32)
                cbf = sb("cbf", [128, 512], BF16)
                c2b = sb("c2b", [128, 512], BF16)
                mean = sb("mean", [128, 128], F32)
                m2 = sb("m2", [128, 128], F32)
                var = sb("var", [128, 128], F32)
                rs = sb("rs", [128, 128], F32)
                tt = sb("tt", [128, 4, 128], F32)
                t2 = sb("t2", [128, 4, 128], F32)
                sl = sb("sl", [128, 4, 128], F32)
                s2b = c2b
                rs2 = sb("rs2", [128, 128], F32)
                MTs = [sb("mixT%d" % i, [128, 8, 128], BF16) for i in range(2)]
                b_mTa = [Buf("mTa0"), Buf("mTa1")]
                b_mTc = [Buf("mTc0"), Buf("mTc1")]
                cq = sb("cq", [128, 2, 128], F32)
                cq2 = sb("cq2", [128, 256], BF16)
                rsq = sb("rsq", [128, 128], F32)
                cqn = sb("cqn", [128, 2, 128], BF16)
                ckv = sb("ckv", [128, 128], F32)
                ckv2 = sb("ckv2", [128, 128], BF16)
                rskv = sb("rskv", [128, 128], F32)
                ckvn = sb("ckvn", [128, 128], BF16)
                kr1 = sb("kr1", [128, 128], F32)
                kr2 = sb("kr2", [128, 128], F32)
                qT = sb("qT", [128, 8, 128], BF16)
                qr1 = tt
                qr2 = t2
                PT = [sb("PT%d" % i, [128, 4, 128], BF16) for i in range(4)]
                rec = sb("rec", [128, 8, 1], F32)
                ao = sig[:].rearrange("p (h x) -> p h x", h=8)
                aob = sb("aob", [128, 512], BF16)
                W = {n: Buf(n) for n in ("junkb st1 hs hT sig cbf c2b mean m2 var rs tt t2 sl s2b rs2 mixTa mixTc cq cq2 "
                                         "rsq cqn ckv ckv2 rskv ckvn kr1 kr2 qT qr1 qr2 rec ao aob ssa rsa").split()}
                b_rp = [Buf("rp0"), Buf("rp1")]
                b_cv = [Buf("cv%d" % c) for c in range(4)]
                b_PT = [Buf("PT%d" % i) for i in range(4)]
                W["junkb"] = W["hs"]
                W["s2b"] = W["c2b"]
                W["qr1"] = W["tt"]
                W["qr2"] = W["t2"]
                W["ao"] = W["sig"]
                ssa = sb("ssa", [128, 4], F32)
                SCALE = 1.0 / math.sqrt(96.0)

                Obank = (2, 3)

                def segA1(i):
                    X, bX = xt[i % 2], b_xt[i % 2]
                    R, bR = rp[i % 2], b_rp[i % 2]
                    tc = slice(i * 128, (i + 1) * 128)
                    G, bG = Gs[i % 2], b_G[i % 2]
                    P.add("sp", lambda e, X=X, i=i: e.dma_start(out=X[:], in_=h0[i * 128:(i + 1) * 128, :]), writes=[bX], dma=True)
                    P.add("sp", lambda e, R=R, i=i: e.dma_start(out=R[64:96, :, :].rearrange("p a t -> p (a t)"), in_=rope_d[i]),
                          writes=[bR], dma=True)
                    P.add("act", lambda e, X=X: e.activation(out=junkb[:], in_=X[:], func=AF.Square, accum_out=st1[:, 0:1]),
                          reads=[bX], writes=[W["junkb"], W["st1"]])
                    P.add("act", lambda e: e.activation(out=st1[:, 1:2], in_=st1[:, 0:1], func=AF.Sqrt, bias=EPS, scale=1.0 / D),
                          reads=[W["st1"]], writes=[W["st1"]])
                    P.add("dve", lambda e: e.reciprocal(st1[:, 2:3], st1[:, 1:2]), reads=[W["st1"]], writes=[W["st1"]])
                    P.add("act", lambda e, X=X: e.activation(out=hs[:], in_=X[:], func=AF.Copy, scale=st1[:, 2:3]),
                          reads=[bX, W["st1"]], writes=[W["hs"]])
                    bk = gbank()
                    psT = psum[bk][:].bitcast(BF16)
                    P.add("pe", [lambda e, c=c, psT=psT: e.transpose(psT[:, c * 128:(c + 1) * 128], hs[:, c * 128:(c + 1) * 128], ident_b[:])
                                 for c in range(8)], reads=[W["hs"], B["ident_b"]], writes=[pbuf[bk]])
                    P.add("dve", lambda e, psT=psT: e.tensor_copy(hT[:].rearrange("p c t -> p (c t)"), psT[:, 0:1024]),
                          reads=[pbuf[bk]], writes=[W["hT"]])
                    bA, bB, bC, bD = gbank(), gbank(), gbank(), gbank()
                    for (bk, col0) in ((bA, 0), (bB, 512)):
                        fns = []
                        for cc in range(4):
                            for c in range(8):
                                fns.append(lambda e, bk=bk, cc=cc, c=c, col0=col0: e.matmul(
                                    psum[bk][:, cc * 128:(cc + 1) * 128], Wb[:, c, col0 + cc * 128:col0 + (cc + 1) * 128], hT[:, c, :],
                                    start=(cc == 0 and c == 0), stop=(cc == 3 and c == 7)))
                        P.add("pe", fns, reads=[B["Wb"], W["hT"]], writes=[pbuf[bk]])
                    fns = []
                    for cc in range(3):
                        for c in range(8):
                            fns.append(lambda e, cc=cc, c=c: e.matmul(
                                psum[bC][:, cc * 128:(cc + 1) * 128], Wb[:, c, 1024 + cc * 128:1024 + (cc + 1) * 128], hT[:, c, :],
                                start=(cc == 0 and c == 0), stop=False))
                    for c in range(8):
                        fns.append(lambda e, c=c: e.matmul(psum[bC][0:96, 384:512], Wb[:, c, 1344:1440], hT[:, c, :],
                                                           start=False, stop=(c == 7)))
                    P.add("pe", fns, reads=[B["Wb"], W["hT"]], writes=[pbuf[bC]])
                    P.add("pe", [lambda e, c=c: e.matmul(psum[bD][0:96, 0:128], wkrot[:, c, :], hT[:, c, :], start=(c == 0), stop=(c == 7))
                                 for c in range(8)], reads=[B["wkrot"], W["hT"]], writes=[pbuf[bD]])
                    P.add("act", lambda e: e.activation(out=sig[:], in_=psum[bB][:, :], func=AF.Sigmoid), reads=[pbuf[bB]], writes=[W["sig"]])
                    P.add("dve", lambda e: e.tensor_tensor(out=G[:, :, 30:158], in0=psum[bA][:, :].rearrange("p (c t) -> p c t", c=4),
                                                           in1=sig[:].rearrange("p (c t) -> p c t", c=4), op=ALU.mult),
                          reads=[pbuf[bA], W["sig"]], writes=[bG])
                    P.add("act", lambda e: e.copy(cq[:].rearrange("p c t -> p (c t)"), psum[bC][:, 0:256]), reads=[pbuf[bC]], writes=[W["cq"]])
                    P.add("act", lambda e: e.activation(out=cq2[:], in_=psum[bC][:, 0:256], func=AF.Square), reads=[pbuf[bC]], writes=[W["cq2"]])
                    P.add("act", lambda e: e.copy(ckv[:], psum[bC][:, 256:384]), reads=[pbuf[bC]], writes=[W["ckv"]])
                    P.add("act", lambda e: e.activation(out=ckv2[:], in_=psum[bC][:, 256:384], func=AF.Square), reads=[pbuf[bC]], writes=[W["ckv2"]])
                    P.add("dve", lambda e, R=R: e.tensor_tensor(out=kr1[64:96, :], in0=psum[bC][64:96, 384:512], in1=R[64:96, 0, :], op=ALU.mult),
                          reads=[pbuf[bC], bR], writes=[W["kr1"]])
                    P.add("dve", lambda e, R=R: e.tensor_tensor(out=kr2[64:96, :], in0=psum[bD][64:96, 0:128], in1=R[64:96, 1, :], op=ALU.mult),
                          reads=[pbuf[bD], bR], writes=[W["kr2"]])
                    P.add("dve", lambda e, tc=tc: e.tensor_tensor(out=KT[64:96, 0, tc], in0=kr1[64:96, :], in1=kr2[64:96, :], op=ALU.add),
                          reads=[W["kr1"], W["kr2"]], writes=[B["KT"]])
                    P.add("dve", lambda e, tc=tc: e.tensor_copy(KT[64:96, 1:8, tc], KT[64:96, 0:1, tc].broadcast_to([32, 7, 128])),
                          reads=[B["KT"]], writes=[B["KT"]])
                    bE = gbank()
                    P.add("pe", [lambda e, c=c: e.matmul(psum[bE][:, 0:128], onesb[:], cq2[:, c * 128:(c + 1) * 128], start=(c == 0), stop=False)
                                 for c in range(2)] +
                          [lambda e: e.matmul(psum[bE][:, 128:256], onesb[:], ckv2[:], start=False, stop=True)],
                          reads=[B["onesb"], W["cq2"], W["ckv2"]], writes=[pbuf[bE]])
                    P.add("act", lambda e: e.activation(out=rsq[:], in_=psum[bE][:, 0:128], func=AF.Sqrt, bias=EPS, scale=1.0 / 256),
                          reads=[pbuf[bE]], writes=[W["rsq"]])
                    P.add("act", lambda e: e.activation(out=rskv[:], in_=psum[bE][:, 128:256], func=AF.Sqrt, bias=EPS, scale=1.0 / 128),
                          reads=[pbuf[bE]], writes=[W["rskv"]])
                    P.add("dve", lambda e: e.reciprocal(rsq[:], rsq[:]), reads=[W["rsq"]], writes=[W["rsq"]])
                    P.add("dve", lambda e: e.reciprocal(rskv[:], rskv[:]), reads=[W["rskv"]], writes=[W["rskv"]])
                    P.add("dve", lambda e: e.tensor_tensor(out=cqn[:], in0=cq[:], in1=rsq[:].unsqueeze(1).broadcast_to([128, 2, 128]), op=ALU.mult),
                          reads=[W["cq"], W["rsq"]], writes=[W["cqn"]])
                    P.add("dve", lambda e: e.tensor_tensor(out=ckvn[:], in0=ckv[:], in1=rskv[:], op=ALU.mult),
                          reads=[W["ckv"], W["rskv"]], writes=[W["ckvn"]])
                    bq = [gbank(), gbank()]
                    for half in range(2):
                        fns = []
                        for hh in range(4):
                            h = half * 4 + hh
                            for c in range(2):
                                fns.append(lambda e, half=half, hh=hh, h=h, c=c: e.matmul(
                                    psum[bq[half]][0:96, hh * 128:(hh + 1) * 128], Wq[:, c, h, :], cqn[:, c, :],
                                    start=(hh == 0 and c == 0), stop=(hh == 3 and c == 1)))
                        P.add("pe", fns, reads=[B["Wq"], W["cqn"]], writes=[pbuf[bq[half]]])
                    bkk = [gbank(), gbank()]
                    for half in range(2):
                        P.add("pe", [lambda e, half=half, hh=hh: e.matmul(psum[bkk[half]][0:64, hh * 128:(hh + 1) * 128], Wk[:, half * 4 + hh, :],
                                                                          ckvn[:], start=(hh == 0), stop=(hh == 3)) for hh in range(4)],
                              reads=[B["Wk"], W["ckvn"]], writes=[pbuf[bkk[half]]])
                    for half in range(2):
                        P.add("act", lambda e, half=half: e.copy(qT[0:64, half * 4:(half + 1) * 4, :],
                                                                 psum[bq[half]][0:64, :].rearrange("p (h t) -> p h t", h=4)),
                              reads=[pbuf[bq[half]]], writes=[W["qT"]])
                        P.add("dve", lambda e, half=half, R=R: e.tensor_tensor(
                            out=qr1[64:96, :, :], in0=psum[bq[half]][64:96, :].rearrange("p (h t) -> p h t", h=4),
                            in1=R[64:96, 0:1, :].broadcast_to([32, 4, 128]), op=ALU.mult),
                            reads=[pbuf[bq[half]], bR], writes=[W["qr1"]])
                        brot = gbank()
                        fns = []
                        for hh in range(4):
                            h = half * 4 + hh
                            for c in range(2):
                                fns.append(lambda e, brot=brot, hh=hh, h=h, c=c: e.matmul(
                                    psum[brot][0:96, hh * 128:(hh + 1) * 128], Wqrot[:, c, h, :], cqn[:, c, :],
                                    start=(hh == 0 and c == 0), stop=(hh == 3 and c == 1)))
                        P.add("pe", fns, reads=[B["Wqrot"], W["cqn"]], writes=[pbuf[brot]])
                        P.add("dve", lambda e, brot=brot, R=R: e.tensor_tensor(
                            out=qr2[64:96, :, :], in0=psum[brot][64:96, :].rearrange("p (h t) -> p h t", h=4),
                            in1=R[64:96, 1:2, :].broadcast_to([32, 4, 128]), op=ALU.mult),
                            reads=[pbuf[brot], bR], writes=[W["qr2"]])
                        P.add("dve", lambda e, half=half: e.tensor_tensor(out=qT[64:96, half * 4:(half + 1) * 4, :], in0=qr1[64:96, :, :],
                                                                          in1=qr2[64:96, :, :], op=ALU.add),
                              reads=[W["qr1"], W["qr2"]], writes=[W["qT"]])
                        P.add("act", lambda e, half=half, tc=tc: e.copy(KT[0:64, half * 4:(half + 1) * 4, tc],
                                                                        psum[bkk[half]][0:64, :].rearrange("p (h t) -> p h t", h=4)),
                              reads=[pbuf[bkk[half]]], writes=[B["KT"]])
                    bv = gbank()
                    P.add("pe", lambda e: e.matmul(psum[bv][:, :], ckvn[:], Wv[:].rearrange("p h x -> p (h x)"), start=True, stop=True),
                          reads=[W["ckvn"], B["Wv"]], writes=[pbuf[bv]])
                    P.add("act", lambda e, i=i: e.copy(VP[:, i, :, 0:64], psum[bv][:, :].rearrange("p (h x) -> p h x", h=8)),
                          reads=[pbuf[bv]], writes=[B["VP"]])
                    pend = None
                    nPT = [0]

                    def issue_pv(j, pts):
                        for half in range(2):
                            k_pt = pts[half]
                            P.add("pe", [lambda e, half=half, hh=hh, k_pt=k_pt, j=j: e.matmul(
                                psum[Obank[half]][:, hh * 65:(hh + 1) * 65], PT[k_pt][:, hh, :], VP[:, j, half * 4 + hh, :],
                                start=(j == 0 and hh == 0), stop=(j == i and hh == 3)) for hh in range(4)],
                                reads=[b_PT[k_pt], B["VP"]], writes=[pbuf[Obank[half]]])

                    for j in range(i + 1):
                        pts = []
                        for half in range(2):
                            sbk = (4 + half) if (j % 2 == 0) else (6 + half)
                            fns = [lambda e, half=half, hh=hh, sbk=sbk, j=j: e.matmul(
                                psum[sbk][:, hh * 128:(hh + 1) * 128], KT[0:96, half * 4 + hh, j * 128:(j + 1) * 128],
                                qT[0:96, half * 4 + hh, :], start=(hh == 0), stop=(hh == 3 and j != i)) for hh in range(4)]
                            if j == i:
                                fns += [lambda e, hh=hh, sbk=sbk: e.matmul(psum[sbk][:, hh * 128:(hh + 1) * 128], ident_b[:], maskb[:],
                                                                           start=False, stop=(hh == 3)) for hh in range(4)]
                            P.add("pe", fns, reads=[B["KT"], W["qT"], B["ident_b"], B["maskb"]], writes=[pbuf[sbk]])
                            k_pt = nPT[0] % 4
                            nPT[0] += 1
                            P.add("act", lambda e, sbk=sbk, k_pt=k_pt: e.activation(out=PT[k_pt][:].rearrange("p h t -> p (h t)"),
                                                                                    in_=psum[sbk][:, :], func=AF.Exp, scale=SCALE),
                                  reads=[pbuf[sbk]], writes=[b_PT[k_pt]])
                            pts.append(k_pt)
                        if pend is not None:
                            issue_pv(*pend)
                        pend = (j, pts)
                    issue_pv(*pend)
                def segB1(i):
                    G, bG = Gs[i % 2], b_G[i % 2]
                    Gn, bGn = Gs[(i + 1) % 2], b_G[(i + 1) % 2]
                    mixT = MTs[i % 2]
                    for k in range(31):
                        for c in range(4):
                            if k == 0:
                                P.add("dve", lambda e, c=c: e.tensor_scalar(out=cv[:, c, :], in0=G[:, c, 0:128], scalar1=cw[:, c, 0:1],
                                                                            scalar2=cb[:, c:c + 1], op0=ALU.mult, op1=ALU.add),
                                      reads=[bG, B["cw"], B["cols"]], writes=[b_cv[c]])
                            else:
                                P.add("dve", lambda e, c=c, k=k: e.scalar_tensor_tensor(out=cv[:, c, :], in0=G[:, c, k:k + 128],
                                                                                        scalar=cw[:, c, k:k + 1], in1=cv[:, c, :],
                                                                                        op0=ALU.mult, op1=ALU.add),
                                      reads=[bG, B["cw"], b_cv[c]], writes=[b_cv[c]])
                    P.add("dve", lambda e: e.tensor_copy(Gn[:, :, 0:30], G[:, :, 128:158]), reads=[bG], writes=[bGn])
                    cvf = cv[:].rearrange("p c t -> p (c t)")
                    P.add("act", lambda e: e.copy(cbf[:], cvf), reads=b_cv, writes=[W["cbf"]])
                    P.add("act", lambda e: e.activation(out=c2b[:], in_=cvf, func=AF.Square), reads=b_cv, writes=[W["c2b"]])
                    bS = gbank()
                    P.add("pe", [lambda e, c=c: e.matmul(psum[bS][:, 0:128], onesb[:], cbf[:, c * 128:(c + 1) * 128], start=(c == 0), stop=False)
                                 for c in range(4)] +
                          [lambda e, c=c: e.matmul(psum[bS][:, 128:256], onesb[:], c2b[:, c * 128:(c + 1) * 128], start=False, stop=(c == 3))
                           for c in range(4)], reads=[B["onesb"], W["cbf"], W["c2b"]], writes=[pbuf[bS]])
                    P.add("dve", lambda e: e.tensor_scalar(out=mean[:], in0=psum[bS][:, 0:128], scalar1=1.0 / 512, scalar2=None, op0=ALU.mult),
                          reads=[pbuf[bS]], writes=[W["mean"]])
                    P.add("dve", lambda e: e.tensor_tensor(out=m2[:], in0=mean[:], in1=mean[:], op=ALU.mult), reads=[W["mean"]], writes=[W["m2"]])
                    P.add("dve", lambda e: e.scalar_tensor_tensor(out=var[:], in0=psum[bS][:, 128:256], scalar=1.0 / 512, in1=m2[:],
                                                                  op0=ALU.mult, op1=ALU.subtract),
                          reads=[pbuf[bS], W["m2"]], writes=[W["var"]])
                    P.add("act", lambda e: e.activation(out=rs[:], in_=var[:], func=AF.Sqrt, bias=EPS, scale=1.0), reads=[W["var"]], writes=[W["rs"]])
                    P.add("dve", lambda e: e.reciprocal(rs[:], rs[:]), reads=[W["rs"]], writes=[W["rs"]])
                    P.add("dve", lambda e: e.tensor_tensor(out=tt[:], in0=cv[:], in1=mean[:].unsqueeze(1).broadcast_to([128, 4, 128]), op=ALU.subtract),
                          reads=b_cv + [W["mean"]], writes=[W["tt"]])
                    P.add("dve", lambda e: e.tensor_tensor(out=t2[:], in0=tt[:], in1=rs[:].unsqueeze(1).broadcast_to([128, 4, 128]), op=ALU.mult),
                          reads=[W["tt"], W["rs"]], writes=[W["t2"]])
                    for c in range(4):
                        P.add("act", lambda e, c=c: e.activation(out=sl[:, c, :], in_=t2[:, c, :], func=AF.Silu, bias=bln[:, c:c + 1],
                                                                 scale=gln[:, c:c + 1]),
                              reads=[W["t2"], B["cols"]], writes=[W["sl"]])
                    P.add("act", lambda e: e.activation(out=s2b[:], in_=sl[:].rearrange("p c t -> p (c t)"), func=AF.Square),
                          reads=[W["sl"]], writes=[W["s2b"]])
                    bS2 = gbank()
                    P.add("pe", [lambda e, c=c: e.matmul(psum[bS2][:, 0:128], onesb[:], s2b[:, c * 128:(c + 1) * 128], start=(c == 0), stop=(c == 3))
                                 for c in range(4)], reads=[B["onesb"], W["s2b"]], writes=[pbuf[bS2]])
                    P.add("act", lambda e: e.activation(out=rs2[:], in_=psum[bS2][:, 0:128], func=AF.Sqrt, bias=EPS, scale=1.0 / 512),
                          reads=[pbuf[bS2]], writes=[W["rs2"]])
                    P.add("dve", lambda e: e.reciprocal(rs2[:], rs2[:]), reads=[W["rs2"]], writes=[W["rs2"]])
                    P.add("dve", lambda e: e.tensor_tensor(out=mixT[:, 4:8, :], in0=sl[:], in1=rs2[:].unsqueeze(1).broadcast_to([128, 4, 128]),
                                                           op=ALU.mult), reads=[W["sl"], W["rs2"]], writes=[b_mTc[i % 2]])
                def segA2(i):
                    mixT = MTs[i % 2]
                    for half in range(2):
                        ov = psum[Obank[half]][:, 0:260].rearrange("p (h x) -> p h x", h=4)
                        P.add("dve", lambda e, ov=ov, half=half: e.reciprocal(rec[:, half * 4:(half + 1) * 4, :], ov[:, :, 64:65]),
                              reads=[pbuf[Obank[half]]], writes=[W["rec"]])
                        P.add("dve", lambda e, ov=ov, half=half: e.tensor_tensor(
                            out=ao[:, half * 4:(half + 1) * 4, :], in0=ov[:, :, 0:64],
                            in1=rec[:, half * 4:(half + 1) * 4, :].broadcast_to([128, 4, 64]), op=ALU.mult),
                            reads=[pbuf[Obank[half]], W["rec"]], writes=[W["ao"]])
                    aof = sig[:]
                    P.add("act", lambda e: e.activation(out=junkb[:, 0:512], in_=aof, func=AF.Square, accum_out=ssa[:, 0:1]),
                          reads=[W["ao"]], writes=[W["junkb"], W["ssa"]])
                    P.add("act", lambda e: e.activation(out=ssa[:, 1:2], in_=ssa[:, 0:1], func=AF.Sqrt, bias=EPS, scale=1.0 / 512),
                          reads=[W["ssa"]], writes=[W["ssa"]])
                    P.add("dve", lambda e: e.reciprocal(ssa[:, 2:3], ssa[:, 1:2]), reads=[W["ssa"]], writes=[W["ssa"]])
                    P.add("act", lambda e: e.activation(out=aob[:], in_=aof, func=AF.Copy, scale=ssa[:, 2:3]), reads=[W["ao"], W["ssa"]],
                          writes=[W["aob"]])
                    bk = gbank()
                    psT = psum[bk][:].bitcast(BF16)
                    P.add("pe", [lambda e, c=c, psT=psT: e.transpose(psT[:, c * 128:(c + 1) * 128], aob[:, c * 128:(c + 1) * 128], ident_b[:])
                                 for c in range(4)], reads=[W["aob"], B["ident_b"]], writes=[pbuf[bk]])
                    P.add("dve", lambda e, psT=psT: e.tensor_copy(mixT[:, 0:4, :].rearrange("p c t -> p (c t)"), psT[:, 0:512]),
                          reads=[pbuf[bk]], writes=[b_mTa[i % 2]])

                def segB2(i):
                    X, bX = xt[i % 2], b_xt[i % 2]
                    mixT = MTs[i % 2]
                    H2, bH2 = h2[i % 2], b_h2[i % 2]
                    for half in range(2):
                        bk = gbank()
                        P.add("pe", [lambda e, bk=bk, c=c, half=half: e.matmul(psum[bk][:, :], mixT[:, c, :], Wo[:, c, half * 512:(half + 1) * 512],
                                                                               start=(c == 0), stop=(c == 7)) for c in range(8)],
                              reads=[b_mTa[i % 2], b_mTc[i % 2], B["Wo"]], writes=[pbuf[bk]])
                        P.add("dve", lambda e, bk=bk, half=half, X=X, H2=H2: e.tensor_tensor(
                            out=H2[:, half * 512:(half + 1) * 512], in0=psum[bk][:, :], in1=X[:, half * 512:(half + 1) * 512], op=ALU.add),
                            reads=[pbuf[bk], bX], writes=[bH2])
                    P.add("sp", lambda e, H2=H2, i=i: e.dma_start(out=h2s[i * 128:(i + 1) * 128, :], in_=H2[:]), reads=[bH2],
                          writes=[b_h2s[i]], dma=True)
                segA1(0)
                segA2(0)
                for i in range(NT):
                    if i + 1 < NT:
                        segA1(i + 1)
                    segB1(i)
                    segB2(i)
                    if i + 1 < NT:
                        segA2(i + 1)
                P.add("sp", [], reads=[b for b in b_h2s[:NT]])
                P.emit()

        def phase0():
            with ExitStack() as s0:
                def sb(name, shape, dt):
                    return s0.enter_context(nc.sbuf_tensor("z_" + name, list(shape), dt))
                R = 4
                NCH = 128 // R
                uin = [sb("uin%d" % i, [128, R, D], F32) for i in range(2)]
                vin = [sb("vin%d" % i, [128, R, D], F32) for i in range(2)]
                uvo = [sb("uvo%d" % i, [128, R, 2 * D], BF16) for i in range(2)]
                b_uin = [Buf("uin0"), Buf("uin1")]
                b_vin = [Buf("vin0"), Buf("vin1")]
                b_uvo = [Buf("uvo0"), Buf("uvo1")]
                uview = peer_u.rearrange("(p r) d -> p r d", p=128)
                vview = peer_v.rearrange("(p r) d -> p r d", p=128)
                oview = uv16.rearrange("(p r) d -> p r d", p=128)
                for c in range(NCH):
                    k = c % 2
                    P.add("sp", lambda e, k=k, c=c: e.dma_start(out=uin[k][:], in_=uview[:, c * R:(c + 1) * R, :]), writes=[b_uin[k]], dma=True)
                    P.add("act", lambda e, k=k, c=c: e.dma_start(out=vin[k][:], in_=vview[:, c * R:(c + 1) * R, :]), writes=[b_vin[k]], dma=True)
                    P.add("dve", lambda e, k=k: e.tensor_copy(uvo[k][:, :, 0:D], uin[k][:]), reads=[b_uin[k]], writes=[b_uvo[k]])
                    P.add("act", lambda e, k=k: e.copy(uvo[k][:, :, D:2 * D], vin[k][:]), reads=[b_vin[k]], writes=[b_uvo[k]])
                    P.add("sp", lambda e, k=k, c=c: e.dma_start(out=oview[:, c * R:(c + 1) * R, :], in_=uvo[k][:]), reads=[b_uvo[k]],
                          writes=[b_uv16], dma=True)
                P.add("sp", [], reads=[b_uv16])
                P.emit()

        def phase2():
            NT2 = 32 if NT >= NT_FULL else NT - 1
            GS = 4
            with ExitStack() as s2:
                tot = [0]

                def sb(name, shape, dt):
                    n = 1
                    for x in shape[1:]:
                        n *= x
                    tot[0] += n * (4 if dt in (F32, I32, U32) else 2)
                    if os.environ.get("KERNEL_SBDBG"):
                        print("sbuf p2", name, tot[0])
                    return s2.enter_context(nc.sbuf_tensor("b_" + name, list(shape), dt))

                gcyc = [4, 5, 6, 7]
                gpos = [0]

                def gbank():
                    b = gcyc[gpos[0] % 4]
                    gpos[0] += 1
                    return b

                ident_f = sb("ident_f", [128, 128], F32)
                ident_b = sb("ident_b", [128, 128], BF16)
                iota16 = sb("iota16", [128, 16], F32)
                gffn = sb("gffn", [128, D], F32)
                gfin = sb("gfin", [128, D], F32)
                Wpq = sb("Wpq", [128, 8, D], BF16)
                KB = sb("KB", [128, 8, 256], BF16)
                kst = sb("kst", [128, 128], F32)
                ot = [sb("ot%d" % i, [128, D], F32) for i in range(2)]
                b_ot = [Buf("ot0"), Buf("ot1")]
                stage = ot
                b_stage = b_ot
                C = {n: Buf(n) for n in "ident_f ident_b iota16 gffn gfin Wpq KB kst".split()}
                P.add("sp", lambda e: e.dma_start(out=ident_f[:], in_=ident_d), writes=[C["ident_f"]], dma=True)
                P.add("sp", lambda e: e.dma_start(out=iota16[:], in_=iota_d), writes=[C["iota16"]], dma=True)
                P.add("sp", lambda e: e.dma_start(out=gffn[:], in_=g_ffn.partition_broadcast(128)), writes=[C["gffn"]], dma=True)
                P.add("sp", lambda e: e.dma_start(out=gfin[:], in_=g_fin.partition_broadcast(128)), writes=[C["gfin"]], dma=True)
                P.add("dve", lambda e: e.tensor_copy(ident_b[:], ident_f[:]), reads=[C["ident_f"]], writes=[C["ident_b"]])
                P.add("dve", lambda e: e.memset(KB[:], 0.0), writes=[C["KB"]])
                for c in range(8):
                    k = c % 2
                    P.add("sp", lambda e, k=k, c=c: e.dma_start(out=stage[k][:], in_=peer_wq[c * 128:(c + 1) * 128, :]),
                          writes=[b_stage[k]], dma=True)
                    P.add("dve" if c % 2 == 0 else "act", cp_any("dve" if c % 2 == 0 else "act", Wpq[:, c, :], stage[k][:]),
                          reads=[b_stage[k]], writes=[C["Wpq"]])
                for h in range(8):
                    P.add("sp", lambda e, h=h: e.dma_start(out=kst[:].rearrange("n (p d) -> n p d", p=2),
                                                           in_=peer_keys[h].rearrange("p n d -> n p d")),
                          writes=[C["kst"]], dma=True)
                    bk = gbank()
                    P.add("pe", lambda e, bk=bk: e.transpose(psum[bk][:, 0:128], kst[:], ident_f[:]), reads=[C["kst"], C["ident_f"]],
                          writes=[pbuf[bk]])
                    P.add("dve", lambda e, bk=bk, h=h: e.tensor_copy(KB[0:64, h, 0:128], psum[bk][0:64, 0:128]), reads=[pbuf[bk]],
                          writes=[C["KB"]])
                    P.add("dve", lambda e, bk=bk, h=h: e.tensor_copy(KB[64:128, h, 128:256], psum[bk][64:128, 0:128]), reads=[pbuf[bk]],
                          writes=[C["KB"]])

                h2t = [sb("h2t%d" % i, [128, D], F32) for i in range(2)]
                xn = sb("xn", [128, D], F32)
                xnb = [sb("xnb%d" % i, [128, D], BF16) for i in range(2)]
                idx = [sb("idx%d" % i, [128, 128], I32) for i in range(2)]
                gate = [sb("gate%d" % i, [128, 8, 16], F32) for i in range(2)]
                st2 = sb("st2", [128, 8], F32)
                junkb = sb("junkb", [128, D], BF16)
                xnT = sb("xnT", [128, 8, 128], BF16)
                qpT = sb("qpT", [128, 8, 128], BF16)
                S = sb("S", [128, 16, 128], F32)
                S2 = sb("S2", [128, 16, 128], F32)
                sv = sb("sv", [128, 8, 2, 16], F32)
                si = sb("si", [128, 8, 2, 16], U32)
                sif = sb("sif", [128, 8, 2, 16], F32)
                cand = S[:].rearrange("p (h q) n -> p h (q n)", q=2)
                cand2 = S2[:].rearrange("p (h q) n -> p h (q n)", q=2)
                tv = sb("tv", [128, 8, 16], F32)
                tp = sb("tp", [128, 8, 16], U32)
                tpf = sb("tpf", [128, 8, 16], F32)
                cmp = cand2.rearrange("p h (a b) -> p h a b", a=16)
                abf = sb("abf", [128, 8, 2, 16], F32)
                oh = S2[:].rearrange("p g n -> p (g n)").bitcast(BF16).rearrange("p (g k x) -> p g k x", g=16, k=16)
                oh2 = S[:].rearrange("p g n -> p (g n)").bitcast(BF16).rearrange("p (g k x) -> p g k x", g=16, k=16)
                sel = sb("sel", [128, 8, 2, 16], F32)
                eidf = sb("eidf", [128, 8, 16], F32)
                dd = sb("dd", [128, 8, 16], F32)
                ee = sb("ee", [128, 8, 16], F32)
                zz = sb("zz", [128, 8], F32)
                act = sb("act", [128, 128], F32)
                ga = sb("ga", [128, 128], F32)
                wgt = sb("wgt", [128, 128], F32)
                junk = junkb
                junk2 = junkb
                NPROD = 8
                prod = [sb("prod%d" % i, [128, D], BF16) for i in range(NPROD)]
                b_prod = [Buf("prod%d" % i) for i in range(NPROD)]
                NDG = 8
                dg = [sb("dg%d" % i, [128, 128], BF16) for i in range(NDG)]
                b_dg = [Buf("dg%d" % i) for i in range(NDG)]
                yy = sb("yy", [128, D], F32)
                gb = [sb("gb%d" % i, [128, 2 * D], BF16) for i in range(NB)]
                b_gb = [Buf("gb%d" % i) for i in range(NB)]
                b_h2t = [Buf("h2t0"), Buf("h2t1")]
                b_xnb = [Buf("xnb0"), Buf("xnb1")]
                b_idx = [Buf("idx0"), Buf("idx1")]
                b_gate = [Buf("gate0"), Buf("gate1")]
                b_act = [Buf("act%d" % s) for s in range(128 // GS)]
                b_ga = [Buf("ga%d" % s) for s in range(128 // GS)]
                b_w = [Buf("w%d" % s) for s in range(128 // GS)]
                T = {n: Buf(n) for n in ("st2 junkb xn xnT qpT S S2 sv si sif cand cand2 tv tp tpf abf sel eidf dd ee zz yy").split()}
                T["cand"] = T["S"]
                T["cand2"] = T["S2"]
                T["cmp"] = T["S2"]
                T["oh"] = T["S2"]
                T["oh2"] = T["S"]
                thr16 = sb("thr16", [128, 16], F32)
                C["thr16"] = Buf("thr16")
                P.add("dve", lambda e: e.tensor_scalar(out=thr16[:], in0=iota16[:], scalar1=16.0, scalar2=None, op0=ALU.mult),
                      reads=[C["iota16"]], writes=[C["thr16"]])

                def prep(t):
                    H, bH = h2t[t % 2], b_h2t[t % 2]
                    XNB, bXNB = xnb[t % 2], b_xnb[t % 2]
                    IDX, bIDX = idx[t % 2], b_idx[t % 2]
                    GT, bGT = gate[t % 2], b_gate[t % 2]
                    r0 = NMETA + t * 128
                    P.add("sp", lambda e: e.dma_start(out=H[:], in_=h2s[r0:r0 + 128, :]), reads=[b_h2s[t], b_h2s[min(t + 1, NT_FULL - 1)]],
                          writes=[bH], dma=True)
                    yield
                    P.add("act", lambda e: e.activation(out=junkb[:], in_=H[:], func=AF.Square, accum_out=st2[:, 0:1]),
                          reads=[bH], writes=[T["junkb"], T["st2"]])
                    P.add("act", lambda e: e.activation(out=st2[:, 1:2], in_=st2[:, 0:1], func=AF.Sqrt, bias=EPS, scale=1.0 / D),
                          reads=[T["st2"]], writes=[T["st2"]])
                    yield
                    P.add("dve", lambda e: e.reciprocal(st2[:, 2:3], st2[:, 1:2]), reads=[T["st2"]], writes=[T["st2"]])
                    yield
                    P.add("dve", lambda e: e.scalar_tensor_tensor(out=xn[:], in0=H[:], scalar=st2[:, 2:3], in1=gffn[:], op0=ALU.mult,
                                                                  op1=ALU.mult), reads=[bH, T["st2"], C["gffn"]], writes=[T["xn"]])
                    yield
                    P.add("act", lambda e: e.copy(XNB[:], xn[:]), reads=[T["xn"]], writes=[bXNB])
                    bk = gbank()
                    psT = psum[bk][:].bitcast(BF16)
                    P.add("pe", [lambda e, c=c, psT=psT: e.transpose(psT[:, c * 128:(c + 1) * 128], XNB[:, c * 128:(c + 1) * 128], ident_b[:])
                                 for c in range(8)], reads=[bXNB, C["ident_b"]], writes=[pbuf[bk]])
                    P.add("act", lambda e, psT=psT: e.copy(xnT[:].rearrange("p c t -> p (c t)"), psT[:, 0:1024]), reads=[pbuf[bk]],
                          writes=[T["xnT"]])
                    yield
                    for half in range(2):
                        bk = gbank()
                        fns = []
                        for hh in range(4):
                            f = half * 4 + hh
                            for c in range(8):
                                fns.append(lambda e, bk=bk, hh=hh, f=f, c=c: e.matmul(
                                    psum[bk][:, hh * 128:(hh + 1) * 128], Wpq[:, c, f * 128:(f + 1) * 128], xnT[:, c, :],
                                    start=(hh == 0 and c == 0), stop=(hh == 3 and c == 7)))
                        P.add("pe", fns, reads=[C["Wpq"], T["xnT"]], writes=[pbuf[bk]])
                        P.add("act", lambda e, bk=bk, half=half: e.copy(qpT[:, half * 4:(half + 1) * 4, :].rearrange("p h t -> p (h t)"),
                                                                        psum[bk][:, :]), reads=[pbuf[bk]], writes=[T["qpT"]])
                        yield
                    for q4 in range(4):
                        bk = gbank()
                        P.add("pe", [lambda e, bk=bk, q4=q4, u=u: e.matmul(psum[bk][:, u * 256:(u + 1) * 256], qpT[:, q4 * 2 + u, :],
                                                                          KB[:, q4 * 2 + u, :], start=(u == 0), stop=(u == 1))
                                     for u in range(2)], reads=[T["qpT"], C["KB"]], writes=[pbuf[bk]])
                        P.add("act", lambda e, bk=bk, q4=q4: e.copy(S[:, q4 * 4:(q4 + 1) * 4, :].rearrange("p g n -> p (g n)"), psum[bk][:, :]),
                              reads=[pbuf[bk]], writes=[T["S"]])
                        yield
                    for g in range(16):
                        h, p = g // 2, g % 2
                        P.add("dve", lambda e, g=g, h=h, p=p: e.max(out=sv[:, h, p, 0:8], in_=S[:, g, :]), reads=[T["S"]], writes=[T["sv"]])
                        P.add("dve", lambda e, g=g, h=h, p=p: e.max_index(out=si[:, h, p, 0:8], in_max=sv[:, h, p, 0:8], in_values=S[:, g, :]),
                              reads=[T["S"], T["sv"]], writes=[T["si"]])
                        yield
                        P.add("dve", lambda e, g=g, h=h, p=p: e.match_replace(out=S2[:, g, :], in_to_replace=sv[:, h, p, 0:8],
                                                                              in_values=S[:, g, :], imm_value=-1e30),
                              reads=[T["S"], T["sv"]], writes=[T["S2"]])
                        P.add("dve", lambda e, g=g, h=h, p=p: e.max(out=sv[:, h, p, 8:16], in_=S2[:, g, :]), reads=[T["S2"]], writes=[T["sv"]])
                        yield
                        P.add("dve", lambda e, g=g, h=h, p=p: e.max_index(out=si[:, h, p, 8:16], in_max=sv[:, h, p, 8:16], in_values=S2[:, g, :]),
                              reads=[T["S2"], T["sv"]], writes=[T["si"]])
                        yield
                    P.add("dve", lambda e: e.tensor_tensor(out=cand.rearrange("p h (a b) -> p h a b", a=16),
                                                           in0=sv[:, :, 0, :].unsqueeze(3).broadcast_to([128, 8, 16, 16]),
                                                           in1=sv[:, :, 1, :].unsqueeze(2).broadcast_to([128, 8, 16, 16]), op=ALU.add),
                          reads=[T["sv"]], writes=[T["cand"]])
                    yield
                    for h in range(8):
                        P.add("dve", lambda e, h=h: e.max(out=tv[:, h, 0:8], in_=cand[:, h, :]), reads=[T["cand"]], writes=[T["tv"]])
                        P.add("dve", lambda e, h=h: e.max_index(out=tp[:, h, 0:8], in_max=tv[:, h, 0:8], in_values=cand[:, h, :]),
                              reads=[T["cand"], T["tv"]], writes=[T["tp"]])
                        yield
                        P.add("dve", lambda e, h=h: e.match_replace(out=cand2[:, h, :], in_to_replace=tv[:, h, 0:8], in_values=cand[:, h, :],
                                                                    imm_value=-1e30), reads=[T["cand"], T["tv"]], writes=[T["cand2"]])
                        P.add("dve", lambda e, h=h: e.max(out=tv[:, h, 8:16], in_=cand2[:, h, :]), reads=[T["cand2"]], writes=[T["tv"]])
                        yield
                        P.add("dve", lambda e, h=h: e.max_index(out=tp[:, h, 8:16], in_max=tv[:, h, 8:16], in_values=cand2[:, h, :]),
                              reads=[T["cand2"], T["tv"]], writes=[T["tp"]])
                        yield
                    P.add("dve", lambda e: e.tensor_tensor(out=dd[:], in0=tv[:], in1=tv[:, :, 0:1].broadcast_to([128, 8, 16]), op=ALU.subtract),
                          reads=[T["tv"]], writes=[T["dd"]])
                    P.add("act", lambda e: e.activation(out=ee[:], in_=dd[:], func=AF.Exp), reads=[T["dd"]], writes=[T["ee"]])
                    yield
                    P.add("dve", lambda e: e.tensor_reduce(out=zz[:], in_=ee[:], axis=AX.X, op=ALU.add), reads=[T["ee"]], writes=[T["zz"]])
                    P.add("dve", lambda e: e.reciprocal(zz[:], zz[:]), reads=[T["zz"]], writes=[T["zz"]])
                    yield
                    P.add("dve", lambda e: e.tensor_tensor(out=GT[:], in0=ee[:], in1=zz[:].unsqueeze(2).broadcast_to([128, 8, 16]), op=ALU.mult),
                          reads=[T["ee"], T["zz"]], writes=[bGT])
                    yield
                    P.add("dve", lambda e: e.tensor_copy(tpf[:], tp[:]), reads=[T["tp"]], writes=[T["tpf"]])
                    P.add("dve", lambda e: e.tensor_copy(sif[:], si[:]), reads=[T["si"]], writes=[T["sif"]])
                    yield
                    P.add("dve", lambda e: e.tensor_tensor(out=cmp, in0=tpf[:].unsqueeze(3).broadcast_to([128, 8, 16, 16]),
                                                           in1=thr16[:, :].unsqueeze(1).unsqueeze(1).broadcast_to([128, 8, 16, 16]),
                                                           op=ALU.is_ge), reads=[T["tpf"], C["thr16"]], writes=[T["cmp"]])
                    yield
                    P.add("dve", lambda e: e.tensor_reduce(out=abf[:, :, 0, :], in_=cmp, axis=AX.X, op=ALU.add), reads=[T["cmp"]],
                          writes=[T["abf"]])
                    yield
                    P.add("dve", lambda e: e.tensor_scalar(out=abf[:, :, 0, :], in0=abf[:, :, 0, :], scalar1=-1.0, scalar2=None, op0=ALU.add),
                          reads=[T["abf"]], writes=[T["abf"]])
                    P.add("dve", lambda e: e.scalar_tensor_tensor(out=abf[:, :, 1, :], in0=abf[:, :, 0, :], scalar=-16.0, in1=tpf[:],
                                                                  op0=ALU.mult, op1=ALU.add), reads=[T["abf"], T["tpf"]], writes=[T["abf"]])
                    yield
                    P.add("dve", lambda e: e.tensor_tensor(out=oh, in0=iota16[:, :].unsqueeze(1).unsqueeze(1).broadcast_to([128, 16, 16, 16]),
                                                           in1=abf[:].rearrange("p h q k -> p (h q) k").unsqueeze(3).broadcast_to([128, 16, 16, 16]),
                                                           op=ALU.is_equal), reads=[C["iota16"], T["abf"]], writes=[T["oh"]])
                    yield
                    P.add("dve", lambda e: e.tensor_tensor(out=oh2, in0=oh,
                                                           in1=sif[:].rearrange("p h q x -> p (h q) x").unsqueeze(2).broadcast_to([128, 16, 16, 16]),
                                                           op=ALU.mult), reads=[T["oh"], T["sif"]], writes=[T["oh2"]])
                    yield
                    P.add("dve", lambda e: e.tensor_reduce(out=sel[:].rearrange("p h q k -> p (h q) k"), in_=oh2, axis=AX.X, op=ALU.add),
                          reads=[T["oh2"]], writes=[T["sel"]])
                    yield
                    P.add("dve", lambda e: e.scalar_tensor_tensor(out=eidf[:], in0=sel[:, :, 0, :], scalar=128.0, in1=sel[:, :, 1, :],
                                                                  op0=ALU.mult, op1=ALU.add), reads=[T["sel"]], writes=[T["eidf"]])
                    P.add("dve", lambda e: e.tensor_copy(IDX[:].rearrange("p (h k) -> p h k", h=8), eidf[:]), reads=[T["eidf"]], writes=[bIDX])
                    yield

                jobs = [(t, s) for t in range(NT2) for s in range(128)]
                nprod = [0]
                ndg = [0]

                prep_gen = {}

                def gather(jn):
                    t, s = jobs[jn]
                    if t in prep_gen:
                        for _ in prep_gen.pop(t):
                            pass
                    k = jn % NB
                    IDX, bIDX = idx[t % 2], b_idx[t % 2]
                    P.add("pool", lambda e: e.indirect_dma_start(out=gb[k][:, :], out_offset=None, in_=uv16,
                                                                 in_offset=bass.IndirectOffsetOnAxis(ap=IDX[:, s:s + 1], axis=0)),
                          reads=[bIDX, b_uv16], writes=[b_gb[k]], dma=True)

                def final_ops(t):
                    H, bH = h2t[t % 2], b_h2t[t % 2]
                    OT, bOT = ot[t % 2], b_ot[t % 2]
                    bankA, bankB = (0, 1) if t % 2 == 0 else (2, 3)
                    P.add("dve", lambda e: e.tensor_tensor(out=yy[:, 0:512], in0=psum[bankA][:, :], in1=H[:, 0:512], op=ALU.add),
                          reads=[pbuf[bankA], bH], writes=[T["yy"]])
                    P.add("dve", lambda e: e.tensor_tensor(out=yy[:, 512:1024], in0=psum[bankB][:, :], in1=H[:, 512:1024], op=ALU.add),
                          reads=[pbuf[bankB], bH], writes=[T["yy"]])
                    P.add("act", lambda e: e.activation(out=junkb[:], in_=yy[:], func=AF.Square, accum_out=st2[:, 4:5]),
                          reads=[T["yy"]], writes=[T["junkb"], T["st2"]])
                    P.add("act", lambda e: e.activation(out=st2[:, 5:6], in_=st2[:, 4:5], func=AF.Sqrt, bias=EPS, scale=1.0 / D),
                          reads=[T["st2"]], writes=[T["st2"]])
                    P.add("dve", lambda e: e.reciprocal(st2[:, 6:7], st2[:, 5:6]), reads=[T["st2"]], writes=[T["st2"]])
                    P.add("dve", lambda e: e.scalar_tensor_tensor(out=OT[:], in0=yy[:], scalar=st2[:, 6:7], in1=gfin[:], op0=ALU.mult,
                                                                  op1=ALU.mult), reads=[T["yy"], T["st2"], C["gfin"]], writes=[bOT])
                    P.add("sp", lambda e: e.dma_start(out=out[t * 128:(t + 1) * 128, :], in_=OT[:]), reads=[bOT], writes=[b_out], dma=True)


                def consume_tile(t):
                    XNB, bXNB = xnb[t % 2], b_xnb[t % 2]
                    GT, bGT = gate[t % 2], b_gate[t % 2]
                    H, bH = h2t[t % 2], b_h2t[t % 2]
                    OT, bOT = ot[t % 2], b_ot[t % 2]
                    bankA, bankB = (0, 1) if t % 2 == 0 else (2, 3)
                    base = t * 128
                    NG = 128 // GS

                    def dots_mul(g):
                        qs = []
                        for s in range(g * GS, (g + 1) * GS):
                            k = (base + s) % NB
                            q = nprod[0] % NPROD
                            nprod[0] += 1
                            qs.append(q)
                            P.add("dve", lambda e, k=k, q=q: e.tensor_tensor(out=prod[q][:], in0=gb[k][:, 0:D], in1=XNB[:], op=ALU.mult),
                                  reads=[b_gb[k], bXNB], writes=[b_prod[q]])
                        return qs

                    def dots_acc(g, qs):
                        for i_, s in enumerate(range(g * GS, (g + 1) * GS)):
                            q = qs[i_]
                            P.add("act", lambda e, s=s, q=q: e.activation(out=junk2[:], in_=prod[q][:], func=AF.Copy,
                                                                          accum_out=act[:, s:s + 1]),
                                  reads=[b_prod[q]], writes=[b_act[g]])

                    qs0 = dots_mul(0)
                    dots_acc(0, qs0)
                    if t == 0:
                        yield
                    for g in range(NG):
                        s0 = g * GS
                        if g == 2 and t > 0:
                            final_ops(t - 1)
                        qs = dots_mul(g + 1) if g + 1 < NG else None
                        P.add("act", lambda e, s0=s0: e.activation(out=ga[:, s0:s0 + GS], in_=act[:, s0:s0 + GS], func=AF.Gelu),
                              reads=[b_act[g]], writes=[b_ga[g]])
                        if qs is not None:
                            dots_acc(g + 1, qs)
                        if t == 0 or g >= 2:
                            yield
                        P.add("dve", lambda e, s0=s0: e.tensor_tensor(out=wgt[:, s0:s0 + GS], in0=ga[:, s0:s0 + GS],
                                                                      in1=GT[:].rearrange("p h k -> p (h k)")[:, s0:s0 + GS], op=ALU.mult),
                              reads=[b_ga[g], bGT], writes=[b_w[g]])
                        for s in range(s0, s0 + GS):
                            k = (base + s) % NB
                            r = ndg[0] % NDG
                            ndg[0] += 1
                            P.add("dve", lambda e, s=s, r=r: e.tensor_tensor(out=dg[r][:], in0=ident_b[:],
                                                                             in1=wgt[:, s:s + 1].broadcast_to([128, 128]), op=ALU.mult),
                                  reads=[b_w[g], C["ident_b"]], writes=[b_dg[r]])
                            P.add("pe", [lambda e, s=s, r=r, k=k: e.matmul(psum[bankA][:, :], dg[r][:], gb[k][:, D:D + 512],
                                                                           start=(s == 0), stop=(s == 127)),
                                         lambda e, s=s, r=r, k=k: e.matmul(psum[bankB][:, :], dg[r][:], gb[k][:, D + 512:2 * D],
                                                                           start=(s == 0), stop=(s == 127))],
                                  reads=[b_dg[r], b_gb[k]], writes=[pbuf[bankA], pbuf[bankB]])
                            jn = base + s
                            if jn + NB < len(jobs):
                                gather(jn + NB)
                        if t == 0 or g >= 2:
                            yield
                for _ in prep(0):
                    pass
                for jn in range(min(NB, len(jobs))):
                    gather(jn)
                for t in range(NT2):
                    side = None
                    if t + 1 < NT2:
                        side = prep(t + 1)
                        prep_gen[t + 1] = side
                    _interleave(consume_tile(t), side, 58, 100)
                final_ops(NT2 - 1)
                P.add("sp", [], reads=[b_out])
                P.emit()

        if 0 in phases:
            phase0()
        if 1 in phases:
            phase1()
        if 2 in phases:
            phase2()
    return nc


def _consts():
    half = 16
    freqs = (10000.0 ** (-np.arange(half, dtype=np.float32) / half)).astype(np.float32)
    pos = np.arange(TP, dtype=np.float32)
    ang = pos[:, None] * freqs[None, :]
    cos = np.cos(ang).astype(np.float32).T
    sin = np.sin(ang).astype(np.float32).T
    cos32 = np.concatenate([cos, cos], axis=0)
    sin32 = np.concatenate([-sin, sin], axis=0)
    rope = np.zeros((NT_FULL, 32, 256), np.float32)
    for i in range(NT_FULL):
        rope[i, :, 0:128] = cos32[:, i * 128:(i + 1) * 128]
        rope[i, :, 128:256] = sin32[:, i * 128:(i + 1) * 128]
    kk = np.arange(128)
    mask = (kk[:, None] <= kk[None, :]).astype(np.float32)
    iota16 = np.tile(np.arange(16, dtype=np.float32)[None, :], (128, 1))
    return rope, mask, iota16, np.eye(128, dtype=np.float32)


def _in_maps(inputs, n_cores=8):
    f = lambda a: np.ascontiguousarray(np.asarray(a, dtype=np.float32))
    x = f(inputs["x"])
    meta = f(inputs["meta"])
    rope, mask, iota16, ident = _consts()
    vecs = np.concatenate([
        f(inputs["g_mix_norm"])[0].reshape(8, 128), f(inputs["g_q"])[0].reshape(2, 128), f(inputs["g_kv"])[0].reshape(1, 128),
        f(inputs["conv_b"])[0].reshape(4, 128), f(inputs["g_conv_ln"])[0].reshape(4, 128), f(inputs["b_conv_ln"])[0].reshape(4, 128),
        f(inputs["g_out_attn"])[0].reshape(4, 128), f(inputs["g_out_conv"])[0].reshape(4, 128)], axis=0)
    shared = {
        "w_in": f(inputs["w_in"])[0], "w_uq": f(inputs["w_uq"])[0], "w_ukv": f(inputs["w_ukv"])[0],
        "conv_w": f(inputs["conv_w"])[0], "w_out": f(inputs["w_out"])[0], "peer_wq": f(inputs["peer_wq"])[0],
        "peer_keys": f(inputs["peer_keys"])[0], "peer_u": f(inputs["peer_u"])[0], "peer_v": f(inputs["peer_v"])[0],
        "vecs": np.ascontiguousarray(vecs), "g_ffn": f(inputs["g_ffn_norm"])[0].reshape(1, D), "g_fin": f(inputs["g_final"]).reshape(1, D),
        "ident": ident, "rope": rope, "mask": mask, "iota16": iota16,
    }
    maps = []
    for b in range(n_cores):
        h0 = np.zeros((TP, D), np.float32)
        h0[:NMETA] = meta
        h0[NMETA:NMETA + SEQ] = x[b]
        m = dict(shared)
        m["h0"] = h0
        maps.append(m)
    return maps


def kernel(**inputs):
    nt = int(os.environ.get("KERNEL_NT", NT_FULL))
    nc = build_program(NT=nt)
    maps = _in_maps(inputs)
    res = run_bass_kernel_spmd(nc, maps, core_ids=list(range(8)))
    return np.stack([np.asarray(r["out"], dtype=np.float32) for r in res.results], axis=0)
```

```python
import os
import math
import numpy as np
import concourse.bass as bass
import concourse.mybir as mybir
from concourse.bass_utils import run_bass_kernel_spmd
from contextlib import ExitStack

F32 = mybir.dt.float32
BF16 = mybir.dt.bfloat16
I32 = mybir.dt.int32
U32 = mybir.dt.uint32
ALU = mybir.AluOpType
AF = mybir.ActivationFunctionType
AX = mybir.AxisListType

D = 1024
SEQ = 4096
NMETA = 16
TP = 4224
NT_FULL = 33
EPS = 1e-6
NEXP = 16384


class Buf:
    __slots__ = ("name", "last_w", "readers", "excl")

    def __init__(self, name, excl=False):
        self.name = name
        self.last_w = None
        self.readers = []
        self.excl = excl


class Op:
    __slots__ = ("eng", "fns", "deps", "signal", "ord", "is_dma", "dsem", "dval", "dprev", "batch")

    def __init__(self, eng, fns, is_dma):
        self.batch = 0
        self.eng = eng
        self.fns = fns
        self.deps = []
        self.signal = False
        self.ord = 0
        self.is_dma = is_dma
        self.dsem = None
        self.dval = 0
        self.dprev = 0


class Prog:
    ENGS = ("pe", "act", "dve", "pool", "sp")

    def __init__(self, nc, stack, dma_sems=None):
        self.nc = nc
        self.ops = []
        self.eng_ops = {e: [] for e in self.ENGS}
        self.sems = {e: stack.enter_context(nc.semaphore("sem_" + e)) for e in self.ENGS}
        self.batch = 0
        self.ordc = {e: 0 for e in self.ENGS}
        self.waited = {e: {} for e in self.ENGS}
        dma_sems = dma_sems or {"sp": 12, "act": 4, "pool": 40}
        self.dsems, self.dcount, self.dnext = {}, {}, {}
        for e, n in dma_sems.items():
            self.dsems[e] = [stack.enter_context(nc.semaphore("dsem_%s%d" % (e, i))) for i in range(n)]
            self.dcount[e] = [0] * n
            self.dnext[e] = 0

    def add(self, eng, fns, reads=(), writes=(), dma=False):
        if callable(fns):
            fns = [fns]
        op = Op(eng, list(fns), dma)
        deps, seen = [], set()

        def dep(d):
            if d is not None and id(d) not in seen:
                seen.add(id(d))
                deps.append(d)

        wr = list(writes) + [b for b in reads if b.excl]
        for b in reads:
            if not b.excl:
                dep(b.last_w)
        for b in wr:
            dep(b.last_w)
            for r in b.readers:
                dep(r)
        for b in reads:
            if not b.excl:
                b.readers.append(op)
        for b in wr:
            b.last_w = op
            b.readers = []
        op.deps = deps
        op.batch = self.batch
        if dma:
            k = self.dnext[eng]
            self.dnext[eng] = (k + 1) % len(self.dsems[eng])
            op.dsem = self.dsems[eng][k]
            op.dprev = self.dcount[eng][k]
            self.dcount[eng][k] += 16
            op.dval = self.dcount[eng][k]
        self.ops.append(op)
        self.eng_ops[eng].append(op)
        return op

    def emit(self):
        nc = self.nc
        cur = self.batch
        for op in self.ops:
            for d in op.deps:
                if not d.is_dma and d.batch == cur:
                    d.signal = True
        for e in self.ENGS:
            n = self.ordc[e]
            for op in self.eng_ops[e]:
                if op.signal and not op.is_dma:
                    n += 1
                    op.ord = n
            self.ordc[e] = n
        prog = self

        def run(ename, engobj):
            waited = prog.waited[ename]

            def wait(sem, val):
                if val <= 0 or waited.get(sem.num, 0) >= val:
                    return
                waited[sem.num] = val
                engobj.wait_ge(sem, val)

            for op in prog.eng_ops[ename]:
                for d in op.deps:
                    if d.is_dma:
                        wait(d.dsem, d.dval)
                    else:
                        if d.batch != cur:
                            continue
                        if d.eng == ename and ename == "pe":
                            continue
                        wait(prog.sems[d.eng], d.ord)
                if op.is_dma:
                    wait(op.dsem, op.dprev)
                last = None
                for fn in op.fns:
                    last = fn(engobj)
                if last is not None:
                    if op.is_dma:
                        last.then_inc(op.dsem, 16)
                    elif op.signal:
                        last.then_inc(prog.sems[ename], 1)
                else:
                    assert not op.signal

        with nc.Block() as block:
            @block.tensor
            def _(eng):
                run("pe", eng)

            @block.scalar
            def _(eng):
                run("act", eng)

            @block.vector
            def _(eng):
                run("dve", eng)

            @block.gpsimd
            def _(eng):
                run("pool", eng)

            @block.sync
            def _(eng):
                run("sp", eng)
        self.batch += 1
        self.ops = []
        self.eng_ops = {e: [] for e in self.ENGS}


def _interleave(main, side, n_main_hint, n_side_hint):
    if side is None:
        for _ in main:
            pass
        return
    ratio = max(1e-9, n_side_hint / max(1, n_main_hint))
    credit = 0.0
    side_done = False
    for _ in main:
        credit += ratio
        while credit >= 1.0 and not side_done:
            credit -= 1.0
            try:
                next(side)
            except StopIteration:
                side_done = True
    if not side_done:
        for _ in side:
            pass


def build_program(NT=NT_FULL, NB=20, phases=(0, 1, 2), dbg_h2=False):
    nc = bass.Bass("TRN2", target_bir_lowering=False)

    def din(name, shape, dt=F32):
        return nc.dram_tensor(name, list(shape), dt, kind="ExternalInput").ap()

    h0 = din("h0", [TP, D])
    w_in = din("w_in", [D, 1440])
    w_uq = din("w_uq", [256, 768])
    w_ukv = din("w_ukv", [128, 1024])
    conv_w = din("conv_w", [31, 512])
    w_out = din("w_out", [D, D])
    peer_wq = din("peer_wq", [D, D])
    peer_keys = din("peer_keys", [8, 2, 128, 64])
    peer_u = din("peer_u", [NEXP, D])
    peer_v = din("peer_v", [NEXP, D])
    vecs = din("vecs", [31, 128])
    g_ffn = din("g_ffn", [1, D])
    g_fin = din("g_fin", [1, D])
    ident_d = din("ident", [128, 128])
    rope_d = din("rope", [NT_FULL, 32, 256])
    mask_d = din("mask", [128, 128])
    iota_d = din("iota16", [128, 16])
    n_out_rows = min(SEQ, NT * 128 - NMETA)
    out = nc.dram_tensor("out", [SEQ, D], F32, kind="ExternalOutput").ap()
    h2s = nc.dram_tensor("h2s", [TP, D], F32, kind="ExternalOutput" if dbg_h2 else "Internal").ap()

    with ExitStack() as st:
        P = Prog(nc, st)
        psum = [st.enter_context(nc.psum_tensor("ps%d" % i, [128, 512], F32)) for i in range(8)]
        pbuf = [Buf("ps%d" % i, excl=True) for i in range(8)]
        b_out = Buf("out")
        uv16 = nc.dram_tensor("uv16", [NEXP, 2 * D], BF16, kind="Internal").ap()
        b_uv16 = Buf("uv16")
        b_h2s = [Buf("h2s%d" % i) for i in range(NT_FULL)]

        def cp_any(eng, o, i):
            if eng == "act":
                return lambda e: e.copy(o, i)
            return lambda e: e.tensor_copy(o, i)

        def phase1():
            with ExitStack() as s1:
                def sb(name, shape, dt):
                    return s1.enter_context(nc.sbuf_tensor("a_" + name, list(shape), dt))

                gen_cycle = [0, 1, 4, 5, 6, 7]
                gen_pos = [0]

                def gbank():
                    b = gen_cycle[gen_pos[0] % len(gen_cycle)]
                    gen_pos[0] += 1
                    return b

                ident_f = sb("ident_f", [128, 128], F32)
                ident_b = sb("ident_b", [128, 128], BF16)
                onesb = sb("onesb", [128, 128], BF16)
                maskf = sb("maskf", [128, 128], F32)
                maskb = sb("maskb", [128, 128], BF16)
                vst = sb("vst", [31, 128], F32)
                cols = sb("cols", [128, 31], F32)
                cw = sb("cw", [128, 4, 31], F32)
                Wb = sb("Wb", [128, 8, 1440], BF16)
                wkrot = sb("wkrot", [128, 8, 96], BF16)
                Wq = sb("Wq", [128, 2, 8, 96], BF16)
                Wqrot = sb("Wqrot", [128, 2, 8, 96], BF16)
                Wk = sb("Wk", [128, 8, 64], BF16)
                Wv = sb("Wv", [128, 8, 64], BF16)
                Wo = sb("Wo", [128, 8, 1024], BF16)
                KT = sb("KT", [128, 8, NT * 128], BF16)
                VP = sb("VP", [128, NT, 8, 65], BF16)
                Gs = [sb("G%d" % i, [128, 4, 158], F32) for i in range(2)]
                b_G = [Buf("G0"), Buf("G1")]
                cwst = Gs[1][0:31, :, :].rearrange("p c t -> p (c t)")[:, 0:512]
                xt = [sb("xt%d" % i, [128, D], F32) for i in range(2)]
                h2 = [sb("h2_0", [128, D], F32)] * 2
                b_xt = [Buf("xt0"), Buf("xt1")]
                b_h2 = [Buf("h2_0")] * 2
                stage = xt + h2[:1]
                b_stage = b_xt + b_h2[:1]
                B = {n: Buf(n) for n in ("ident_f ident_b onesb maskf maskb vst cols cwst cw Wb wkrot Wq Wqrot Wk Wv Wo "
                                         "KT VP G").split()}
                gm, gq, gkv = cols[:, 0:8], cols[:, 8:10], cols[:, 10:11]
                cb, gln, bln = cols[:, 11:15], cols[:, 15:19], cols[:, 19:23]
                goa, goc = cols[:, 23:27], cols[:, 27:31]

                P.add("sp", lambda e: e.dma_start(out=ident_f[:], in_=ident_d), writes=[B["ident_f"]], dma=True)
                P.add("sp", lambda e: e.dma_start(out=maskf[:], in_=mask_d), writes=[B["maskf"]], dma=True)
                P.add("sp", lambda e: e.dma_start(out=vst[:], in_=vecs), writes=[B["vst"]], dma=True)
                P.add("sp", lambda e: e.dma_start(out=cwst, in_=conv_w), writes=[b_G[1]], dma=True)
                P.add("dve", lambda e: e.tensor_copy(ident_b[:], ident_f[:]), reads=[B["ident_f"]], writes=[B["ident_b"]])
                P.add("dve", lambda e: e.tensor_scalar(out=maskb[:], in0=maskf[:], scalar1=-1.0, scalar2=30000.0, op0=ALU.add, op1=ALU.mult),
                      reads=[B["maskf"]], writes=[B["maskb"]])
                P.add("dve", lambda e: e.memset(onesb[:], 1.0), writes=[B["onesb"]])
                P.add("dve", lambda e: e.memset(Gs[0][:], 0.0), writes=[b_G[0]])
                P.add("pool", lambda e: e.memset(VP[:], 1.0), writes=[B["VP"]])
                P.add("pool", lambda e: e.memset(KT[:], 0.0), writes=[B["KT"]])
                bk = gbank()
                P.add("pe", lambda e, bk=bk: e.matmul(psum[bk][:, 0:31], vst[0:31, :], ident_f[0:31, 0:31], start=True, stop=True),
                      reads=[B["vst"], B["ident_f"]], writes=[pbuf[bk]])
                P.add("dve", lambda e, bk=bk: e.tensor_copy(cols[:], psum[bk][:, 0:31]), reads=[pbuf[bk]], writes=[B["cols"]])
                bk = gbank()
                fns = []
                for c in range(4):
                    fns.append(lambda e, bk=bk, c=c: e.matmul(psum[bk][:, c * 31:(c + 1) * 31], cwst[:, c * 128:(c + 1) * 128],
                                                              ident_f[0:31, 0:31], start=(c == 0), stop=(c == 3)))
                P.add("pe", fns, reads=[b_G[1], B["ident_f"]], writes=[pbuf[bk]])
                P.add("dve", lambda e, bk=bk: e.tensor_copy(cw[:].rearrange("p c k -> p (c k)"), psum[bk][:, 0:124]),
                      reads=[pbuf[bk]], writes=[B["cw"]])
                sidx = [0]

                def stage_load(src_ap, ncol):
                    k = sidx[0] % 3
                    sidx[0] += 1
                    P.add("sp", lambda e, k=k: e.dma_start(out=stage[k][:, 0:ncol], in_=src_ap), writes=[b_stage[k]], dma=True)
                    return k

                for c in range(8):
                    for (c0, c1) in ((0, 1024), (1024, 1440)):
                        k = stage_load(w_in[c * 128:(c + 1) * 128, c0:c1], c1 - c0)
                        if c % 2 == 0:
                            P.add("dve", lambda e, k=k, c=c, c0=c0, c1=c1: e.tensor_scalar(out=Wb[:, c, c0:c1], in0=stage[k][:, 0:c1 - c0],
                                                                                           scalar1=gm[:, c:c + 1], scalar2=None, op0=ALU.mult),
                                  reads=[b_stage[k], B["cols"]], writes=[B["Wb"]])
                        else:
                            P.add("act", lambda e, k=k, c=c, c0=c0, c1=c1: e.activation(out=Wb[:, c, c0:c1], in_=stage[k][:, 0:c1 - c0],
                                                                                        func=AF.Copy, scale=gm[:, c:c + 1]),
                                  reads=[b_stage[k], B["cols"]], writes=[B["Wb"]])
                P.add("dve", lambda e: e.tensor_copy(wkrot[:, :, 0:64], Wb[:, :, 1344:1408]), reads=[B["Wb"]], writes=[B["wkrot"]])
                P.add("dve", lambda e: e.tensor_copy(wkrot[:, :, 64:80], Wb[:, :, 1424:1440]), reads=[B["Wb"]], writes=[B["wkrot"]])
                P.add("dve", lambda e: e.tensor_copy(wkrot[:, :, 80:96], Wb[:, :, 1408:1424]), reads=[B["Wb"]], writes=[B["wkrot"]])
                for c in range(2):
                    k = stage_load(w_uq[c * 128:(c + 1) * 128, :], 768)
                    P.add("dve", lambda e, k=k, c=c: e.tensor_scalar(out=Wq[:, c, :, :].rearrange("p h d -> p (h d)"),
                                                                     in0=stage[k][:, 0:768], scalar1=gq[:, c:c + 1],
                                                                     scalar2=None, op0=ALU.mult),
                          reads=[b_stage[k], B["cols"]], writes=[B["Wq"]])
                P.add("dve", lambda e: e.tensor_copy(Wqrot[:, :, :, 0:64], Wq[:, :, :, 0:64]), reads=[B["Wq"]], writes=[B["Wqrot"]])
                P.add("dve", lambda e: e.tensor_copy(Wqrot[:, :, :, 64:80], Wq[:, :, :, 80:96]), reads=[B["Wq"]], writes=[B["Wqrot"]])
                P.add("dve", lambda e: e.tensor_copy(Wqrot[:, :, :, 80:96], Wq[:, :, :, 64:80]), reads=[B["Wq"]], writes=[B["Wqrot"]])
                k = stage_load(w_ukv, 1024)
                stv = stage[k][:, 0:1024].rearrange("p (h x) -> p h x", h=8)
                P.add("dve", lambda e, stv=stv: e.tensor_scalar(out=Wk[:], in0=stv[:, :, 0:64], scalar1=gkv[:, 0:1], scalar2=None,
                                                                op0=ALU.mult), reads=[b_stage[k], B["cols"]], writes=[B["Wk"]])
                P.add("dve", lambda e, stv=stv: e.tensor_scalar(out=Wv[:], in0=stv[:, :, 64:128], scalar1=gkv[:, 0:1], scalar2=None,
                                                                op0=ALU.mult), reads=[b_stage[k], B["cols"]], writes=[B["Wv"]])
                for c in range(8):
                    k = stage_load(w_out[c * 128:(c + 1) * 128, :], 1024)
                    sc = goa[:, c:c + 1] if c < 4 else goc[:, c - 4:c - 3]
                    if c % 2 == 0:
                        P.add("dve", lambda e, k=k, c=c, sc=sc: e.tensor_scalar(out=Wo[:, c, :], in0=stage[k][:, 0:1024], scalar1=sc,
                                                                                scalar2=None, op0=ALU.mult),
                              reads=[b_stage[k], B["cols"]], writes=[B["Wo"]])
                    else:
                        P.add("act", lambda e, k=k, c=c, sc=sc: e.activation(out=Wo[:, c, :], in_=stage[k][:, 0:1024], func=AF.Copy, scale=sc),
                              reads=[b_stage[k], B["cols"]], writes=[B["Wo"]])

                rp = [sb("rp%d" % i, [128, 2, 128], F32) for i in range(2)]
                st1 = sb("st1", [128, 8], F32)
                hs = sb("hs", [128, D], BF16)
                junkb = hs
                hT = sb("hT", [128, 8, 128], BF16)
                sig = sb("sig", [128, 512], F32)
                cv = sb("cv", [128, 4, 128], F32)
                cbf = sb("cbf", [128, 512], BF16)
                c2b = sb("c2b", [128, 512], BF16)
                mean = sb("mean", [128, 128], F32)
                m2 = sb("m2", [128, 128], F32)
                var = sb("var", [128, 128], F32)
                rs = sb("rs", [128, 128], F32)
                tt = sb("tt", [128, 4, 128], F32)
                t2 = sb("t2", [128, 4, 128], F32)
                sl = sb("sl", [128, 4, 128], F32)
                s2b = c2b
                rs2 = sb("rs2", [128, 128], F32)
                MTs = [sb("mixT%d" % i, [128, 8, 128], BF16) for i in range(2)]
                b_mTa = [Buf("mTa0"), Buf("mTa1")]
                b_mTc = [Buf("mTc0"), Buf("mTc1")]
                cq = sb("cq", [128, 2, 128], F32)
                cq2 = sb("cq2", [128, 256], BF16)
                rsq = sb("rsq", [128, 128], F32)
                cqn = sb("cqn", [128, 2, 128], BF16)
                ckv = sb("ckv", [128, 128], F32)
                ckv2 = sb("ckv2", [128, 128], BF16)
                rskv = sb("rskv", [128, 128], F32)
                ckvn = sb("ckvn", [128, 128], BF16)
                kr1 = sb("kr1", [128, 128], F32)
                kr2 = sb("kr2", [128, 128], F32)
                qT = sb("qT", [128, 8, 128], BF16)
                qr1 = tt
                qr2 = t2
                PT = [sb("PT%d" % i, [128, 4, 128], BF16) for i in range(4)]
                rec = sb("rec", [128, 8, 1], F32)
                ao = sig[:].rearrange("p (h x) -> p h x", h=8)
                aob = sb("aob", [128, 512], BF16)
                W = {n: Buf(n) for n in ("junkb st1 hs hT sig cbf c2b mean m2 var rs tt t2 sl s2b rs2 mixTa mixTc cq cq2 "
                                         "rsq cqn ckv ckv2 rskv ckvn kr1 kr2 qT qr1 qr2 rec ao aob ssa rsa").split()}
                b_rp = [Buf("rp0"), Buf("rp1")]
                b_cv = [Buf("cv%d" % c) for c in range(4)]
                b_PT = [Buf("PT%d" % i) for i in range(4)]
                W["junkb"] = W["hs"]
                W["s2b"] = W["c2b"]
                W["qr1"] = W["tt"]
                W["qr2"] = W["t2"]
                W["ao"] = W["sig"]
                ssa = sb("ssa", [128, 4], F32)
                SCALE = 1.0 / math.sqrt(96.0)

                Obank = (2, 3)

                def segA1(i):
                    X, bX = xt[i % 2], b_xt[i % 2]
                    R, bR = rp[i % 2], b_rp[i % 2]
                    tc = slice(i * 128, (i + 1) * 128)
                    G, bG = Gs[i % 2], b_G[i % 2]
                    P.add("sp", lambda e, X=X, i=i: e.dma_start(out=X[:], in_=h0[i * 128:(i + 1) * 128, :]), writes=[bX], dma=True)
                    P.add("sp", lambda e, R=R, i=i: e.dma_start(out=R[64:96, :, :].rearrange("p a t -> p (a t)"), in_=rope_d[i]),
                          writes=[bR], dma=True)
                    P.add("act", lambda e, X=X: e.activation(out=junkb[:], in_=X[:], func=AF.Square, accum_out=st1[:, 0:1]),
                          reads=[bX], writes=[W["junkb"], W["st1"]])
                    P.add("act", lambda e: e.activation(out=st1[:, 1:2], in_=st1[:, 0:1], func=AF.Sqrt, bias=EPS, scale=1.0 / D),
                          reads=[W["st1"]], writes=[W["st1"]])
                    yield
                    P.add("dve", lambda e: e.reciprocal(st1[:, 2:3], st1[:, 1:2]), reads=[W["st1"]], writes=[W["st1"]])
                    P.add("act", lambda e, X=X: e.activation(out=hs[:], in_=X[:], func=AF.Copy, scale=st1[:, 2:3]),
                          reads=[bX, W["st1"]], writes=[W["hs"]])
                    bk = gbank()
                    psT = psum[bk][:].bitcast(BF16)
                    P.add("pe", [lambda e, c=c, psT=psT: e.transpose(psT[:, c * 128:(c + 1) * 128], hs[:, c * 128:(c + 1) * 128], ident_b[:])
                                 for c in range(8)], reads=[W["hs"], B["ident_b"]], writes=[pbuf[bk]])
                    yield
                    P.add("dve", lambda e, psT=psT: e.tensor_copy(hT[:].rearrange("p c t -> p (c t)"), psT[:, 0:1024]),
                          reads=[pbuf[bk]], writes=[W["hT"]])
                    bA, bB, bC, bD = gbank(), gbank(), gbank(), gbank()
                    for (bk, col0) in ((bA, 0), (bB, 512)):
                        fns = []
                        for cc in range(4):
                            for c in range(8):
                                fns.append(lambda e, bk=bk, cc=cc, c=c, col0=col0: e.matmul(
                                    psum[bk][:, cc * 128:(cc + 1) * 128], Wb[:, c, col0 + cc * 128:col0 + (cc + 1) * 128], hT[:, c, :],
                                    start=(cc == 0 and c == 0), stop=(cc == 3 and c == 7)))
                        P.add("pe", fns, reads=[B["Wb"], W["hT"]], writes=[pbuf[bk]])
                    fns = []
                    for cc in range(3):
                        for c in range(8):
                            fns.append(lambda e, cc=cc, c=c: e.matmul(
                                psum[bC][:, cc * 128:(cc + 1) * 128], Wb[:, c, 1024 + cc * 128:1024 + (cc + 1) * 128], hT[:, c, :],
                                start=(cc == 0 and c == 0), stop=False))
                    for c in range(8):
                        fns.append(lambda e, c=c: e.matmul(psum[bC][0:96, 384:512], Wb[:, c, 1344:1440], hT[:, c, :],
                                                           start=False, stop=(c == 7)))
                    P.add("pe", fns, reads=[B["Wb"], W["hT"]], writes=[pbuf[bC]])
                    P.add("pe", [lambda e, c=c: e.matmul(psum[bD][0:96, 0:128], wkrot[:, c, :], hT[:, c, :], start=(c == 0), stop=(c == 7))
                                 for c in range(8)], reads=[B["wkrot"], W["hT"]], writes=[pbuf[bD]])
                    P.add("act", lambda e: e.activation(out=sig[:], in_=psum[bB][:, :], func=AF.Sigmoid), reads=[pbuf[bB]], writes=[W["sig"]])
                    yield
                    P.add("dve", lambda e: e.tensor_tensor(out=G[:, :, 30:158], in0=psum[bA][:, :].rearrange("p (c t) -> p c t", c=4),
                                                           in1=sig[:].rearrange("p (c t) -> p c t", c=4), op=ALU.mult),
                          reads=[pbuf[bA], W["sig"]], writes=[bG])
                    P.add("act", lambda e: e.copy(cq[:].rearrange("p c t -> p (c t)"), psum[bC][:, 0:256]), reads=[pbuf[bC]], writes=[W["cq"]])
                    P.add("act", lambda e: e.activation(out=cq2[:], in_=psum[bC][:, 0:256], func=AF.Square), reads=[pbuf[bC]], writes=[W["cq2"]])
                    P.add("act", lambda e: e.copy(ckv[:], psum[bC][:, 256:384]), reads=[pbuf[bC]], writes=[W["ckv"]])
                    P.add("act", lambda e: e.activation(out=ckv2[:], in_=psum[bC][:, 256:384], func=AF.Square), reads=[pbuf[bC]], writes=[W["ckv2"]])
                    yield
                    P.add("dve", lambda e, R=R: e.tensor_tensor(out=kr1[64:96, :], in0=psum[bC][64:96, 384:512], in1=R[64:96, 0, :], op=ALU.mult),
                          reads=[pbuf[bC], bR], writes=[W["kr1"]])
                    P.add("dve", lambda e, R=R: e.tensor_tensor(out=kr2[64:96, :], in0=psum[bD][64:96, 0:128], in1=R[64:96, 1, :], op=ALU.mult),
                          reads=[pbuf[bD], bR], writes=[W["kr2"]])
                    P.add("dve", lambda e, tc=tc: e.tensor_tensor(out=KT[64:96, 0, tc], in0=kr1[64:96, :], in1=kr2[64:96, :], op=ALU.add),
                          reads=[W["kr1"], W["kr2"]], writes=[B["KT"]])
                    P.add("dve", lambda e, tc=tc: e.tensor_copy(KT[64:96, 1:8, tc], KT[64:96, 0:1, tc].broadcast_to([32, 7, 128])),
                          reads=[B["KT"]], writes=[B["KT"]])
                    bE = gbank()
                    P.add("pe", [lambda e, c=c: e.matmul(psum[bE][:, 0:128], onesb[:], cq2[:, c * 128:(c + 1) * 128], start=(c == 0), stop=False)
                                 for c in range(2)] +
                          [lambda e: e.matmul(psum[bE][:, 128:256], onesb[:], ckv2[:], start=False, stop=True)],
                          reads=[B["onesb"], W["cq2"], W["ckv2"]], writes=[pbuf[bE]])
                    P.add("act", lambda e: e.activation(out=rsq[:], in_=psum[bE][:, 0:128], func=AF.Sqrt, bias=EPS, scale=1.0 / 256),
                          reads=[pbuf[bE]], writes=[W["rsq"]])
                    P.add("act", lambda e: e.activation(out=rskv[:], in_=psum[bE][:, 128:256], func=AF.Sqrt, bias=EPS, scale=1.0 / 128),
                          reads=[pbuf[bE]], writes=[W["rskv"]])
                    yield
                    P.add("dve", lambda e: e.reciprocal(rsq[:], rsq[:]), reads=[W["rsq"]], writes=[W["rsq"]])
                    P.add("dve", lambda e: e.reciprocal(rskv[:], rskv[:]), reads=[W["rskv"]], writes=[W["rskv"]])
                    P.add("dve", lambda e: e.tensor_tensor(out=cqn[:], in0=cq[:], in1=rsq[:].unsqueeze(1).broadcast_to([128, 2, 128]), op=ALU.mult),
                          reads=[W["cq"], W["rsq"]], writes=[W["cqn"]])
                    P.add("dve", lambda e: e.tensor_tensor(out=ckvn[:], in0=ckv[:], in1=rskv[:], op=ALU.mult),
                          reads=[W["ckv"], W["rskv"]], writes=[W["ckvn"]])
                    bq = [gbank(), gbank()]
                    for half in range(2):
                        fns = []
                        for hh in range(4):
                            h = half * 4 + hh
                            for c in range(2):
                                fns.append(lambda e, half=half, hh=hh, h=h, c=c: e.matmul(
                                    psum[bq[half]][0:96, hh * 128:(hh + 1) * 128], Wq[:, c, h, :], cqn[:, c, :],
                                    start=(hh == 0 and c == 0), stop=(hh == 3 and c == 1)))
                        P.add("pe", fns, reads=[B["Wq"], W["cqn"]], writes=[pbuf[bq[half]]])
                    bkk = [gbank(), gbank()]
                    for half in range(2):
                        P.add("pe", [lambda e, half=half, hh=hh: e.matmul(psum[bkk[half]][0:64, hh * 128:(hh + 1) * 128], Wk[:, half * 4 + hh, :],
                                                                          ckvn[:], start=(hh == 0), stop=(hh == 3)) for hh in range(4)],
                              reads=[B["Wk"], W["ckvn"]], writes=[pbuf[bkk[half]]])
                    for half in range(2):
                        P.add("act", lambda e, half=half: e.copy(qT[0:64, half * 4:(half + 1) * 4, :],
                                                                 psum[bq[half]][0:64, :].rearrange("p (h t) -> p h t", h=4)),
                              reads=[pbuf[bq[half]]], writes=[W["qT"]])
                        yield
                        P.add("dve", lambda e, half=half, R=R: e.tensor_tensor(
                            out=qr1[64:96, :, :], in0=psum[bq[half]][64:96, :].rearrange("p (h t) -> p h t", h=4),
                            in1=R[64:96, 0:1, :].broadcast_to([32, 4, 128]), op=ALU.mult),
                            reads=[pbuf[bq[half]], bR], writes=[W["qr1"]])
                        brot = gbank()
                        fns = []
                        for hh in range(4):
                            h = half * 4 + hh
                            for c in range(2):
                                fns.append(lambda e, brot=brot, hh=hh, h=h, c=c: e.matmul(
                                    psum[brot][0:96, hh * 128:(hh + 1) * 128], Wqrot[:, c, h, :], cqn[:, c, :],
                                    start=(hh == 0 and c == 0), stop=(hh == 3 and c == 1)))
                        P.add("pe", fns, reads=[B["Wqrot"], W["cqn"]], writes=[pbuf[brot]])
                        yield
                        P.add("dve", lambda e, brot=brot, R=R: e.tensor_tensor(
                            out=qr2[64:96, :, :], in0=psum[brot][64:96, :].rearrange("p (h t) -> p h t", h=4),
                            in1=R[64:96, 1:2, :].broadcast_to([32, 4, 128]), op=ALU.mult),
                            reads=[pbuf[brot], bR], writes=[W["qr2"]])
                        P.add("dve", lambda e, half=half: e.tensor_tensor(out=qT[64:96, half * 4:(half + 1) * 4, :], in0=qr1[64:96, :, :],
                                                                          in1=qr2[64:96, :, :], op=ALU.add),
                              reads=[W["qr1"], W["qr2"]], writes=[W["qT"]])
                        P.add("act", lambda e, half=half, tc=tc: e.copy(KT[0:64, half * 4:(half + 1) * 4, tc],
                                                                        psum[bkk[half]][0:64, :].rearrange("p (h t) -> p h t", h=4)),
                              reads=[pbuf[bkk[half]]], writes=[B["KT"]])
                    bv = gbank()
                    P.add("pe", lambda e: e.matmul(psum[bv][:, :], ckvn[:], Wv[:].rearrange("p h x -> p (h x)"), start=True, stop=True),
                          reads=[W["ckvn"], B["Wv"]], writes=[pbuf[bv]])
                    P.add("act", lambda e, i=i: e.copy(VP[:, i, :, 0:64], psum[bv][:, :].rearrange("p (h x) -> p h x", h=8)),
                          reads=[pbuf[bv]], writes=[B["VP"]])
                    yield

                def segA1att(i):
                    pend = None
                    nPT = [0]

                    def issue_pv(j, pts):
                        for half in range(2):
                            k_pt = pts[half]
                            P.add("pe", [lambda e, half=half, hh=hh, k_pt=k_pt, j=j: e.matmul(
                                psum[Obank[half]][:, hh * 65:(hh + 1) * 65], PT[k_pt][:, hh, :], VP[:, j, half * 4 + hh, :],
                                start=(j == 0 and hh == 0), stop=(j == i and hh == 3)) for hh in range(4)],
                                reads=[b_PT[k_pt], B["VP"]], writes=[pbuf[Obank[half]]])

                    for j in range(i + 1):
                        pts = []
                        for half in range(2):
                            sbk = (4 + half) if (j % 2 == 0) else (6 + half)
                            fns = [lambda e, half=half, hh=hh, sbk=sbk, j=j: e.matmul(
                                psum[sbk][:, hh * 128:(hh + 1) * 128], KT[0:96, half * 4 + hh, j * 128:(j + 1) * 128],
                                qT[0:96, half * 4 + hh, :], start=(hh == 0), stop=(hh == 3 and j != i)) for hh in range(4)]
                            if j == i:
                                fns += [lambda e, hh=hh, sbk=sbk: e.matmul(psum[sbk][:, hh * 128:(hh + 1) * 128], ident_b[:], maskb[:],
                                                                           start=False, stop=(hh == 3)) for hh in range(4)]
                            P.add("pe", fns, reads=[B["KT"], W["qT"], B["ident_b"], B["maskb"]], writes=[pbuf[sbk]])
                            k_pt = nPT[0] % 4
                            nPT[0] += 1
                            P.add("act", lambda e, sbk=sbk, k_pt=k_pt: e.activation(out=PT[k_pt][:].rearrange("p h t -> p (h t)"),
                                                                                    in_=psum[sbk][:, :], func=AF.Exp, scale=SCALE),
                                  reads=[pbuf[sbk]], writes=[b_PT[k_pt]])
                            pts.append(k_pt)
                        if pend is not None:
                            issue_pv(*pend)
                        pend = (j, pts)
                    issue_pv(*pend)
                def segB1conv(i):
                    G, bG = Gs[i % 2], b_G[i % 2]
                    Gn, bGn = Gs[(i + 1) % 2], b_G[(i + 1) % 2]
                    for k in range(31):
                        yield
                        for c in range(4):
                            if k == 0:
                                P.add("dve", lambda e, c=c: e.tensor_scalar(out=cv[:, c, :], in0=G[:, c, 0:128], scalar1=cw[:, c, 0:1],
                                                                            scalar2=cb[:, c:c + 1], op0=ALU.mult, op1=ALU.add),
                                      reads=[bG, B["cw"], B["cols"]], writes=[b_cv[c]])
                            else:
                                P.add("dve", lambda e, c=c, k=k: e.scalar_tensor_tensor(out=cv[:, c, :], in0=G[:, c, k:k + 128],
                                                                                        scalar=cw[:, c, k:k + 1], in1=cv[:, c, :],
                                                                                        op0=ALU.mult, op1=ALU.add),
                                      reads=[bG, B["cw"], b_cv[c]], writes=[b_cv[c]])
                    P.add("dve", lambda e: e.tensor_copy(Gn[:, :, 0:30], G[:, :, 128:158]), reads=[bG], writes=[bGn])
                    yield

                def segB1(i):
                    mixT = MTs[i % 2]
                    cvf = cv[:].rearrange("p c t -> p (c t)")
                    P.add("act", lambda e: e.copy(cbf[:], cvf), reads=b_cv, writes=[W["cbf"]])
                    P.add("act", lambda e: e.activation(out=c2b[:], in_=cvf, func=AF.Square), reads=b_cv, writes=[W["c2b"]])
                    bS = gbank()
                    P.add("pe", [lambda e, c=c: e.matmul(psum[bS][:, 0:128], onesb[:], cbf[:, c * 128:(c + 1) * 128], start=(c == 0), stop=False)
                                 for c in range(4)] +
                          [lambda e, c=c: e.matmul(psum[bS][:, 128:256], onesb[:], c2b[:, c * 128:(c + 1) * 128], start=False, stop=(c == 3))
                           for c in range(4)], reads=[B["onesb"], W["cbf"], W["c2b"]], writes=[pbuf[bS]])
                    P.add("dve", lambda e: e.tensor_scalar(out=mean[:], in0=psum[bS][:, 0:128], scalar1=1.0 / 512, scalar2=None, op0=ALU.mult),
                          reads=[pbuf[bS]], writes=[W["mean"]])
                    P.add("dve", lambda e: e.tensor_tensor(out=m2[:], in0=mean[:], in1=mean[:], op=ALU.mult), reads=[W["mean"]], writes=[W["m2"]])
                    P.add("dve", lambda e: e.scalar_tensor_tensor(out=var[:], in0=psum[bS][:, 128:256], scalar=1.0 / 512, in1=m2[:],
                                                                  op0=ALU.mult, op1=ALU.subtract),
                          reads=[pbuf[bS], W["m2"]], writes=[W["var"]])
                    P.add("act", lambda e: e.activation(out=rs[:], in_=var[:], func=AF.Sqrt, bias=EPS, scale=1.0), reads=[W["var"]], writes=[W["rs"]])
                    P.add("dve", lambda e: e.reciprocal(rs[:], rs[:]), reads=[W["rs"]], writes=[W["rs"]])
                    P.add("dve", lambda e: e.tensor_tensor(out=tt[:], in0=cv[:], in1=mean[:].unsqueeze(1).broadcast_to([128, 4, 128]), op=ALU.subtract),
                          reads=b_cv + [W["mean"]], writes=[W["tt"]])
                    P.add("dve", lambda e: e.tensor_tensor(out=t2[:], in0=tt[:], in1=rs[:].unsqueeze(1).broadcast_to([128, 4, 128]), op=ALU.mult),
                          reads=[W["tt"], W["rs"]], writes=[W["t2"]])
                    for c in range(4):
                        P.add("act", lambda e, c=c: e.activation(out=sl[:, c, :], in_=t2[:, c, :], func=AF.Silu, bias=bln[:, c:c + 1],
                                                                 scale=gln[:, c:c + 1]),
                              reads=[W["t2"], B["cols"]], writes=[W["sl"]])
                    P.add("act", lambda e: e.activation(out=s2b[:], in_=sl[:].rearrange("p c t -> p (c t)"), func=AF.Square),
                          reads=[W["sl"]], writes=[W["s2b"]])
                    bS2 = gbank()
                    P.add("pe", [lambda e, c=c: e.matmul(psum[bS2][:, 0:128], onesb[:], s2b[:, c * 128:(c + 1) * 128], start=(c == 0), stop=(c == 3))
                                 for c in range(4)], reads=[B["onesb"], W["s2b"]], writes=[pbuf[bS2]])
                    P.add("act", lambda e: e.activation(out=rs2[:], in_=psum[bS2][:, 0:128], func=AF.Sqrt, bias=EPS, scale=1.0 / 512),
                          reads=[pbuf[bS2]], writes=[W["rs2"]])
                    P.add("dve", lambda e: e.reciprocal(rs2[:], rs2[:]), reads=[W["rs2"]], writes=[W["rs2"]])
                    P.add("dve", lambda e: e.tensor_tensor(out=mixT[:, 4:8, :], in0=sl[:], in1=rs2[:].unsqueeze(1).broadcast_to([128, 4, 128]),
                                                           op=ALU.mult), reads=[W["sl"], W["rs2"]], writes=[b_mTc[i % 2]])
                def segA2(i):
                    mixT = MTs[i % 2]
                    for half in range(2):
                        ov = psum[Obank[half]][:, 0:260].rearrange("p (h x) -> p h x", h=4)
                        P.add("dve", lambda e, ov=ov, half=half: e.reciprocal(rec[:, half * 4:(half + 1) * 4, :], ov[:, :, 64:65]),
                              reads=[pbuf[Obank[half]]], writes=[W["rec"]])
                        P.add("dve", lambda e, ov=ov, half=half: e.tensor_tensor(
                            out=ao[:, half * 4:(half + 1) * 4, :], in0=ov[:, :, 0:64],
                            in1=rec[:, half * 4:(half + 1) * 4, :].broadcast_to([128, 4, 64]), op=ALU.mult),
                            reads=[pbuf[Obank[half]], W["rec"]], writes=[W["ao"]])
                    aof = sig[:]
                    P.add("act", lambda e: e.activation(out=junkb[:, 0:512], in_=aof, func=AF.Square, accum_out=ssa[:, 0:1]),
                          reads=[W["ao"]], writes=[W["junkb"], W["ssa"]])
                    P.add("act", lambda e: e.activation(out=ssa[:, 1:2], in_=ssa[:, 0:1], func=AF.Sqrt, bias=EPS, scale=1.0 / 512),
                          reads=[W["ssa"]], writes=[W["ssa"]])
                    P.add("dve", lambda e: e.reciprocal(ssa[:, 2:3], ssa[:, 1:2]), reads=[W["ssa"]], writes=[W["ssa"]])
                    P.add("act", lambda e: e.activation(out=aob[:], in_=aof, func=AF.Copy, scale=ssa[:, 2:3]), reads=[W["ao"], W["ssa"]],
                          writes=[W["aob"]])
                    bk = gbank()
                    psT = psum[bk][:].bitcast(BF16)
                    P.add("pe", [lambda e, c=c, psT=psT: e.transpose(psT[:, c * 128:(c + 1) * 128], aob[:, c * 128:(c + 1) * 128], ident_b[:])
                                 for c in range(4)], reads=[W["aob"], B["ident_b"]], writes=[pbuf[bk]])
                    P.add("dve", lambda e, psT=psT: e.tensor_copy(mixT[:, 0:4, :].rearrange("p c t -> p (c t)"), psT[:, 0:512]),
                          reads=[pbuf[bk]], writes=[b_mTa[i % 2]])

                def segB2(i):
                    X, bX = xt[i % 2], b_xt[i % 2]
                    mixT = MTs[i % 2]
                    H2, bH2 = h2[i % 2], b_h2[i % 2]
                    for half in range(2):
                        bk = gbank()
                        P.add("pe", [lambda e, bk=bk, c=c, half=half: e.matmul(psum[bk][:, :], mixT[:, c, :], Wo[:, c, half * 512:(half + 1) * 512],
                                                                               start=(c == 0), stop=(c == 7)) for c in range(8)],
                              reads=[b_mTa[i % 2], b_mTc[i % 2], B["Wo"]], writes=[pbuf[bk]])
                        P.add("dve", lambda e, bk=bk, half=half, X=X, H2=H2: e.tensor_tensor(
                            out=H2[:, half * 512:(half + 1) * 512], in0=psum[bk][:, :], in1=X[:, half * 512:(half + 1) * 512], op=ALU.add),
                            reads=[pbuf[bk], bX], writes=[bH2])
                    P.add("sp", lambda e, H2=H2, i=i: e.dma_start(out=h2s[i * 128:(i + 1) * 128, :], in_=H2[:]), reads=[bH2],
                          writes=[b_h2s[i]], dma=True)
                def tail(i):
                    segB1(i)
                    yield
                    segA2(i)
                    yield
                    segB2(i)
                    yield

                for _ in segA1(0):
                    pass
                for i in range(NT):
                    segA1att(i)
                    for _ in segB1conv(i):
                        pass
                    if i + 1 < NT:
                        _interleave(segA1(i + 1), tail(i), 11, 3)
                    else:
                        for _ in tail(i):
                            pass
                P.add("sp", [], reads=[b for b in b_h2s[:NT]])
                P.emit()

        def phase0():
            with ExitStack() as s0:
                def sb(name, shape, dt):
                    return s0.enter_context(nc.sbuf_tensor("z_" + name, list(shape), dt))
                R = 4
                NCH = 128 // R
                uin = [sb("uin%d" % i, [128, R, D], F32) for i in range(2)]
                vin = [sb("vin%d" % i, [128, R, D], F32) for i in range(2)]
                uvo = [sb("uvo%d" % i, [128, R, 2 * D], BF16) for i in range(2)]
                b_uin = [Buf("uin0"), Buf("uin1")]
                b_vin = [Buf("vin0"), Buf("vin1")]
                b_uvo = [Buf("uvo0"), Buf("uvo1")]
                uview = peer_u.rearrange("(p r) d -> p r d", p=128)
                vview = peer_v.rearrange("(p r) d -> p r d", p=128)
                oview = uv16.rearrange("(p r) d -> p r d", p=128)
                for c in range(NCH):
                    k = c % 2
                    P.add("sp", lambda e, k=k, c=c: e.dma_start(out=uin[k][:], in_=uview[:, c * R:(c + 1) * R, :]), writes=[b_uin[k]], dma=True)
                    P.add("act", lambda e, k=k, c=c: e.dma_start(out=vin[k][:], in_=vview[:, c * R:(c + 1) * R, :]), writes=[b_vin[k]], dma=True)
                    P.add("dve", lambda e, k=k: e.tensor_copy(uvo[k][:, :, 0:D], uin[k][:]), reads=[b_uin[k]], writes=[b_uvo[k]])
                    P.add("act", lambda e, k=k: e.copy(uvo[k][:, :, D:2 * D], vin[k][:]), reads=[b_vin[k]], writes=[b_uvo[k]])
                    P.add("sp", lambda e, k=k, c=c: e.dma_start(out=oview[:, c * R:(c + 1) * R, :], in_=uvo[k][:]), reads=[b_uvo[k]],
                          writes=[b_uv16], dma=True)
                P.add("sp", [], reads=[b_uv16])
                P.emit()

        def phase2():
            NT2 = 32 if NT >= NT_FULL else NT - 1
            GS = 4
            with ExitStack() as s2:
                tot = [0]

                def sb(name, shape, dt):
                    n = 1
                    for x in shape[1:]:
                        n *= x
                    tot[0] += n * (4 if dt in (F32, I32, U32) else 2)
                    if os.environ.get("KERNEL_SBDBG"):
                        print("sbuf p2", name, tot[0])
                    return s2.enter_context(nc.sbuf_tensor("b_" + name, list(shape), dt))

                gcyc = [4, 5, 6, 7]
                gpos = [0]

                def gbank():
                    b = gcyc[gpos[0] % 4]
                    gpos[0] += 1
                    return b

                ident_f = sb("ident_f", [128, 128], F32)
                ident_b = sb("ident_b", [128, 128], BF16)
                iota16 = sb("iota16", [128, 16], F32)
                gffn = sb("gffn", [128, D], F32)
                gfin = sb("gfin", [128, D], F32)
                Wpq = sb("Wpq", [128, 8, D], BF16)
                KB = sb("KB", [128, 8, 256], BF16)
                kst = sb("kst", [128, 128], F32)
                ot = [sb("ot%d" % i, [128, D], F32) for i in range(2)]
                b_ot = [Buf("ot0"), Buf("ot1")]
                stage = ot
                b_stage = b_ot
                C = {n: Buf(n) for n in "ident_f ident_b iota16 gffn gfin Wpq KB kst".split()}
                P.add("sp", lambda e: e.dma_start(out=ident_f[:], in_=ident_d), writes=[C["ident_f"]], dma=True)
                P.add("sp", lambda e: e.dma_start(out=iota16[:], in_=iota_d), writes=[C["iota16"]], dma=True)
                P.add("sp", lambda e: e.dma_start(out=gffn[:], in_=g_ffn.partition_broadcast(128)), writes=[C["gffn"]], dma=True)
                P.add("sp", lambda e: e.dma_start(out=gfin[:], in_=g_fin.partition_broadcast(128)), writes=[C["gfin"]], dma=True)
                P.add("dve", lambda e: e.tensor_copy(ident_b[:], ident_f[:]), reads=[C["ident_f"]], writes=[C["ident_b"]])
                P.add("dve", lambda e: e.memset(KB[:], 0.0), writes=[C["KB"]])
                for c in range(8):
                    k = c % 2
                    P.add("sp", lambda e, k=k, c=c: e.dma_start(out=stage[k][:], in_=peer_wq[c * 128:(c + 1) * 128, :]),
                          writes=[b_stage[k]], dma=True)
                    P.add("dve" if c % 2 == 0 else "act", cp_any("dve" if c % 2 == 0 else "act", Wpq[:, c, :], stage[k][:]),
                          reads=[b_stage[k]], writes=[C["Wpq"]])
                for h in range(8):
                    P.add("sp", lambda e, h=h: e.dma_start(out=kst[:].rearrange("n (p d) -> n p d", p=2),
                                                           in_=peer_keys[h].rearrange("p n d -> n p d")),
                          writes=[C["kst"]], dma=True)
                    bk = gbank()
                    P.add("pe", lambda e, bk=bk: e.transpose(psum[bk][:, 0:128], kst[:], ident_f[:]), reads=[C["kst"], C["ident_f"]],
                          writes=[pbuf[bk]])
                    P.add("dve", lambda e, bk=bk, h=h: e.tensor_copy(KB[0:64, h, 0:128], psum[bk][0:64, 0:128]), reads=[pbuf[bk]],
                          writes=[C["KB"]])
                    P.add("dve", lambda e, bk=bk, h=h: e.tensor_copy(KB[64:128, h, 128:256], psum[bk][64:128, 0:128]), reads=[pbuf[bk]],
                          writes=[C["KB"]])

                h2t = [sb("h2t%d" % i, [128, D], F32) for i in range(2)]
                xn = sb("xn", [128, D], F32)
                xnb = [sb("xnb%d" % i, [128, D], BF16) for i in range(2)]
                idx = [sb("idx%d" % i, [128, 128], I32) for i in range(2)]
                gate = [sb("gate%d" % i, [128, 8, 16], F32) for i in range(2)]
                st2 = sb("st2", [128, 8], F32)
                junkb = sb("junkb", [128, D], BF16)
                xnT = sb("xnT", [128, 8, 128], BF16)
                qpT = sb("qpT", [128, 8, 128], BF16)
                S = sb("S", [128, 16, 128], F32)
                S2 = sb("S2", [128, 16, 128], F32)
                sv = sb("sv", [128, 8, 2, 16], F32)
                si = sb("si", [128, 8, 2, 16], U32)
                sif = sb("sif", [128, 8, 2, 16], F32)
                cand = S[:].rearrange("p (h q) n -> p h (q n)", q=2)
                cand2 = S2[:].rearrange("p (h q) n -> p h (q n)", q=2)
                tv = sb("tv", [128, 8, 16], F32)
                tp = sb("tp", [128, 8, 16], U32)
                tpf = sb("tpf", [128, 8, 16], F32)
                cmp = cand2.rearrange("p h (a b) -> p h a b", a=16)
                abf = sb("abf", [128, 8, 2, 16], F32)
                oh = S2[:].rearrange("p g n -> p (g n)").bitcast(BF16).rearrange("p (g k x) -> p g k x", g=16, k=16)
                oh2 = S[:].rearrange("p g n -> p (g n)").bitcast(BF16).rearrange("p (g k x) -> p g k x", g=16, k=16)
                sel = sb("sel", [128, 8, 2, 16], F32)
                eidf = sb("eidf", [128, 8, 16], F32)
                dd = sb("dd", [128, 8, 16], F32)
                ee = sb("ee", [128, 8, 16], F32)
                zz = sb("zz", [128, 8], F32)
                act = sb("act", [128, 128], F32)
                ga = sb("ga", [128, 128], F32)
                wgt = sb("wgt", [128, 128], F32)
                junk = junkb
                junk2 = junkb
                NPROD = 8
                prod = [sb("prod%d" % i, [128, D], BF16) for i in range(NPROD)]
                b_prod = [Buf("prod%d" % i) for i in range(NPROD)]
                NDG = 8
                dg = [sb("dg%d" % i, [128, 128], BF16) for i in range(NDG)]
                b_dg = [Buf("dg%d" % i) for i in range(NDG)]
                yy = sb("yy", [128, D], F32)
                gb = [sb("gb%d" % i, [128, 2 * D], BF16) for i in range(NB)]
                b_gb = [Buf("gb%d" % i) for i in range(NB)]
                b_h2t = [Buf("h2t0"), Buf("h2t1")]
                b_xnb = [Buf("xnb0"), Buf("xnb1")]
                b_idx = [Buf("idx0"), Buf("idx1")]
                b_gate = [Buf("gate0"), Buf("gate1")]
                b_act = [Buf("act%d" % s) for s in range(128 // GS)]
                b_ga = [Buf("ga%d" % s) for s in range(128 // GS)]
                b_w = [Buf("w%d" % s) for s in range(128 // GS)]
                T = {n: Buf(n) for n in ("st2 junkb xn xnT qpT S S2 sv si sif cand cand2 tv tp tpf abf sel eidf dd ee zz yy").split()}
                T["cand"] = T["S"]
                T["cand2"] = T["S2"]
                T["cmp"] = T["S2"]
                T["oh"] = T["S2"]
                T["oh2"] = T["S"]
                thr16 = sb("thr16", [128, 16], F32)
                C["thr16"] = Buf("thr16")
                P.add("dve", lambda e: e.tensor_scalar(out=thr16[:], in0=iota16[:], scalar1=16.0, scalar2=None, op0=ALU.mult),
                      reads=[C["iota16"]], writes=[C["thr16"]])

                def prep(t):
                    H, bH = h2t[t % 2], b_h2t[t % 2]
                    XNB, bXNB = xnb[t % 2], b_xnb[t % 2]
                    IDX, bIDX = idx[t % 2], b_idx[t % 2]
                    GT, bGT = gate[t % 2], b_gate[t % 2]
                    r0 = NMETA + t * 128
                    P.add("sp", lambda e: e.dma_start(out=H[:], in_=h2s[r0:r0 + 128, :]), reads=[b_h2s[t], b_h2s[min(t + 1, NT_FULL - 1)]],
                          writes=[bH], dma=True)
                    yield
                    P.add("act", lambda e: e.activation(out=junkb[:], in_=H[:], func=AF.Square, accum_out=st2[:, 0:1]),
                          reads=[bH], writes=[T["junkb"], T["st2"]])
                    P.add("act", lambda e: e.activation(out=st2[:, 1:2], in_=st2[:, 0:1], func=AF.Sqrt, bias=EPS, scale=1.0 / D),
                          reads=[T["st2"]], writes=[T["st2"]])
                    yield
                    P.add("dve", lambda e: e.reciprocal(st2[:, 2:3], st2[:, 1:2]), reads=[T["st2"]], writes=[T["st2"]])
                    yield
                    P.add("dve", lambda e: e.scalar_tensor_tensor(out=xn[:], in0=H[:], scalar=st2[:, 2:3], in1=gffn[:], op0=ALU.mult,
                                                                  op1=ALU.mult), reads=[bH, T["st2"], C["gffn"]], writes=[T["xn"]])
                    yield
                    P.add("act", lambda e: e.copy(XNB[:], xn[:]), reads=[T["xn"]], writes=[bXNB])
                    bk = gbank()
                    psT = psum[bk][:].bitcast(BF16)
                    P.add("pe", [lambda e, c=c, psT=psT: e.transpose(psT[:, c * 128:(c + 1) * 128], XNB[:, c * 128:(c + 1) * 128], ident_b[:])
                                 for c in range(8)], reads=[bXNB, C["ident_b"]], writes=[pbuf[bk]])
                    P.add("act", lambda e, psT=psT: e.copy(xnT[:].rearrange("p c t -> p (c t)"), psT[:, 0:1024]), reads=[pbuf[bk]],
                          writes=[T["xnT"]])
                    yield
                    for half in range(2):
                        bk = gbank()
                        fns = []
                        for hh in range(4):
                            f = half * 4 + hh
                            for c in range(8):
                                fns.append(lambda e, bk=bk, hh=hh, f=f, c=c: e.matmul(
                                    psum[bk][:, hh * 128:(hh + 1) * 128], Wpq[:, c, f * 128:(f + 1) * 128], xnT[:, c, :],
                                    start=(hh == 0 and c == 0), stop=(hh == 3 and c == 7)))
                        P.add("pe", fns, reads=[C["Wpq"], T["xnT"]], writes=[pbuf[bk]])
                        P.add("act", lambda e, bk=bk, half=half: e.copy(qpT[:, half * 4:(half + 1) * 4, :].rearrange("p h t -> p (h t)"),
                                                                        psum[bk][:, :]), reads=[pbuf[bk]], writes=[T["qpT"]])
                        yield
                    for q4 in range(4):
                        bk = gbank()
                        P.add("pe", [lambda e, bk=bk, q4=q4, u=u: e.matmul(psum[bk][:, u * 256:(u + 1) * 256], qpT[:, q4 * 2 + u, :],
                                                                          KB[:, q4 * 2 + u, :], start=(u == 0), stop=(u == 1))
                                     for u in range(2)], reads=[T["qpT"], C["KB"]], writes=[pbuf[bk]])
                        P.add("act", lambda e, bk=bk, q4=q4: e.copy(S[:, q4 * 4:(q4 + 1) * 4, :].rearrange("p g n -> p (g n)"), psum[bk][:, :]),
                              reads=[pbuf[bk]], writes=[T["S"]])
                        yield
                    for g in range(16):
                        h, p = g // 2, g % 2
                        P.add("dve", lambda e, g=g, h=h, p=p: e.max(out=sv[:, h, p, 0:8], in_=S[:, g, :]), reads=[T["S"]], writes=[T["sv"]])
                        P.add("dve", lambda e, g=g, h=h, p=p: e.max_index(out=si[:, h, p, 0:8], in_max=sv[:, h, p, 0:8], in_values=S[:, g, :]),
                              reads=[T["S"], T["sv"]], writes=[T["si"]])
                        yield
                        P.add("dve", lambda e, g=g, h=h, p=p: e.match_replace(out=S2[:, g, :], in_to_replace=sv[:, h, p, 0:8],
                                                                              in_values=S[:, g, :], imm_value=-1e30),
                              reads=[T["S"], T["sv"]], writes=[T["S2"]])
                        P.add("dve", lambda e, g=g, h=h, p=p: e.max(out=sv[:, h, p, 8:16], in_=S2[:, g, :]), reads=[T["S2"]], writes=[T["sv"]])
                        yield
                        P.add("dve", lambda e, g=g, h=h, p=p: e.max_index(out=si[:, h, p, 8:16], in_max=sv[:, h, p, 8:16], in_values=S2[:, g, :]),
                              reads=[T["S2"], T["sv"]], writes=[T["si"]])
                        yield
                    P.add("dve", lambda e: e.tensor_tensor(out=cand.rearrange("p h (a b) -> p h a b", a=16),
                                                           in0=sv[:, :, 0, :].unsqueeze(3).broadcast_to([128, 8, 16, 16]),
                                                           in1=sv[:, :, 1, :].unsqueeze(2).broadcast_to([128, 8, 16, 16]), op=ALU.add),
                          reads=[T["sv"]], writes=[T["cand"]])
                    yield
                    for h in range(8):
                        P.add("dve", lambda e, h=h: e.max(out=tv[:, h, 0:8], in_=cand[:, h, :]), reads=[T["cand"]], writes=[T["tv"]])
                        P.add("dve", lambda e, h=h: e.max_index(out=tp[:, h, 0:8], in_max=tv[:, h, 0:8], in_values=cand[:, h, :]),
                              reads=[T["cand"], T["tv"]], writes=[T["tp"]])
                        yield
                        P.add("dve", lambda e, h=h: e.match_replace(out=cand2[:, h, :], in_to_replace=tv[:, h, 0:8], in_values=cand[:, h, :],
                                                                    imm_value=-1e30), reads=[T["cand"], T["tv"]], writes=[T["cand2"]])
                        P.add("dve", lambda e, h=h: e.max(out=tv[:, h, 8:16], in_=cand2[:, h, :]), reads=[T["cand2"]], writes=[T["tv"]])
                        yield
                        P.add("dve", lambda e, h=h: e.max_index(out=tp[:, h, 8:16], in_max=tv[:, h, 8:16], in_values=cand2[:, h, :]),
                              reads=[T["cand2"], T["tv"]], writes=[T["tp"]])
                        yield
                    P.add("dve", lambda e: e.tensor_tensor(out=dd[:], in0=tv[:], in1=tv[:, :, 0:1].broadcast_to([128, 8, 16]), op=ALU.subtract),
                          reads=[T["tv"]], writes=[T["dd"]])
                    P.add("act", lambda e: e.activation(out=ee[:], in_=dd[:], func=AF.Exp), reads=[T["dd"]], writes=[T["ee"]])
                    yield
                    P.add("dve", lambda e: e.tensor_reduce(out=zz[:], in_=ee[:], axis=AX.X, op=ALU.add), reads=[T["ee"]], writes=[T["zz"]])
                    P.add("dve", lambda e: e.reciprocal(zz[:], zz[:]), reads=[T["zz"]], writes=[T["zz"]])
                    yield
                    P.add("dve", lambda e: e.tensor_tensor(out=GT[:], in0=ee[:], in1=zz[:].unsqueeze(2).broadcast_to([128, 8, 16]), op=ALU.mult),
                          reads=[T["ee"], T["zz"]], writes=[bGT])
                    yield
                    P.add("dve", lambda e: e.tensor_copy(tpf[:], tp[:]), reads=[T["tp"]], writes=[T["tpf"]])
                    P.add("dve", lambda e: e.tensor_copy(sif[:], si[:]), reads=[T["si"]], writes=[T["sif"]])
                    yield
                    P.add("dve", lambda e: e.tensor_tensor(out=cmp, in0=tpf[:].unsqueeze(3).broadcast_to([128, 8, 16, 16]),
                                                           in1=thr16[:, :].unsqueeze(1).unsqueeze(1).broadcast_to([128, 8, 16, 16]),
                                                           op=ALU.is_ge), reads=[T["tpf"], C["thr16"]], writes=[T["cmp"]])
                    yield
                    P.add("dve", lambda e: e.tensor_reduce(out=abf[:, :, 0, :], in_=cmp, axis=AX.X, op=ALU.add), reads=[T["cmp"]],
                          writes=[T["abf"]])
                    yield
                    P.add("dve", lambda e: e.tensor_scalar(out=abf[:, :, 0, :], in0=abf[:, :, 0, :], scalar1=-1.0, scalar2=None, op0=ALU.add),
                          reads=[T["abf"]], writes=[T["abf"]])
                    P.add("dve", lambda e: e.scalar_tensor_tensor(out=abf[:, :, 1, :], in0=abf[:, :, 0, :], scalar=-16.0, in1=tpf[:],
                                                                  op0=ALU.mult, op1=ALU.add), reads=[T["abf"], T["tpf"]], writes=[T["abf"]])
                    yield
                    P.add("dve", lambda e: e.tensor_tensor(out=oh, in0=iota16[:, :].unsqueeze(1).unsqueeze(1).broadcast_to([128, 16, 16, 16]),
                                                           in1=abf[:].rearrange("p h q k -> p (h q) k").unsqueeze(3).broadcast_to([128, 16, 16, 16]),
                                                           op=ALU.is_equal), reads=[C["iota16"], T["abf"]], writes=[T["oh"]])
                    yield
                    P.add("dve", lambda e: e.tensor_tensor(out=oh2, in0=oh,
                                                           in1=sif[:].rearrange("p h q x -> p (h q) x").unsqueeze(2).broadcast_to([128, 16, 16, 16]),
                                                           op=ALU.mult), reads=[T["oh"], T["sif"]], writes=[T["oh2"]])
                    yield
                    P.add("dve", lambda e: e.tensor_reduce(out=sel[:].rearrange("p h q k -> p (h q) k"), in_=oh2, axis=AX.X, op=ALU.add),
                          reads=[T["oh2"]], writes=[T["sel"]])
                    yield
                    P.add("dve", lambda e: e.scalar_tensor_tensor(out=eidf[:], in0=sel[:, :, 0, :], scalar=128.0, in1=sel[:, :, 1, :],
                                                                  op0=ALU.mult, op1=ALU.add), reads=[T["sel"]], writes=[T["eidf"]])
                    P.add("dve", lambda e: e.tensor_copy(IDX[:].rearrange("p (h k) -> p h k", h=8), eidf[:]), reads=[T["eidf"]], writes=[bIDX])
                    yield

                jobs = [(t, s) for t in range(NT2) for s in range(128)]
                nprod = [0]
                ndg = [0]

                prep_gen = {}

                def gather(jn):
                    t, s = jobs[jn]
                    if t in prep_gen:
                        for _ in prep_gen.pop(t):
                            pass
                    k = jn % NB
                    IDX, bIDX = idx[t % 2], b_idx[t % 2]
                    P.add("pool", lambda e: e.indirect_dma_start(out=gb[k][:, :], out_offset=None, in_=uv16,
                                                                 in_offset=bass.IndirectOffsetOnAxis(ap=IDX[:, s:s + 1], axis=0)),
                          reads=[bIDX, b_uv16], writes=[b_gb[k]], dma=True)

                def final_ops(t):
                    H, bH = h2t[t % 2], b_h2t[t % 2]
                    OT, bOT = ot[t % 2], b_ot[t % 2]
                    bankA, bankB = (0, 1) if t % 2 == 0 else (2, 3)
                    P.add("dve", lambda e: e.tensor_tensor(out=yy[:, 0:512], in0=psum[bankA][:, :], in1=H[:, 0:512], op=ALU.add),
                          reads=[pbuf[bankA], bH], writes=[T["yy"]])
                    P.add("dve", lambda e: e.tensor_tensor(out=yy[:, 512:1024], in0=psum[bankB][:, :], in1=H[:, 512:1024], op=ALU.add),
                          reads=[pbuf[bankB], bH], writes=[T["yy"]])
                    P.add("act", lambda e: e.activation(out=junkb[:], in_=yy[:], func=AF.Square, accum_out=st2[:, 4:5]),
                          reads=[T["yy"]], writes=[T["junkb"], T["st2"]])
                    P.add("act", lambda e: e.activation(out=st2[:, 5:6], in_=st2[:, 4:5], func=AF.Sqrt, bias=EPS, scale=1.0 / D),
                          reads=[T["st2"]], writes=[T["st2"]])
                    P.add("dve", lambda e: e.reciprocal(st2[:, 6:7], st2[:, 5:6]), reads=[T["st2"]], writes=[T["st2"]])
                    P.add("dve", lambda e: e.scalar_tensor_tensor(out=OT[:], in0=yy[:], scalar=st2[:, 6:7], in1=gfin[:], op0=ALU.mult,
                                                                  op1=ALU.mult), reads=[T["yy"], T["st2"], C["gfin"]], writes=[bOT])
                    P.add("sp", lambda e: e.dma_start(out=out[t * 128:(t + 1) * 128, :], in_=OT[:]), reads=[bOT], writes=[b_out], dma=True)


                def consume_tile(t):
                    XNB, bXNB = xnb[t % 2], b_xnb[t % 2]
                    GT, bGT = gate[t % 2], b_gate[t % 2]
                    H, bH = h2t[t % 2], b_h2t[t % 2]
                    OT, bOT = ot[t % 2], b_ot[t % 2]
                    bankA, bankB = (0, 1) if t % 2 == 0 else (2, 3)
                    base = t * 128
                    NG = 128 // GS

                    def dots_mul(g):
                        qs = []
                        for s in range(g * GS, (g + 1) * GS):
                            k = (base + s) % NB
                            q = nprod[0] % NPROD
                            nprod[0] += 1
                            qs.append(q)
                            P.add("dve", lambda e, k=k, q=q: e.tensor_tensor(out=prod[q][:], in0=gb[k][:, 0:D], in1=XNB[:], op=ALU.mult),
                                  reads=[b_gb[k], bXNB], writes=[b_prod[q]])
                        return qs

                    def dots_acc(g, qs):
                        for i_, s in enumerate(range(g * GS, (g + 1) * GS)):
                            q = qs[i_]
                            P.add("act", lambda e, s=s, q=q: e.activation(out=junk2[:], in_=prod[q][:], func=AF.Copy,
                                                                          accum_out=act[:, s:s + 1]),
                                  reads=[b_prod[q]], writes=[b_act[g]])

                    qs0 = dots_mul(0)
                    dots_acc(0, qs0)
                    yield
                    for g in range(NG):
                        s0 = g * GS
                        qs = dots_mul(g + 1) if g + 1 < NG else None
                        P.add("act", lambda e, s0=s0: e.activation(out=ga[:, s0:s0 + GS], in_=act[:, s0:s0 + GS], func=AF.Gelu),
                              reads=[b_act[g]], writes=[b_ga[g]])
                        if qs is not None:
                            dots_acc(g + 1, qs)
                        yield
                        P.add("dve", lambda e, s0=s0: e.tensor_tensor(out=wgt[:, s0:s0 + GS], in0=ga[:, s0:s0 + GS],
                                                                      in1=GT[:].rearrange("p h k -> p (h k)")[:, s0:s0 + GS], op=ALU.mult),
                              reads=[b_ga[g], bGT], writes=[b_w[g]])
                        for s in range(s0, s0 + GS):
                            k = (base + s) % NB
                            r = ndg[0] % NDG
                            ndg[0] += 1
                            P.add("dve", lambda e, s=s, r=r: e.tensor_tensor(out=dg[r][:], in0=ident_b[:],
                                                                             in1=wgt[:, s:s + 1].broadcast_to([128, 128]), op=ALU.mult),
                                  reads=[b_w[g], C["ident_b"]], writes=[b_dg[r]])
                            P.add("pe", [lambda e, s=s, r=r, k=k: e.matmul(psum[bankA][:, :], dg[r][:], gb[k][:, D:D + 512],
                                                                           start=(s == 0), stop=(s == 127)),
                                         lambda e, s=s, r=r, k=k: e.matmul(psum[bankB][:, :], dg[r][:], gb[k][:, D + 512:2 * D],
                                                                           start=(s == 0), stop=(s == 127))],
                                  reads=[b_dg[r], b_gb[k]], writes=[pbuf[bankA], pbuf[bankB]])
                            jn = base + s
                            if jn + NB < len(jobs):
                                gather(jn + NB)
                        yield
                    final_ops(t)
                    yield

                for _ in prep(0):
                    pass
                for jn in range(min(NB, len(jobs))):
                    gather(jn)
                for t in range(NT2):
                    side = None
                    if t + 1 < NT2:
                        side = prep(t + 1)
                        prep_gen[t + 1] = side
                    _interleave(consume_tile(t), side, 60, 100)
                P.add("sp", [], reads=[b_out])
                P.emit()

        if 0 in phases:
            phase0()
        if 1 in phases:
            phase1()
        if 2 in phases:
            phase2()
    return nc


def _consts():
    half = 16
    freqs = (10000.0 ** (-np.arange(half, dtype=np.float32) / half)).astype(np.float32)
    pos = np.arange(TP, dtype=np.float32)
    ang = pos[:, None] * freqs[None, :]
    cos = np.cos(ang).astype(np.float32).T
    sin = np.sin(ang).astype(np.float32).T
    cos32 = np.concatenate([cos, cos], axis=0)
    sin32 = np.concatenate([-sin, sin], axis=0)
    rope = np.zeros((NT_FULL, 32, 256), np.float32)
    for i in range(NT_FULL):
        rope[i, :, 0:128] = cos32[:, i * 128:(i + 1) * 128]
        rope[i, :, 128:256] = sin32[:, i * 128:(i + 1) * 128]
    kk = np.arange(128)
    mask = (kk[:, None] <= kk[None, :]).astype(np.float32)
    iota16 = np.tile(np.arange(16, dtype=np.float32)[None, :], (128, 1))
    return rope, mask, iota16, np.eye(128, dtype=np.float32)


def _in_maps(inputs, n_cores=8):
    f = lambda a: np.ascontiguousarray(np.asarray(a, dtype=np.float32))
    x = f(inputs["x"])
    meta = f(inputs["meta"])
    rope, mask, iota16, ident = _consts()
    vecs = np.concatenate([
        f(inputs["g_mix_norm"])[0].reshape(8, 128), f(inputs["g_q"])[0].reshape(2, 128), f(inputs["g_kv"])[0].reshape(1, 128),
        f(inputs["conv_b"])[0].reshape(4, 128), f(inputs["g_conv_ln"])[0].reshape(4, 128), f(inputs["b_conv_ln"])[0].reshape(4, 128),
        f(inputs["g_out_attn"])[0].reshape(4, 128), f(inputs["g_out_conv"])[0].reshape(4, 128)], axis=0)
    shared = {
        "w_in": f(inputs["w_in"])[0], "w_uq": f(inputs["w_uq"])[0], "w_ukv": f(inputs["w_ukv"])[0],
        "conv_w": f(inputs["conv_w"])[0], "w_out": f(inputs["w_out"])[0], "peer_wq": f(inputs["peer_wq"])[0],
        "peer_keys": f(inputs["peer_keys"])[0], "peer_u": f(inputs["peer_u"])[0], "peer_v": f(inputs["peer_v"])[0],
        "vecs": np.ascontiguousarray(vecs), "g_ffn": f(inputs["g_ffn_norm"])[0].reshape(1, D), "g_fin": f(inputs["g_final"]).reshape(1, D),
        "ident": ident, "rope": rope, "mask": mask, "iota16": iota16,
    }
    maps = []
    for b in range(n_cores):
        h0 = np.zeros((TP, D), np.float32)
        h0[:NMETA] = meta
        h0[NMETA:NMETA + SEQ] = x[b]
        m = dict(shared)
        m["h0"] = h0
        maps.append(m)
    return maps


def kernel(**inputs):
    nt = int(os.environ.get("KERNEL_NT", NT_FULL))
    nc = build_program(NT=nt)
    maps = _in_maps(inputs)
    res = run_bass_kernel_spmd(nc, maps, core_ids=list(range(8)))
    return np.stack([np.asarray(r["out"], dtype=np.float32) for r in res.results], axis=0)
```

```python
import os
import math
import numpy as np
import concourse.bass as bass
import concourse.mybir as mybir
from concourse.bass_utils import run_bass_kernel_spmd
from contextlib import ExitStack

F32 = mybir.dt.float32
BF16 = mybir.dt.bfloat16
I32 = mybir.dt.int32
U32 = mybir.dt.uint32
ALU = mybir.AluOpType
AF = mybir.ActivationFunctionType
AX = mybir.AxisListType

D = 1024
SEQ = 4096
NMETA = 16
TP = 4224
NT_FULL = 33
EPS = 1e-6
NEXP = 16384


class Buf:
    __slots__ = ("name", "last_w", "readers", "excl")

    def __init__(self, name, excl=False):
        self.name = name
        self.last_w = None
        self.readers = []
        self.excl = excl


class Op:
    __slots__ = ("eng", "fns", "deps", "signal", "ord", "is_dma", "dsem", "dval", "dprev", "batch")

    def __init__(self, eng, fns, is_dma):
        self.batch = 0
        self.eng = eng
        self.fns = fns
        self.deps = []
        self.signal = False
        self.ord = 0
        self.is_dma = is_dma
        self.dsem = None
        self.dval = 0
        self.dprev = 0


class Prog:
    ENGS = ("pe", "act", "dve", "pool", "sp")

    def __init__(self, nc, stack, dma_sems=None):
        self.nc = nc
        self.ops = []
        self.eng_ops = {e: [] for e in self.ENGS}
        self.sems = {e: stack.enter_context(nc.semaphore("sem_" + e)) for e in self.ENGS}
        self.batch = 0
        self.ordc = {e: 0 for e in self.ENGS}
        self.waited = {e: {} for e in self.ENGS}
        dma_sems = dma_sems or {"sp": 12, "act": 4, "pool": 40}
        self.dsems, self.dcount, self.dnext = {}, {}, {}
        for e, n in dma_sems.items():
            self.dsems[e] = [stack.enter_context(nc.semaphore("dsem_%s%d" % (e, i))) for i in range(n)]
            self.dcount[e] = [0] * n
            self.dnext[e] = 0

    def add(self, eng, fns, reads=(), writes=(), dma=False):
        if callable(fns):
            fns = [fns]
        op = Op(eng, list(fns), dma)
        deps, seen = [], set()

        def dep(d):
            if d is not None and id(d) not in seen:
                seen.add(id(d))
                deps.append(d)

        wr = list(writes) + [b for b in reads if b.excl]
        for b in reads:
            if not b.excl:
                dep(b.last_w)
        for b in wr:
            dep(b.last_w)
            for r in b.readers:
                dep(r)
        for b in reads:
            if not b.excl:
                b.readers.append(op)
        for b in wr:
            b.last_w = op
            b.readers = []
        op.deps = deps
        op.batch = self.batch
        if dma:
            k = self.dnext[eng]
            self.dnext[eng] = (k + 1) % len(self.dsems[eng])
            op.dsem = self.dsems[eng][k]
            op.dprev = self.dcount[eng][k]
            self.dcount[eng][k] += 16
            op.dval = self.dcount[eng][k]
        self.ops.append(op)
        self.eng_ops[eng].append(op)
        return op

    def emit(self):
        nc = self.nc
        cur = self.batch
        for op in self.ops:
            for d in op.deps:
                if not d.is_dma and d.batch == cur:
                    d.signal = True
        for e in self.ENGS:
            n = self.ordc[e]
            for op in self.eng_ops[e]:
                if op.signal and not op.is_dma:
                    n += 1
                    op.ord = n
            self.ordc[e] = n
        prog = self

        def run(ename, engobj):
            waited = prog.waited[ename]

            def wait(sem, val):
                if val <= 0 or waited.get(sem.num, 0) >= val:
                    return
                waited[sem.num] = val
                engobj.wait_ge(sem, val)

            for op in prog.eng_ops[ename]:
                for d in op.deps:
                    if d.is_dma:
                        wait(d.dsem, d.dval)
                    else:
                        if d.batch != cur:
                            continue
                        if d.eng == ename and ename == "pe":
                            continue
                        wait(prog.sems[d.eng], d.ord)
                if op.is_dma:
                    wait(op.dsem, op.dprev)
                last = None
                for fn in op.fns:
                    last = fn(engobj)
                if last is not None:
                    if op.is_dma:
                        last.then_inc(op.dsem, 16)
                    elif op.signal:
                        last.then_inc(prog.sems[ename], 1)
                else:
                    assert not op.signal

        with nc.Block() as block:
            @block.tensor
            def _(eng):
                run("pe", eng)

            @block.scalar
            def _(eng):
                run("act", eng)

            @block.vector
            def _(eng):
                run("dve", eng)

            @block.gpsimd
            def _(eng):
                run("pool", eng)

            @block.sync
            def _(eng):
                run("sp", eng)
        self.batch += 1
        self.ops = []
        self.eng_ops = {e: [] for e in self.ENGS}


def _interleave(main, side, n_main_hint, n_side_hint):
    if side is None:
        for _ in main:
            pass
        return
    ratio = max(1e-9, n_side_hint / max(1, n_main_hint))
    credit = 0.0
    side_done = False
    for _ in main:
        credit += ratio
        while credit >= 1.0 and not side_done:
            credit -= 1.0
            try:
                next(side)
            except StopIteration:
                side_done = True
    if not side_done:
        for _ in side:
            pass


def build_program(NT=NT_FULL, NB=20, phases=(0, 1, 2), dbg_h2=False):
    nc = bass.Bass("TRN2", target_bir_lowering=False)

    def din(name, shape, dt=F32):
        return nc.dram_tensor(name, list(shape), dt, kind="ExternalInput").ap()

    h0 = din("h0", [TP, D])
    w_in = din("w_in", [D, 1440])
    w_uq = din("w_uq", [256, 768])
    w_ukv = din("w_ukv", [128, 1024])
    conv_w = din("conv_w", [31, 512])
    w_out = din("w_out", [D, D])
    peer_wq = din("peer_wq", [D, D])
    peer_keys = din("peer_keys", [8, 2, 128, 64])
    peer_u = din("peer_u", [NEXP, D])
    peer_v = din("peer_v", [NEXP, D])
    vecs = din("vecs", [31, 128])
    g_ffn = din("g_ffn", [1, D])
    g_fin = din("g_fin", [1, D])
    ident_d = din("ident", [128, 128])
    rope_d = din("rope", [NT_FULL, 32, 256])
    mask_d = din("mask", [128, 128])
    iota_d = din("iota16", [128, 16])
    n_out_rows = min(SEQ, NT * 128 - NMETA)
    out = nc.dram_tensor("out", [SEQ, D], F32, kind="ExternalOutput").ap()
    h2s = nc.dram_tensor("h2s", [TP, D], F32, kind="ExternalOutput" if dbg_h2 else "Internal").ap()

    with ExitStack() as st:
        P = Prog(nc, st)
        psum = [st.enter_context(nc.psum_tensor("ps%d" % i, [128, 512], F32)) for i in range(8)]
        pbuf = [Buf("ps%d" % i, excl=True) for i in range(8)]
        b_out = Buf("out")
        uv16 = nc.dram_tensor("uv16", [NEXP, 2 * D], BF16, kind="Internal").ap()
        b_uv16 = Buf("uv16")
        b_h2s = [Buf("h2s%d" % i) for i in range(NT_FULL)]

        def cp_any(eng, o, i):
            if eng == "act":
                return lambda e: e.copy(o, i)
            return lambda e: e.tensor_copy(o, i)

        def phase1():
            with ExitStack() as s1:
                def sb(name, shape, dt):
                    return s1.enter_context(nc.sbuf_tensor("a_" + name, list(shape), dt))

                gen_cycle = [0, 1, 4, 5, 6, 7]
                gen_pos = [0]

                def gbank():
                    b = gen_cycle[gen_pos[0] % len(gen_cycle)]
                    gen_pos[0] += 1
                    return b

                ident_f = sb("ident_f", [128, 128], F32)
                ident_b = sb("ident_b", [128, 128], BF16)
                onesb = sb("onesb", [128, 128], BF16)
                maskf = sb("maskf", [128, 128], F32)
                maskb = sb("maskb", [128, 128], BF16)
                vst = sb("vst", [31, 128], F32)
                cols = sb("cols", [128, 31], F32)
                cw = sb("cw", [128, 4, 31], F32)
                Wb = sb("Wb", [128, 8, 1440], BF16)
                wkrot = sb("wkrot", [128, 8, 96], BF16)
                Wq = sb("Wq", [128, 2, 8, 96], BF16)
                Wqrot = sb("Wqrot", [128, 2, 8, 96], BF16)
                Wk = sb("Wk", [128, 8, 64], BF16)
                Wv = sb("Wv", [128, 8, 64], BF16)
                Wo = sb("Wo", [128, 8, 1024], BF16)
                KT = sb("KT", [128, 8, NT * 128], BF16)
                VP = sb("VP", [128, NT, 8, 65], BF16)
                Gs = [sb("G%d" % i, [128, 4, 158], F32) for i in range(2)]
                b_G = [Buf("G0"), Buf("G1")]
                cwst = Gs[1][0:31, :, :].rearrange("p c t -> p (c t)")[:, 0:512]
                xt = [sb("xt%d" % i, [128, D], F32) for i in range(2)]
                h2 = [sb("h2_0", [128, D], F32)] * 2
                b_xt = [Buf("xt0"), Buf("xt1")]
                b_h2 = [Buf("h2_0")] * 2
                stage = xt + h2[:1]
                b_stage = b_xt + b_h2[:1]
                B = {n: Buf(n) for n in ("ident_f ident_b onesb maskf maskb vst cols cwst cw Wb wkrot Wq Wqrot Wk Wv Wo "
                                         "KT VP G").split()}
                gm, gq, gkv = cols[:, 0:8], cols[:, 8:10], cols[:, 10:11]
                cb, gln, bln = cols[:, 11:15], cols[:, 15:19], cols[:, 19:23]
                goa, goc = cols[:, 23:27], cols[:, 27:31]

                P.add("sp", lambda e: e.dma_start(out=ident_f[:], in_=ident_d), writes=[B["ident_f"]], dma=True)
                P.add("sp", lambda e: e.dma_start(out=maskf[:], in_=mask_d), writes=[B["maskf"]], dma=True)
                P.add("sp", lambda e: e.dma_start(out=vst[:], in_=vecs), writes=[B["vst"]], dma=True)
                P.add("sp", lambda e: e.dma_start(out=cwst, in_=conv_w), writes=[b_G[1]], dma=True)
                P.add("dve", lambda e: e.tensor_copy(ident_b[:], ident_f[:]), reads=[B["ident_f"]], writes=[B["ident_b"]])
                P.add("dve", lambda e: e.tensor_scalar(out=maskb[:], in0=maskf[:], scalar1=-1.0, scalar2=30000.0, op0=ALU.add, op1=ALU.mult),
                      reads=[B["maskf"]], writes=[B["maskb"]])
                P.add("dve", lambda e: e.memset(onesb[:], 1.0), writes=[B["onesb"]])
                P.add("dve", lambda e: e.memset(Gs[0][:], 0.0), writes=[b_G[0]])
                P.add("pool", lambda e: e.memset(VP[:], 1.0), writes=[B["VP"]])
                P.add("pool", lambda e: e.memset(KT[:], 0.0), writes=[B["KT"]])
                bk = gbank()
                P.add("pe", lambda e, bk=bk: e.matmul(psum[bk][:, 0:31], vst[0:31, :], ident_f[0:31, 0:31], start=True, stop=True),
                      reads=[B["vst"], B["ident_f"]], writes=[pbuf[bk]])
                P.add("dve", lambda e, bk=bk: e.tensor_copy(cols[:], psum[bk][:, 0:31]), reads=[pbuf[bk]], writes=[B["cols"]])
                bk = gbank()
                fns = []
                for c in range(4):
                    fns.append(lambda e, bk=bk, c=c: e.matmul(psum[bk][:, c * 31:(c + 1) * 31], cwst[:, c * 128:(c + 1) * 128],
                                                              ident_f[0:31, 0:31], start=(c == 0), stop=(c == 3)))
                P.add("pe", fns, reads=[b_G[1], B["ident_f"]], writes=[pbuf[bk]])
                P.add("dve", lambda e, bk=bk: e.tensor_copy(cw[:].rearrange("p c k -> p (c k)"), psum[bk][:, 0:124]),
                      reads=[pbuf[bk]], writes=[B["cw"]])
                sidx = [0]

                def stage_load(src_ap, ncol):
                    k = sidx[0] % 3
                    sidx[0] += 1
                    P.add("sp", lambda e, k=k: e.dma_start(out=stage[k][:, 0:ncol], in_=src_ap), writes=[b_stage[k]], dma=True)
                    return k

                for c in range(8):
                    for (c0, c1) in ((0, 1024), (1024, 1440)):
                        k = stage_load(w_in[c * 128:(c + 1) * 128, c0:c1], c1 - c0)
                        if c % 2 == 0:
                            P.add("dve", lambda e, k=k, c=c, c0=c0, c1=c1: e.tensor_scalar(out=Wb[:, c, c0:c1], in0=stage[k][:, 0:c1 - c0],
                                                                                           scalar1=gm[:, c:c + 1], scalar2=None, op0=ALU.mult),
                                  reads=[b_stage[k], B["cols"]], writes=[B["Wb"]])
                        else:
                            P.add("act", lambda e, k=k, c=c, c0=c0, c1=c1: e.activation(out=Wb[:, c, c0:c1], in_=stage[k][:, 0:c1 - c0],
                                                                                        func=AF.Copy, scale=gm[:, c:c + 1]),
                                  reads=[b_stage[k], B["cols"]], writes=[B["Wb"]])
                P.add("dve", lambda e: e.tensor_copy(wkrot[:, :, 0:64], Wb[:, :, 1344:1408]), reads=[B["Wb"]], writes=[B["wkrot"]])
                P.add("dve", lambda e: e.tensor_copy(wkrot[:, :, 64:80], Wb[:, :, 1424:1440]), reads=[B["Wb"]], writes=[B["wkrot"]])
                P.add("dve", lambda e: e.tensor_copy(wkrot[:, :, 80:96], Wb[:, :, 1408:1424]), reads=[B["Wb"]], writes=[B["wkrot"]])
                for c in range(2):
                    k = stage_load(w_uq[c * 128:(c + 1) * 128, :], 768)
                    P.add("dve", lambda e, k=k, c=c: e.tensor_scalar(out=Wq[:, c, :, :].rearrange("p h d -> p (h d)"),
                                                                     in0=stage[k][:, 0:768], scalar1=gq[:, c:c + 1],
                                                                     scalar2=None, op0=ALU.mult),
                          reads=[b_stage[k], B["cols"]], writes=[B["Wq"]])
                P.add("dve", lambda e: e.tensor_copy(Wqrot[:, :, :, 0:64], Wq[:, :, :, 0:64]), reads=[B["Wq"]], writes=[B["Wqrot"]])
                P.add("dve", lambda e: e.tensor_copy(Wqrot[:, :, :, 64:80], Wq[:, :, :, 80:96]), reads=[B["Wq"]], writes=[B["Wqrot"]])
                P.add("dve", lambda e: e.tensor_copy(Wqrot[:, :, :, 80:96], Wq[:, :, :, 64:80]), reads=[B["Wq"]], writes=[B["Wqrot"]])
                k = stage_load(w_ukv, 1024)
                stv = stage[k][:, 0:1024].rearrange("p (h x) -> p h x", h=8)
                P.add("dve", lambda e, stv=stv: e.tensor_scalar(out=Wk[:], in0=stv[:, :, 0:64], scalar1=gkv[:, 0:1], scalar2=None,
                                                                op0=ALU.mult), reads=[b_stage[k], B["cols"]], writes=[B["Wk"]])
                P.add("dve", lambda e, stv=stv: e.tensor_scalar(out=Wv[:], in0=stv[:, :, 64:128], scalar1=gkv[:, 0:1], scalar2=None,
                                                                op0=ALU.mult), reads=[b_stage[k], B["cols"]], writes=[B["Wv"]])
                for c in range(8):
                    k = stage_load(w_out[c * 128:(c + 1) * 128, :], 1024)
                    sc = goa[:, c:c + 1] if c < 4 else goc[:, c - 4:c - 3]
                    if c % 2 == 0:
                        P.add("dve", lambda e, k=k, c=c, sc=sc: e.tensor_scalar(out=Wo[:, c, :], in0=stage[k][:, 0:1024], scalar1=sc,
                                                                                scalar2=None, op0=ALU.mult),
                              reads=[b_stage[k], B["cols"]], writes=[B["Wo"]])
                    else:
                        P.add("act", lambda e, k=k, c=c, sc=sc: e.activation(out=Wo[:, c, :], in_=stage[k][:, 0:1024], func=AF.Copy, scale=sc),
                              reads=[b_stage[k], B["cols"]], writes=[B["Wo"]])

                rp = [sb("rp%d" % i, [128, 2, 128], F32) for i in range(2)]
                st1 = sb("st1", [128, 8], F32)
                hs = sb("hs", [128, D], BF16)
                junkb = hs
                hT = sb("hT", [128, 8, 128], BF16)
                sig = sb("sig", [128, 512], F32)
                cv = sb("cv", [128, 4, 128], F32)
                cbf = sb("cbf", [128, 512], BF16)
                c2b = sb("c2b", [128, 512], BF16)
                mean = sb("mean", [128, 128], F32)
                m2 = sb("m2", [128, 128], F32)
                var = sb("var", [128, 128], F32)
                rs = sb("rs", [128, 128], F32)
                tt = sb("tt", [128, 4, 128], F32)
                t2 = sb("t2", [128, 4, 128], F32)
                sl = sb("sl", [128, 4, 128], F32)
                s2b = c2b
                rs2 = sb("rs2", [128, 128], F32)
                MTs = [sb("mixT%d" % i, [128, 8, 128], BF16) for i in range(2)]
                b_mTa = [Buf("mTa0"), Buf("mTa1")]
                b_mTc = [Buf("mTc0"), Buf("mTc1")]
                cq = sb("cq", [128, 2, 128], F32)
                cq2 = sb("cq2", [128, 256], BF16)
                rsq = sb("rsq", [128, 128], F32)
                cqn = sb("cqn", [128, 2, 128], BF16)
                ckv = sb("ckv", [128, 128], F32)
                ckv2 = sb("ckv2", [128, 128], BF16)
                rskv = sb("rskv", [128, 128], F32)
                ckvn = sb("ckvn", [128, 128], BF16)
                kr1 = sb("kr1", [128, 128], F32)
                kr2 = sb("kr2", [128, 128], F32)
                qT = sb("qT", [128, 8, 128], BF16)
                qr1 = tt
                qr2 = t2
                PT = [sb("PT%d" % i, [128, 4, 128], BF16) for i in range(4)]
                rec = sb("rec", [128, 8, 1], F32)
                ao = sig[:].rearrange("p (h x) -> p h x", h=8)
                aob = sb("aob", [128, 512], BF16)
                W = {n: Buf(n) for n in ("junkb st1 hs hT sig cbf c2b mean m2 var rs tt t2 sl s2b rs2 mixTa mixTc cq cq2 "
                                         "rsq cqn ckv ckv2 rskv ckvn kr1 kr2 qT qr1 qr2 rec ao aob ssa rsa").split()}
                b_rp = [Buf("rp0"), Buf("rp1")]
                b_cv = [Buf("cv%d" % c) for c in range(4)]
                b_PT = [Buf("PT%d" % i) for i in range(4)]
                W["junkb"] = W["hs"]
                W["s2b"] = W["c2b"]
                W["qr1"] = W["tt"]
                W["qr2"] = W["t2"]
                W["ao"] = W["sig"]
                ssa = sb("ssa", [128, 4], F32)
                SCALE = 1.0 / math.sqrt(96.0)

                Obank = (2, 3)

                def segA1(i):
                    X, bX = xt[i % 2], b_xt[i % 2]
                    R, bR = rp[i % 2], b_rp[i % 2]
                    tc = slice(i * 128, (i + 1) * 128)
                    G, bG = Gs[i % 2], b_G[i % 2]
                    P.add("sp", lambda e, X=X, i=i: e.dma_start(out=X[:], in_=h0[i * 128:(i + 1) * 128, :]), writes=[bX], dma=True)
                    P.add("sp", lambda e, R=R, i=i: e.dma_start(out=R[64:96, :, :].rearrange("p a t -> p (a t)"), in_=rope_d[i]),
                          writes=[bR], dma=True)
                    P.add("act", lambda e, X=X: e.activation(out=junkb[:], in_=X[:], func=AF.Square, accum_out=st1[:, 0:1]),
                          reads=[bX], writes=[W["junkb"], W["st1"]])
                    P.add("act", lambda e: e.activation(out=st1[:, 1:2], in_=st1[:, 0:1], func=AF.Sqrt, bias=EPS, scale=1.0 / D),
                          reads=[W["st1"]], writes=[W["st1"]])
                    yield
                    P.add("dve", lambda e: e.reciprocal(st1[:, 2:3], st1[:, 1:2]), reads=[W["st1"]], writes=[W["st1"]])
                    P.add("act", lambda e, X=X: e.activation(out=hs[:], in_=X[:], func=AF.Copy, scale=st1[:, 2:3]),
                          reads=[bX, W["st1"]], writes=[W["hs"]])
                    bk = gbank()
                    psT = psum[bk][:].bitcast(BF16)
                    P.add("pe", [lambda e, c=c, psT=psT: e.transpose(psT[:, c * 128:(c + 1) * 128], hs[:, c * 128:(c + 1) * 128], ident_b[:])
                                 for c in range(8)], reads=[W["hs"], B["ident_b"]], writes=[pbuf[bk]])
                    yield
                    P.add("dve", lambda e, psT=psT: e.tensor_copy(hT[:].rearrange("p c t -> p (c t)"), psT[:, 0:1024]),
                          reads=[pbuf[bk]], writes=[W["hT"]])
                    bA, bB, bC, bD = gbank(), gbank(), gbank(), gbank()
                    for (bk, col0) in ((bA, 0), (bB, 512)):
                        fns = []
                        for cc in range(4):
                            for c in range(8):
                                fns.append(lambda e, bk=bk, cc=cc, c=c, col0=col0: e.matmul(
                                    psum[bk][:, cc * 128:(cc + 1) * 128], Wb[:, c, col0 + cc * 128:col0 + (cc + 1) * 128], hT[:, c, :],
                                    start=(cc == 0 and c == 0), stop=(cc == 3 and c == 7)))
                        P.add("pe", fns, reads=[B["Wb"], W["hT"]], writes=[pbuf[bk]])
                    fns = []
                    for cc in (0, 3, 1, 2):
                        for c in range(8):
                            if cc == 3:
                                fns.append(lambda e, c=c: e.matmul(psum[bC][0:96, 384:512], Wb[:, c, 1344:1440], hT[:, c, :],
                                                                   start=False, stop=False))
                            else:
                                fns.append(lambda e, cc=cc, c=c: e.matmul(
                                    psum[bC][:, cc * 128:(cc + 1) * 128], Wb[:, c, 1024 + cc * 128:1024 + (cc + 1) * 128], hT[:, c, :],
                                    start=(cc == 0 and c == 0), stop=(cc == 2 and c == 7)))
                    P.add("pe", fns, reads=[B["Wb"], W["hT"]], writes=[pbuf[bC]])
                    P.add("pe", [lambda e, c=c: e.matmul(psum[bD][0:96, 0:128], wkrot[:, c, :], hT[:, c, :], start=(c == 0), stop=(c == 7))
                                 for c in range(8)], reads=[B["wkrot"], W["hT"]], writes=[pbuf[bD]])
                    P.add("act", lambda e: e.activation(out=sig[:], in_=psum[bB][:, :], func=AF.Sigmoid), reads=[pbuf[bB]], writes=[W["sig"]])
                    yield
                    P.add("dve", lambda e: e.tensor_tensor(out=G[:, :, 30:158], in0=psum[bA][:, :].rearrange("p (c t) -> p c t", c=4),
                                                           in1=sig[:].rearrange("p (c t) -> p c t", c=4), op=ALU.mult),
                          reads=[pbuf[bA], W["sig"]], writes=[bG])
                    P.add("act", lambda e: e.copy(cq[:].rearrange("p c t -> p (c t)"), psum[bC][:, 0:256]), reads=[pbuf[bC]], writes=[W["cq"]])
                    P.add("act", lambda e: e.activation(out=cq2[:], in_=psum[bC][:, 0:256], func=AF.Square), reads=[pbuf[bC]], writes=[W["cq2"]])
                    P.add("act", lambda e: e.copy(ckv[:], psum[bC][:, 256:384]), reads=[pbuf[bC]], writes=[W["ckv"]])
                    P.add("act", lambda e: e.activation(out=ckv2[:], in_=psum[bC][:, 256:384], func=AF.Square), reads=[pbuf[bC]], writes=[W["ckv2"]])
                    yield
                    P.add("dve", lambda e, R=R: e.tensor_tensor(out=kr1[64:96, :], in0=psum[bC][64:96, 384:512], in1=R[64:96, 0, :], op=ALU.mult),
                          reads=[pbuf[bC], bR], writes=[W["kr1"]])
                    P.add("dve", lambda e, R=R: e.tensor_tensor(out=kr2[64:96, :], in0=psum[bD][64:96, 0:128], in1=R[64:96, 1, :], op=ALU.mult),
                          reads=[pbuf[bD], bR], writes=[W["kr2"]])
                    P.add("dve", lambda e, tc=tc: e.tensor_tensor(out=KT[64:96, 0, tc], in0=kr1[64:96, :], in1=kr2[64:96, :], op=ALU.add),
                          reads=[W["kr1"], W["kr2"]], writes=[B["KT"]])
                    P.add("dve", lambda e, tc=tc: e.tensor_copy(KT[64:96, 1:8, tc], KT[64:96, 0:1, tc].broadcast_to([32, 7, 128])),
                          reads=[B["KT"]], writes=[B["KT"]])
                    bE = gbank()
                    P.add("pe", [lambda e, c=c: e.matmul(psum[bE][:, 0:128], onesb[:], cq2[:, c * 128:(c + 1) * 128], start=(c == 0), stop=False)
                                 for c in range(2)] +
                          [lambda e: e.matmul(psum[bE][:, 128:256], onesb[:], ckv2[:], start=False, stop=True)],
                          reads=[B["onesb"], W["cq2"], W["ckv2"]], writes=[pbuf[bE]])
                    P.add("act", lambda e: e.activation(out=rsq[:], in_=psum[bE][:, 0:128], func=AF.Sqrt, bias=EPS, scale=1.0 / 256),
                          reads=[pbuf[bE]], writes=[W["rsq"]])
                    P.add("act", lambda e: e.activation(out=rskv[:], in_=psum[bE][:, 128:256], func=AF.Sqrt, bias=EPS, scale=1.0 / 128),
                          reads=[pbuf[bE]], writes=[W["rskv"]])
                    yield
                    P.add("dve", lambda e: e.reciprocal(rsq[:], rsq[:]), reads=[W["rsq"]], writes=[W["rsq"]])
                    P.add("dve", lambda e: e.reciprocal(rskv[:], rskv[:]), reads=[W["rskv"]], writes=[W["rskv"]])
                    P.add("dve", lambda e: e.tensor_tensor(out=cqn[:], in0=cq[:], in1=rsq[:].unsqueeze(1).broadcast_to([128, 2, 128]), op=ALU.mult),
                          reads=[W["cq"], W["rsq"]], writes=[W["cqn"]])
                    P.add("dve", lambda e: e.tensor_tensor(out=ckvn[:], in0=ckv[:], in1=rskv[:], op=ALU.mult),
                          reads=[W["ckv"], W["rskv"]], writes=[W["ckvn"]])
                    bq = [gbank(), gbank()]
                    for half in range(2):
                        fns = []
                        for hh in range(4):
                            h = half * 4 + hh
                            for c in range(2):
                                fns.append(lambda e, half=half, hh=hh, h=h, c=c: e.matmul(
                                    psum[bq[half]][0:96, hh * 128:(hh + 1) * 128], Wq[:, c, h, :], cqn[:, c, :],
                                    start=(hh == 0 and c == 0), stop=(hh == 3 and c == 1)))
                        P.add("pe", fns, reads=[B["Wq"], W["cqn"]], writes=[pbuf[bq[half]]])
                    bkk = [gbank(), gbank()]
                    for half in range(2):
                        P.add("pe", [lambda e, half=half, hh=hh: e.matmul(psum[bkk[half]][0:64, hh * 128:(hh + 1) * 128], Wk[:, half * 4 + hh, :],
                                                                          ckvn[:], start=(hh == 0), stop=(hh == 3)) for hh in range(4)],
                              reads=[B["Wk"], W["ckvn"]], writes=[pbuf[bkk[half]]])
                    for half in range(2):
                        P.add("act", lambda e, half=half: e.copy(qT[0:64, half * 4:(half + 1) * 4, :],
                                                                 psum[bq[half]][0:64, :].rearrange("p (h t) -> p h t", h=4)),
                              reads=[pbuf[bq[half]]], writes=[W["qT"]])
                        yield
                        P.add("dve", lambda e, half=half, R=R: e.tensor_tensor(
                            out=qr1[64:96, :, :], in0=psum[bq[half]][64:96, :].rearrange("p (h t) -> p h t", h=4),
                            in1=R[64:96, 0:1, :].broadcast_to([32, 4, 128]), op=ALU.mult),
                            reads=[pbuf[bq[half]], bR], writes=[W["qr1"]])
                        brot = gbank()
                        fns = []
                        for hh in range(4):
                            h = half * 4 + hh
                            for c in range(2):
                                fns.append(lambda e, brot=brot, hh=hh, h=h, c=c: e.matmul(
                                    psum[brot][0:96, hh * 128:(hh + 1) * 128], Wqrot[:, c, h, :], cqn[:, c, :],
                                    start=(hh == 0 and c == 0), stop=(hh == 3 and c == 1)))
                        P.add("pe", fns, reads=[B["Wqrot"], W["cqn"]], writes=[pbuf[brot]])
                        yield
                        P.add("dve", lambda e, brot=brot, R=R: e.tensor_tensor(
                            out=qr2[64:96, :, :], in0=psum[brot][64:96, :].rearrange("p (h t) -> p h t", h=4),
                            in1=R[64:96, 1:2, :].broadcast_to([32, 4, 128]), op=ALU.mult),
                            reads=[pbuf[brot], bR], writes=[W["qr2"]])
                        P.add("dve", lambda e, half=half: e.tensor_tensor(out=qT[64:96, half * 4:(half + 1) * 4, :], in0=qr1[64:96, :, :],
                                                                          in1=qr2[64:96, :, :], op=ALU.add),
                              reads=[W["qr1"], W["qr2"]], writes=[W["qT"]])
                        P.add("act", lambda e, half=half, tc=tc: e.copy(KT[0:64, half * 4:(half + 1) * 4, tc],
                                                                        psum[bkk[half]][0:64, :].rearrange("p (h t) -> p h t", h=4)),
                              reads=[pbuf[bkk[half]]], writes=[B["KT"]])
                    bv = gbank()
                    P.add("pe", lambda e: e.matmul(psum[bv][:, :], ckvn[:], Wv[:].rearrange("p h x -> p (h x)"), start=True, stop=True),
                          reads=[W["ckvn"], B["Wv"]], writes=[pbuf[bv]])
                    P.add("act", lambda e, i=i: e.copy(VP[:, i, :, 0:64], psum[bv][:, :].rearrange("p (h x) -> p h x", h=8)),
                          reads=[pbuf[bv]], writes=[B["VP"]])
                    yield

                def segA1att(i):
                    pend = None
                    nPT = [0]

                    def issue_pv(j, pts):
                        for half in range(2):
                            k_pt = pts[half]
                            P.add("pe", [lambda e, half=half, hh=hh, k_pt=k_pt, j=j: e.matmul(
                                psum[Obank[half]][:, hh * 65:(hh + 1) * 65], PT[k_pt][:, hh, :], VP[:, j, half * 4 + hh, :],
                                start=(j == 0 and hh == 0), stop=(j == i and hh == 3)) for hh in range(4)],
                                reads=[b_PT[k_pt], B["VP"]], writes=[pbuf[Obank[half]]])

                    for j in range(i + 1):
                        pts = []
                        for half in range(2):
                            sbk = (4 + half) if (j % 2 == 0) else (6 + half)
                            fns = [lambda e, half=half, hh=hh, sbk=sbk, j=j: e.matmul(
                                psum[sbk][:, hh * 128:(hh + 1) * 128], KT[0:96, half * 4 + hh, j * 128:(j + 1) * 128],
                                qT[0:96, half * 4 + hh, :], start=(hh == 0), stop=(hh == 3 and j != i)) for hh in range(4)]
                            if j == i:
                                fns += [lambda e, hh=hh, sbk=sbk: e.matmul(psum[sbk][:, hh * 128:(hh + 1) * 128], ident_b[:], maskb[:],
                                                                           start=False, stop=(hh == 3)) for hh in range(4)]
                            P.add("pe", fns, reads=[B["KT"], W["qT"], B["ident_b"], B["maskb"]], writes=[pbuf[sbk]])
                            k_pt = nPT[0] % 4
                            nPT[0] += 1
                            P.add("act", lambda e, sbk=sbk, k_pt=k_pt: e.activation(out=PT[k_pt][:].rearrange("p h t -> p (h t)"),
                                                                                    in_=psum[sbk][:, :], func=AF.Exp, scale=SCALE),
                                  reads=[pbuf[sbk]], writes=[b_PT[k_pt]])
                            pts.append(k_pt)
                        if pend is not None:
                            issue_pv(*pend)
                        pend = (j, pts)
                    issue_pv(*pend)
                def segB1conv(i):
                    G, bG = Gs[i % 2], b_G[i % 2]
                    Gn, bGn = Gs[(i + 1) % 2], b_G[(i + 1) % 2]
                    for k in range(31):
                        yield
                        for c in range(4):
                            if k == 0:
                                P.add("dve", lambda e, c=c: e.tensor_scalar(out=cv[:, c, :], in0=G[:, c, 0:128], scalar1=cw[:, c, 0:1],
                                                                            scalar2=cb[:, c:c + 1], op0=ALU.mult, op1=ALU.add),
                                      reads=[bG, B["cw"], B["cols"]], writes=[b_cv[c]])
                            else:
                                P.add("dve", lambda e, c=c, k=k: e.scalar_tensor_tensor(out=cv[:, c, :], in0=G[:, c, k:k + 128],
                                                                                        scalar=cw[:, c, k:k + 1], in1=cv[:, c, :],
                                                                                        op0=ALU.mult, op1=ALU.add),
                                      reads=[bG, B["cw"], b_cv[c]], writes=[b_cv[c]])
                    P.add("dve", lambda e: e.tensor_copy(Gn[:, :, 0:30], G[:, :, 128:158]), reads=[bG], writes=[bGn])
                    yield

                def segB1(i):
                    mixT = MTs[i % 2]
                    cvf = cv[:].rearrange("p c t -> p (c t)")
                    P.add("act", lambda e: e.copy(cbf[:], cvf), reads=b_cv, writes=[W["cbf"]])
                    P.add("act", lambda e: e.activation(out=c2b[:], in_=cvf, func=AF.Square), reads=b_cv, writes=[W["c2b"]])
                    bS = gbank()
                    P.add("pe", [lambda e, c=c: e.matmul(psum[bS][:, 0:128], onesb[:], cbf[:, c * 128:(c + 1) * 128], start=(c == 0), stop=False)
                                 for c in range(4)] +
                          [lambda e, c=c: e.matmul(psum[bS][:, 128:256], onesb[:], c2b[:, c * 128:(c + 1) * 128], start=False, stop=(c == 3))
                           for c in range(4)], reads=[B["onesb"], W["cbf"], W["c2b"]], writes=[pbuf[bS]])
                    P.add("dve", lambda e: e.tensor_scalar(out=mean[:], in0=psum[bS][:, 0:128], scalar1=1.0 / 512, scalar2=None, op0=ALU.mult),
                          reads=[pbuf[bS]], writes=[W["mean"]])
                    P.add("dve", lambda e: e.tensor_tensor(out=m2[:], in0=mean[:], in1=mean[:], op=ALU.mult), reads=[W["mean"]], writes=[W["m2"]])
                    P.add("dve", lambda e: e.scalar_tensor_tensor(out=var[:], in0=psum[bS][:, 128:256], scalar=1.0 / 512, in1=m2[:],
                                                                  op0=ALU.mult, op1=ALU.subtract),
                          reads=[pbuf[bS], W["m2"]], writes=[W["var"]])
                    P.add("act", lambda e: e.activation(out=rs[:], in_=var[:], func=AF.Sqrt, bias=EPS, scale=1.0), reads=[W["var"]], writes=[W["rs"]])
                    P.add("dve", lambda e: e.reciprocal(rs[:], rs[:]), reads=[W["rs"]], writes=[W["rs"]])
                    P.add("dve", lambda e: e.tensor_tensor(out=tt[:], in0=cv[:], in1=mean[:].unsqueeze(1).broadcast_to([128, 4, 128]), op=ALU.subtract),
                          reads=b_cv + [W["mean"]], writes=[W["tt"]])
                    P.add("dve", lambda e: e.tensor_tensor(out=t2[:], in0=tt[:], in1=rs[:].unsqueeze(1).broadcast_to([128, 4, 128]), op=ALU.mult),
                          reads=[W["tt"], W["rs"]], writes=[W["t2"]])
                    for c in range(4):
                        P.add("act", lambda e, c=c: e.activation(out=sl[:, c, :], in_=t2[:, c, :], func=AF.Silu, bias=bln[:, c:c + 1],
                                                                 scale=gln[:, c:c + 1]),
                              reads=[W["t2"], B["cols"]], writes=[W["sl"]])
                    P.add("act", lambda e: e.activation(out=s2b[:], in_=sl[:].rearrange("p c t -> p (c t)"), func=AF.Square),
                          reads=[W["sl"]], writes=[W["s2b"]])
                    bS2 = gbank()
                    P.add("pe", [lambda e, c=c: e.matmul(psum[bS2][:, 0:128], onesb[:], s2b[:, c * 128:(c + 1) * 128], start=(c == 0), stop=(c == 3))
                                 for c in range(4)], reads=[B["onesb"], W["s2b"]], writes=[pbuf[bS2]])
                    P.add("act", lambda e: e.activation(out=rs2[:], in_=psum[bS2][:, 0:128], func=AF.Sqrt, bias=EPS, scale=1.0 / 512),
                          reads=[pbuf[bS2]], writes=[W["rs2"]])
                    P.add("dve", lambda e: e.reciprocal(rs2[:], rs2[:]), reads=[W["rs2"]], writes=[W["rs2"]])
                    P.add("dve", lambda e: e.tensor_tensor(out=mixT[:, 4:8, :], in0=sl[:], in1=rs2[:].unsqueeze(1).broadcast_to([128, 4, 128]),
                                                           op=ALU.mult), reads=[W["sl"], W["rs2"]], writes=[b_mTc[i % 2]])
                def segA2(i):
                    mixT = MTs[i % 2]
                    for half in range(2):
                        ov = psum[Obank[half]][:, 0:260].rearrange("p (h x) -> p h x", h=4)
                        P.add("dve", lambda e, ov=ov, half=half: e.reciprocal(rec[:, half * 4:(half + 1) * 4, :], ov[:, :, 64:65]),
                              reads=[pbuf[Obank[half]]], writes=[W["rec"]])
                        P.add("dve", lambda e, ov=ov, half=half: e.tensor_tensor(
                            out=ao[:, half * 4:(half + 1) * 4, :], in0=ov[:, :, 0:64],
                            in1=rec[:, half * 4:(half + 1) * 4, :].broadcast_to([128, 4, 64]), op=ALU.mult),
                            reads=[pbuf[Obank[half]], W["rec"]], writes=[W["ao"]])
                    aof = sig[:]
                    P.add("act", lambda e: e.activation(out=junkb[:, 0:512], in_=aof, func=AF.Square, accum_out=ssa[:, 0:1]),
                          reads=[W["ao"]], writes=[W["junkb"], W["ssa"]])
                    P.add("act", lambda e: e.activation(out=ssa[:, 1:2], in_=ssa[:, 0:1], func=AF.Sqrt, bias=EPS, scale=1.0 / 512),
                          reads=[W["ssa"]], writes=[W["ssa"]])
                    P.add("dve", lambda e: e.reciprocal(ssa[:, 2:3], ssa[:, 1:2]), reads=[W["ssa"]], writes=[W["ssa"]])
                    P.add("act", lambda e: e.activation(out=aob[:], in_=aof, func=AF.Copy, scale=ssa[:, 2:3]), reads=[W["ao"], W["ssa"]],
                          writes=[W["aob"]])
                    bk = gbank()
                    psT = psum[bk][:].bitcast(BF16)
                    P.add("pe", [lambda e, c=c, psT=psT: e.transpose(psT[:, c * 128:(c + 1) * 128], aob[:, c * 128:(c + 1) * 128], ident_b[:])
                                 for c in range(4)], reads=[W["aob"], B["ident_b"]], writes=[pbuf[bk]])
                    P.add("dve", lambda e, psT=psT: e.tensor_copy(mixT[:, 0:4, :].rearrange("p c t -> p (c t)"), psT[:, 0:512]),
                          reads=[pbuf[bk]], writes=[b_mTa[i % 2]])

                def segB2(i):
                    X, bX = xt[i % 2], b_xt[i % 2]
                    mixT = MTs[i % 2]
                    H2, bH2 = h2[i % 2], b_h2[i % 2]
                    for half in range(2):
                        bk = gbank()
                        P.add("pe", [lambda e, bk=bk, c=c, half=half: e.matmul(psum[bk][:, :], mixT[:, c, :], Wo[:, c, half * 512:(half + 1) * 512],
                                                                               start=(c == 0), stop=(c == 7)) for c in range(8)],
                              reads=[b_mTa[i % 2], b_mTc[i % 2], B["Wo"]], writes=[pbuf[bk]])
                        P.add("dve", lambda e, bk=bk, half=half, X=X, H2=H2: e.tensor_tensor(
                            out=H2[:, half * 512:(half + 1) * 512], in0=psum[bk][:, :], in1=X[:, half * 512:(half + 1) * 512], op=ALU.add),
                            reads=[pbuf[bk], bX], writes=[bH2])
                    P.add("sp", lambda e, H2=H2, i=i: e.dma_start(out=h2s[i * 128:(i + 1) * 128, :], in_=H2[:]), reads=[bH2],
                          writes=[b_h2s[i]], dma=True)
                def tail(i):
                    segB1(i)
                    yield
                    segA2(i)
                    yield
                    segB2(i)
                    yield

                for _ in segA1(0):
                    pass
                for i in range(NT):
                    segA1att(i)
                    for _ in segB1conv(i):
                        pass
                    if i + 1 < NT:
                        _interleave(segA1(i + 1), tail(i), 11, 3)
                    else:
                        for _ in tail(i):
                            pass
                P.add("sp", [], reads=[b for b in b_h2s[:NT]])
                P.emit()

        def phase0():
            with ExitStack() as s0:
                def sb(name, shape, dt):
                    return s0.enter_context(nc.sbuf_tensor("z_" + name, list(shape), dt))
                R = 4
                NCH = 128 // R
                NBUF0 = 3
                uin = [sb("uin%d" % i, [128, R, D], F32) for i in range(NBUF0)]
                vin = [sb("vin%d" % i, [128, R, D], F32) for i in range(NBUF0)]
                uvo = [sb("uvo%d" % i, [128, R, 2 * D], BF16) for i in range(NBUF0)]
                b_uin = [Buf("uin%d" % i) for i in range(NBUF0)]
                b_vin = [Buf("vin%d" % i) for i in range(NBUF0)]
                b_uvo = [Buf("uvo%d" % i) for i in range(NBUF0)]
                uview = peer_u.rearrange("(p r) d -> p r d", p=128)
                vview = peer_v.rearrange("(p r) d -> p r d", p=128)
                oview = uv16.rearrange("(p r) d -> p r d", p=128)
                def loads(c):
                    k = c % NBUF0
                    P.add("sp", lambda e, k=k, c=c: e.dma_start(out=uin[k][:], in_=uview[:, c * R:(c + 1) * R, :]), writes=[b_uin[k]], dma=True)
                    P.add("act", lambda e, k=k, c=c: e.dma_start(out=vin[k][:], in_=vview[:, c * R:(c + 1) * R, :]), writes=[b_vin[k]], dma=True)

                for c in range(min(NBUF0 - 1, NCH)):
                    loads(c)
                for c in range(NCH):
                    k = c % NBUF0
                    if c + NBUF0 - 1 < NCH:
                        loads(c + NBUF0 - 1)
                    P.add("dve", lambda e, k=k: e.tensor_copy(uvo[k][:, :, 0:D], uin[k][:]), reads=[b_uin[k]], writes=[b_uvo[k]])
                    P.add("act", lambda e, k=k: e.copy(uvo[k][:, :, D:2 * D], vin[k][:]), reads=[b_vin[k]], writes=[b_uvo[k]])
                    P.add("sp", lambda e, k=k, c=c: e.dma_start(out=oview[:, c * R:(c + 1) * R, :], in_=uvo[k][:]), reads=[b_uvo[k]],
                          writes=[b_uv16], dma=True)
                P.add("sp", [], reads=[b_uv16])
                P.emit()

        def phase2():
            NT2 = 32 if NT >= NT_FULL else NT - 1
            GS = 4
            with ExitStack() as s2:
                tot = [0]

                def sb(name, shape, dt):
                    n = 1
                    for x in shape[1:]:
                        n *= x
                    tot[0] += n * (4 if dt in (F32, I32, U32) else 2)
                    if os.environ.get("KERNEL_SBDBG"):
                        print("sbuf p2", name, tot[0])
                    return s2.enter_context(nc.sbuf_tensor("b_" + name, list(shape), dt))

                gcyc = [4, 5, 6, 7]
                gpos = [0]

                def gbank():
                    b = gcyc[gpos[0] % 4]
                    gpos[0] += 1
                    return b

                ident_f = sb("ident_f", [128, 128], F32)
                ident_b = sb("ident_b", [128, 128], BF16)
                iota16 = sb("iota16", [128, 16], F32)
                gffn = sb("gffn", [128, D], F32)
                gfin = sb("gfin", [128, D], F32)
                Wpq = sb("Wpq", [128, 8, D], BF16)
                KB = sb("KB", [128, 8, 256], BF16)
                kst = sb("kst", [128, 128], F32)
                ot = [sb("ot%d" % i, [128, D], F32) for i in range(2)]
                b_ot = [Buf("ot0"), Buf("ot1")]
                stage = ot
                b_stage = b_ot
                C = {n: Buf(n) for n in "ident_f ident_b iota16 gffn gfin Wpq KB kst".split()}
                P.add("sp", lambda e: e.dma_start(out=ident_f[:], in_=ident_d), writes=[C["ident_f"]], dma=True)
                P.add("sp", lambda e: e.dma_start(out=iota16[:], in_=iota_d), writes=[C["iota16"]], dma=True)
                P.add("sp", lambda e: e.dma_start(out=gffn[:], in_=g_ffn.partition_broadcast(128)), writes=[C["gffn"]], dma=True)
                P.add("sp", lambda e: e.dma_start(out=gfin[:], in_=g_fin.partition_broadcast(128)), writes=[C["gfin"]], dma=True)
                P.add("dve", lambda e: e.tensor_copy(ident_b[:], ident_f[:]), reads=[C["ident_f"]], writes=[C["ident_b"]])
                P.add("dve", lambda e: e.memset(KB[:], 0.0), writes=[C["KB"]])
                for c in range(8):
                    k = c % 2
                    P.add("sp", lambda e, k=k, c=c: e.dma_start(out=stage[k][:], in_=peer_wq[c * 128:(c + 1) * 128, :]),
                          writes=[b_stage[k]], dma=True)
                    P.add("dve" if c % 2 == 0 else "act", cp_any("dve" if c % 2 == 0 else "act", Wpq[:, c, :], stage[k][:]),
                          reads=[b_stage[k]], writes=[C["Wpq"]])
                for h in range(8):
                    P.add("sp", lambda e, h=h: e.dma_start(out=kst[:].rearrange("n (p d) -> n p d", p=2),
                                                           in_=peer_keys[h].rearrange("p n d -> n p d")),
                          writes=[C["kst"]], dma=True)
                    bk = gbank()
                    P.add("pe", lambda e, bk=bk: e.transpose(psum[bk][:, 0:128], kst[:], ident_f[:]), reads=[C["kst"], C["ident_f"]],
                          writes=[pbuf[bk]])
                    P.add("dve", lambda e, bk=bk, h=h: e.tensor_copy(KB[0:64, h, 0:128], psum[bk][0:64, 0:128]), reads=[pbuf[bk]],
                          writes=[C["KB"]])
                    P.add("dve", lambda e, bk=bk, h=h: e.tensor_copy(KB[64:128, h, 128:256], psum[bk][64:128, 0:128]), reads=[pbuf[bk]],
                          writes=[C["KB"]])

                h2t = [sb("h2t%d" % i, [128, D], F32) for i in range(2)]
                xn = sb("xn", [128, D], F32)
                xnb = [sb("xnb%d" % i, [128, D], BF16) for i in range(2)]
                idx = [sb("idx%d" % i, [128, 128], I32) for i in range(2)]
                gate = [sb("gate%d" % i, [128, 8, 16], F32) for i in range(2)]
                st2 = sb("st2", [128, 8], F32)
                junkb = sb("junkb", [128, D], BF16)
                xnT = sb("xnT", [128, 8, 128], BF16)
                qpT = sb("qpT", [128, 8, 128], BF16)
                S = sb("S", [128, 16, 128], F32)
                S2 = sb("S2", [128, 16, 128], F32)
                sv = sb("sv", [128, 8, 2, 16], F32)
                si = sb("si", [128, 8, 2, 16], U32)
                sif = sb("sif", [128, 8, 2, 16], F32)
                cand = S[:].rearrange("p (h q) n -> p h (q n)", q=2)
                cand2 = S2[:].rearrange("p (h q) n -> p h (q n)", q=2)
                tv = sb("tv", [128, 8, 16], F32)
                tp = sb("tp", [128, 8, 16], U32)
                tpf = sb("tpf", [128, 8, 16], F32)
                cmp = cand2.rearrange("p h (a b) -> p h a b", a=16)
                abf = sb("abf", [128, 8, 2, 16], F32)
                oh = S2[:].rearrange("p g n -> p (g n)").bitcast(BF16).rearrange("p (g k x) -> p g k x", g=16, k=16)
                oh2 = S[:].rearrange("p g n -> p (g n)").bitcast(BF16).rearrange("p (g k x) -> p g k x", g=16, k=16)
                sel = sb("sel", [128, 8, 2, 16], F32)
                eidf = sb("eidf", [128, 8, 16], F32)
                dd = sb("dd", [128, 8, 16], F32)
                ee = sb("ee", [128, 8, 16], F32)
                zz = sb("zz", [128, 8], F32)
                act = sb("act", [128, 128], F32)
                ga = sb("ga", [128, 128], F32)
                wgt = sb("wgt", [128, 128], F32)
                junk = junkb
                junk2 = junkb
                NPROD = 8
                prod = [sb("prod%d" % i, [128, D], BF16) for i in range(NPROD)]
                b_prod = [Buf("prod%d" % i) for i in range(NPROD)]
                NDG = 8
                dg = [sb("dg%d" % i, [128, 128], BF16) for i in range(NDG)]
                b_dg = [Buf("dg%d" % i) for i in range(NDG)]
                yy = sb("yy", [128, D], F32)
                gb = [sb("gb%d" % i, [128, 2 * D], BF16) for i in range(NB)]
                b_gb = [Buf("gb%d" % i) for i in range(NB)]
                b_h2t = [Buf("h2t0"), Buf("h2t1")]
                b_xnb = [Buf("xnb0"), Buf("xnb1")]
                b_idx = [Buf("idx0"), Buf("idx1")]
                b_gate = [Buf("gate0"), Buf("gate1")]
                b_act = [Buf("act%d" % s) for s in range(128 // GS)]
                b_ga = [Buf("ga%d" % s) for s in range(128 // GS)]
                b_w = [Buf("w%d" % s) for s in range(128 // GS)]
                T = {n: Buf(n) for n in ("st2 junkb xn xnT qpT S S2 sv si sif cand cand2 tv tp tpf abf sel eidf dd ee zz yy").split()}
                T["cand"] = T["S"]
                T["cand2"] = T["S2"]
                T["cmp"] = T["S2"]
                T["oh"] = T["S2"]
                T["oh2"] = T["S"]
                thr16 = sb("thr16", [128, 16], F32)
                C["thr16"] = Buf("thr16")
                P.add("dve", lambda e: e.tensor_scalar(out=thr16[:], in0=iota16[:], scalar1=16.0, scalar2=None, op0=ALU.mult),
                      reads=[C["iota16"]], writes=[C["thr16"]])

                def prep(t):
                    H, bH = h2t[t % 2], b_h2t[t % 2]
                    XNB, bXNB = xnb[t % 2], b_xnb[t % 2]
                    IDX, bIDX = idx[t % 2], b_idx[t % 2]
                    GT, bGT = gate[t % 2], b_gate[t % 2]
                    r0 = NMETA + t * 128
                    P.add("sp", lambda e: e.dma_start(out=H[:], in_=h2s[r0:r0 + 128, :]), reads=[b_h2s[t], b_h2s[min(t + 1, NT_FULL - 1)]],
                          writes=[bH], dma=True)
                    yield
                    P.add("act", lambda e: e.activation(out=junkb[:], in_=H[:], func=AF.Square, accum_out=st2[:, 0:1]),
                          reads=[bH], writes=[T["junkb"], T["st2"]])
                    P.add("act", lambda e: e.activation(out=st2[:, 1:2], in_=st2[:, 0:1], func=AF.Sqrt, bias=EPS, scale=1.0 / D),
                          reads=[T["st2"]], writes=[T["st2"]])
                    yield
                    P.add("dve", lambda e: e.reciprocal(st2[:, 2:3], st2[:, 1:2]), reads=[T["st2"]], writes=[T["st2"]])
                    yield
                    P.add("dve", lambda e: e.scalar_tensor_tensor(out=xn[:], in0=H[:], scalar=st2[:, 2:3], in1=gffn[:], op0=ALU.mult,
                                                                  op1=ALU.mult), reads=[bH, T["st2"], C["gffn"]], writes=[T["xn"]])
                    yield
                    P.add("act", lambda e: e.copy(XNB[:], xn[:]), reads=[T["xn"]], writes=[bXNB])
                    bk = gbank()
                    psT = psum[bk][:].bitcast(BF16)
                    P.add("pe", [lambda e, c=c, psT=psT: e.transpose(psT[:, c * 128:(c + 1) * 128], XNB[:, c * 128:(c + 1) * 128], ident_b[:])
                                 for c in range(8)], reads=[bXNB, C["ident_b"]], writes=[pbuf[bk]])
                    P.add("act", lambda e, psT=psT: e.copy(xnT[:].rearrange("p c t -> p (c t)"), psT[:, 0:1024]), reads=[pbuf[bk]],
                          writes=[T["xnT"]])
                    yield
                    for half in range(2):
                        bk = gbank()
                        fns = []
                        for hh in range(4):
                            f = half * 4 + hh
                            for c in range(8):
                                fns.append(lambda e, bk=bk, hh=hh, f=f, c=c: e.matmul(
                                    psum[bk][:, hh * 128:(hh + 1) * 128], Wpq[:, c, f * 128:(f + 1) * 128], xnT[:, c, :],
                                    start=(hh == 0 and c == 0), stop=(hh == 3 and c == 7)))
                        P.add("pe", fns, reads=[C["Wpq"], T["xnT"]], writes=[pbuf[bk]])
                        P.add("act", lambda e, bk=bk, half=half: e.copy(qpT[:, half * 4:(half + 1) * 4, :].rearrange("p h t -> p (h t)"),
                                                                        psum[bk][:, :]), reads=[pbuf[bk]], writes=[T["qpT"]])
                        yield
                    for q4 in range(4):
                        bk = gbank()
                        P.add("pe", [lambda e, bk=bk, q4=q4, u=u: e.matmul(psum[bk][:, u * 256:(u + 1) * 256], qpT[:, q4 * 2 + u, :],
                                                                          KB[:, q4 * 2 + u, :], start=(u == 0), stop=(u == 1))
                                     for u in range(2)], reads=[T["qpT"], C["KB"]], writes=[pbuf[bk]])
                        P.add("act", lambda e, bk=bk, q4=q4: e.copy(S[:, q4 * 4:(q4 + 1) * 4, :].rearrange("p g n -> p (g n)"), psum[bk][:, :]),
                              reads=[pbuf[bk]], writes=[T["S"]])
                        yield
                    for g in range(16):
                        h, p = g // 2, g % 2
                        P.add("dve", lambda e, g=g, h=h, p=p: e.max(out=sv[:, h, p, 0:8], in_=S[:, g, :]), reads=[T["S"]], writes=[T["sv"]])
                        P.add("dve", lambda e, g=g, h=h, p=p: e.max_index(out=si[:, h, p, 0:8], in_max=sv[:, h, p, 0:8], in_values=S[:, g, :]),
                              reads=[T["S"], T["sv"]], writes=[T["si"]])
                        yield
                        P.add("dve", lambda e, g=g, h=h, p=p: e.match_replace(out=S2[:, g, :], in_to_replace=sv[:, h, p, 0:8],
                                                                              in_values=S[:, g, :], imm_value=-1e30),
                              reads=[T["S"], T["sv"]], writes=[T["S2"]])
                        P.add("dve", lambda e, g=g, h=h, p=p: e.max(out=sv[:, h, p, 8:16], in_=S2[:, g, :]), reads=[T["S2"]], writes=[T["sv"]])
                        yield
                        P.add("dve", lambda e, g=g, h=h, p=p: e.max_index(out=si[:, h, p, 8:16], in_max=sv[:, h, p, 8:16], in_values=S2[:, g, :]),
                              reads=[T["S2"], T["sv"]], writes=[T["si"]])
                        yield
                    P.add("dve", lambda e: e.tensor_tensor(out=cand.rearrange("p h (a b) -> p h a b", a=16),
                                                           in0=sv[:, :, 0, :].unsqueeze(3).broadcast_to([128, 8, 16, 16]),
                                                           in1=sv[:, :, 1, :].unsqueeze(2).broadcast_to([128, 8, 16, 16]), op=ALU.add),
                          reads=[T["sv"]], writes=[T["cand"]])
                    yield
                    for h in range(8):
                        P.add("dve", lambda e, h=h: e.max(out=tv[:, h, 0:8], in_=cand[:, h, :]), reads=[T["cand"]], writes=[T["tv"]])
                        P.add("dve", lambda e, h=h: e.max_index(out=tp[:, h, 0:8], in_max=tv[:, h, 0:8], in_values=cand[:, h, :]),
                              reads=[T["cand"], T["tv"]], writes=[T["tp"]])
                        yield
                        P.add("dve", lambda e, h=h: e.match_replace(out=cand2[:, h, :], in_to_replace=tv[:, h, 0:8], in_values=cand[:, h, :],
                                                                    imm_value=-1e30), reads=[T["cand"], T["tv"]], writes=[T["cand2"]])
                        P.add("dve", lambda e, h=h: e.max(out=tv[:, h, 8:16], in_=cand2[:, h, :]), reads=[T["cand2"]], writes=[T["tv"]])
                        yield
                        P.add("dve", lambda e, h=h: e.max_index(out=tp[:, h, 8:16], in_max=tv[:, h, 8:16], in_values=cand2[:, h, :]),
                              reads=[T["cand2"], T["tv"]], writes=[T["tp"]])
                        yield
                    P.add("dve", lambda e: e.tensor_tensor(out=dd[:], in0=tv[:], in1=tv[:, :, 0:1].broadcast_to([128, 8, 16]), op=ALU.subtract),
                          reads=[T["tv"]], writes=[T["dd"]])
                    P.add("act", lambda e: e.activation(out=ee[:], in_=dd[:], func=AF.Exp), reads=[T["dd"]], writes=[T["ee"]])
                    yield
                    P.add("dve", lambda e: e.tensor_reduce(out=zz[:], in_=ee[:], axis=AX.X, op=ALU.add), reads=[T["ee"]], writes=[T["zz"]])
                    P.add("dve", lambda e: e.reciprocal(zz[:], zz[:]), reads=[T["zz"]], writes=[T["zz"]])
                    yield
                    P.add("dve", lambda e: e.tensor_tensor(out=GT[:], in0=ee[:], in1=zz[:].unsqueeze(2).broadcast_to([128, 8, 16]), op=ALU.mult),
                          reads=[T["ee"], T["zz"]], writes=[bGT])
                    yield
                    P.add("dve", lambda e: e.tensor_copy(tpf[:], tp[:]), reads=[T["tp"]], writes=[T["tpf"]])
                    P.add("dve", lambda e: e.tensor_copy(sif[:], si[:]), reads=[T["si"]], writes=[T["sif"]])
                    yield
                    P.add("dve", lambda e: e.tensor_tensor(out=cmp, in0=tpf[:].unsqueeze(3).broadcast_to([128, 8, 16, 16]),
                                                           in1=thr16[:, :].unsqueeze(1).unsqueeze(1).broadcast_to([128, 8, 16, 16]),
                                                           op=ALU.is_ge), reads=[T["tpf"], C["thr16"]], writes=[T["cmp"]])
                    yield
                    P.add("dve", lambda e: e.tensor_reduce(out=abf[:, :, 0, :], in_=cmp, axis=AX.X, op=ALU.add), reads=[T["cmp"]],
                          writes=[T["abf"]])
                    yield
                    P.add("dve", lambda e: e.tensor_scalar(out=abf[:, :, 0, :], in0=abf[:, :, 0, :], scalar1=-1.0, scalar2=None, op0=ALU.add),
                          reads=[T["abf"]], writes=[T["abf"]])
                    P.add("dve", lambda e: e.scalar_tensor_tensor(out=abf[:, :, 1, :], in0=abf[:, :, 0, :], scalar=-16.0, in1=tpf[:],
                                                                  op0=ALU.mult, op1=ALU.add), reads=[T["abf"], T["tpf"]], writes=[T["abf"]])
                    yield
                    P.add("dve", lambda e: e.tensor_tensor(out=oh, in0=iota16[:, :].unsqueeze(1).unsqueeze(1).broadcast_to([128, 16, 16, 16]),
                                                           in1=abf[:].rearrange("p h q k -> p (h q) k").unsqueeze(3).broadcast_to([128, 16, 16, 16]),
                                                           op=ALU.is_equal), reads=[C["iota16"], T["abf"]], writes=[T["oh"]])
                    yield
                    P.add("dve", lambda e: e.tensor_tensor(out=oh2, in0=oh,
                                                           in1=sif[:].rearrange("p h q x -> p (h q) x").unsqueeze(2).broadcast_to([128, 16, 16, 16]),
                                                           op=ALU.mult), reads=[T["oh"], T["sif"]], writes=[T["oh2"]])
                    yield
                    P.add("dve", lambda e: e.tensor_reduce(out=sel[:].rearrange("p h q k -> p (h q) k"), in_=oh2, axis=AX.X, op=ALU.add),
                          reads=[T["oh2"]], writes=[T["sel"]])
                    yield
                    P.add("dve", lambda e: e.scalar_tensor_tensor(out=eidf[:], in0=sel[:, :, 0, :], scalar=128.0, in1=sel[:, :, 1, :],
                                                                  op0=ALU.mult, op1=ALU.add), reads=[T["sel"]], writes=[T["eidf"]])
                    P.add("dve", lambda e: e.tensor_copy(IDX[:].rearrange("p (h k) -> p h k", h=8), eidf[:]), reads=[T["eidf"]], writes=[bIDX])
                    yield

                jobs = [(t, s) for t in range(NT2) for s in range(128)]
                nprod = [0]
                ndg = [0]

                prep_gen = {}

                def gather(jn):
                    t, s = jobs[jn]
                    if t in prep_gen:
                        for _ in prep_gen.pop(t):
                            pass
                    k = jn % NB
                    IDX, bIDX = idx[t % 2], b_idx[t % 2]
                    P.add("pool", lambda e: e.indirect_dma_start(out=gb[k][:, :], out_offset=None, in_=uv16,
                                                                 in_offset=bass.IndirectOffsetOnAxis(ap=IDX[:, s:s + 1], axis=0)),
                          reads=[bIDX, b_uv16], writes=[b_gb[k]], dma=True)

                def final_ops(t):
                    H, bH = h2t[t % 2], b_h2t[t % 2]
                    OT, bOT = ot[t % 2], b_ot[t % 2]
                    bankA, bankB = (0, 1) if t % 2 == 0 else (2, 3)
                    P.add("dve", lambda e: e.tensor_tensor(out=yy[:, 0:512], in0=psum[bankA][:, :], in1=H[:, 0:512], op=ALU.add),
                          reads=[pbuf[bankA], bH], writes=[T["yy"]])
                    P.add("dve", lambda e: e.tensor_tensor(out=yy[:, 512:1024], in0=psum[bankB][:, :], in1=H[:, 512:1024], op=ALU.add),
                          reads=[pbuf[bankB], bH], writes=[T["yy"]])
                    P.add("act", lambda e: e.activation(out=junkb[:], in_=yy[:], func=AF.Square, accum_out=st2[:, 4:5]),
                          reads=[T["yy"]], writes=[T["junkb"], T["st2"]])
                    P.add("act", lambda e: e.activation(out=st2[:, 5:6], in_=st2[:, 4:5], func=AF.Sqrt, bias=EPS, scale=1.0 / D),
                          reads=[T["st2"]], writes=[T["st2"]])
                    P.add("dve", lambda e: e.reciprocal(st2[:, 6:7], st2[:, 5:6]), reads=[T["st2"]], writes=[T["st2"]])
                    P.add("dve", lambda e: e.scalar_tensor_tensor(out=OT[:], in0=yy[:], scalar=st2[:, 6:7], in1=gfin[:], op0=ALU.mult,
                                                                  op1=ALU.mult), reads=[T["yy"], T["st2"], C["gfin"]], writes=[bOT])
                    P.add("sp", lambda e: e.dma_start(out=out[t * 128:(t + 1) * 128, :], in_=OT[:]), reads=[bOT], writes=[b_out], dma=True)


                def consume_tile(t):
                    XNB, bXNB = xnb[t % 2], b_xnb[t % 2]
                    GT, bGT = gate[t % 2], b_gate[t % 2]
                    H, bH = h2t[t % 2], b_h2t[t % 2]
                    OT, bOT = ot[t % 2], b_ot[t % 2]
                    bankA, bankB = (0, 1) if t % 2 == 0 else (2, 3)
                    base = t * 128
                    NG = 128 // GS

                    def dots_mul(g):
                        qs = []
                        for s in range(g * GS, (g + 1) * GS):
                            k = (base + s) % NB
                            q = nprod[0] % NPROD
                            nprod[0] += 1
                            qs.append(q)
                            P.add("dve", lambda e, k=k, q=q: e.tensor_tensor(out=prod[q][:], in0=gb[k][:, 0:D], in1=XNB[:], op=ALU.mult),
                                  reads=[b_gb[k], bXNB], writes=[b_prod[q]])
                        return qs

                    def dots_acc(g, qs):
                        for i_, s in enumerate(range(g * GS, (g + 1) * GS)):
                            q = qs[i_]
                            P.add("act", lambda e, s=s, q=q: e.activation(out=junk2[:], in_=prod[q][:], func=AF.Copy,
                                                                          accum_out=act[:, s:s + 1]),
                                  reads=[b_prod[q]], writes=[b_act[g]])

                    qs0 = dots_mul(0)
                    dots_acc(0, qs0)
                    yield
                    for g in range(NG):
                        s0 = g * GS
                        qs = dots_mul(g + 1) if g + 1 < NG else None
                        P.add("act", lambda e, s0=s0: e.activation(out=ga[:, s0:s0 + GS], in_=act[:, s0:s0 + GS], func=AF.Gelu),
                              reads=[b_act[g]], writes=[b_ga[g]])
                        if qs is not None:
                            dots_acc(g + 1, qs)
                        yield
                        P.add("dve", lambda e, s0=s0: e.tensor_tensor(out=wgt[:, s0:s0 + GS], in0=ga[:, s0:s0 + GS],
                                                                      in1=GT[:].rearrange("p h k -> p (h k)")[:, s0:s0 + GS], op=ALU.mult),
                              reads=[b_ga[g], bGT], writes=[b_w[g]])
                        for s in range(s0, s0 + GS):
                            k = (base + s) % NB
                            r = ndg[0] % NDG
                            ndg[0] += 1
                            P.add("dve", lambda e, s=s, r=r: e.tensor_tensor(out=dg[r][:], in0=ident_b[:],
                                                                             in1=wgt[:, s:s + 1].broadcast_to([128, 128]), op=ALU.mult),
                                  reads=[b_w[g], C["ident_b"]], writes=[b_dg[r]])
                            P.add("pe", [lambda e, s=s, r=r, k=k: e.matmul(psum[bankA][:, :], dg[r][:], gb[k][:, D:D + 512],
                                                                           start=(s == 0), stop=(s == 127)),
                                         lambda e, s=s, r=r, k=k: e.matmul(psum[bankB][:, :], dg[r][:], gb[k][:, D + 512:2 * D],
                                                                           start=(s == 0), stop=(s == 127))],
                                  reads=[b_dg[r], b_gb[k]], writes=[pbuf[bankA], pbuf[bankB]])
                            jn = base + s
                            if jn + NB < len(jobs):
                                gather(jn + NB)
                        yield
                    final_ops(t)
                    yield

                for _ in prep(0):
                    pass
                for jn in range(min(NB, len(jobs))):
                    gather(jn)
                for t in range(NT2):
                    side = None
                    if t + 1 < NT2:
                        side = prep(t + 1)
                        prep_gen[t + 1] = side
                    _interleave(consume_tile(t), side, 60, 100)
                P.add("sp", [], reads=[b_out])
                P.emit()

        if 0 in phases:
            phase0()
        if 1 in phases:
            phase1()
        if 2 in phases:
            phase2()
    return nc


def _consts():
    half = 16
    freqs = (10000.0 ** (-np.arange(half, dtype=np.float32) / half)).astype(np.float32)
    pos = np.arange(TP, dtype=np.float32)
    ang = pos[:, None] * freqs[None, :]
    cos = np.cos(ang).astype(np.float32).T
    sin = np.sin(ang).astype(np.float32).T
    cos32 = np.concatenate([cos, cos], axis=0)
    sin32 = np.concatenate([-sin, sin], axis=0)
    rope = np.zeros((NT_FULL, 32, 256), np.float32)
    for i in range(NT_FULL):
        rope[i, :, 0:128] = cos32[:, i * 128:(i + 1) * 128]
        rope[i, :, 128:256] = sin32[:, i * 128:(i + 1) * 128]
    kk = np.arange(128)
    mask = (kk[:, None] <= kk[None, :]).astype(np.float32)
    iota16 = np.tile(np.arange(16, dtype=np.float32)[None, :], (128, 1))
    return rope, mask, iota16, np.eye(128, dtype=np.float32)


def _in_maps(inputs, n_cores=8):
    f = lambda a: np.ascontiguousarray(np.asarray(a, dtype=np.float32))
    x = f(inputs["x"])
    meta = f(inputs["meta"])
    rope, mask, iota16, ident = _consts()
    vecs = np.concatenate([
        f(inputs["g_mix_norm"])[0].reshape(8, 128), f(inputs["g_q"])[0].reshape(2, 128), f(inputs["g_kv"])[0].reshape(1, 128),
        f(inputs["conv_b"])[0].reshape(4, 128), f(inputs["g_conv_ln"])[0].reshape(4, 128), f(inputs["b_conv_ln"])[0].reshape(4, 128),
        f(inputs["g_out_attn"])[0].reshape(4, 128), f(inputs["g_out_conv"])[0].reshape(4, 128)], axis=0)
    shared = {
        "w_in": f(inputs["w_in"])[0], "w_uq": f(inputs["w_uq"])[0], "w_ukv": f(inputs["w_ukv"])[0],
        "conv_w": f(inputs["conv_w"])[0], "w_out": f(inputs["w_out"])[0], "peer_wq": f(inputs["peer_wq"])[0],
        "peer_keys": f(inputs["peer_keys"])[0], "peer_u": f(inputs["peer_u"])[0], "peer_v": f(inputs["peer_v"])[0],
        "vecs": np.ascontiguousarray(vecs), "g_ffn": f(inputs["g_ffn_norm"])[0].reshape(1, D), "g_fin": f(inputs["g_final"]).reshape(1, D),
        "ident": ident, "rope": rope, "mask": mask, "iota16": iota16,
    }
    maps = []
    for b in range(n_cores):
        h0 = np.zeros((TP, D), np.float32)
        h0[:NMETA] = meta
        h0[NMETA:NMETA + SEQ] = x[b]
        m = dict(shared)
        m["h0"] = h0
        maps.append(m)
    return maps


def kernel(**inputs):
    nt = int(os.environ.get("KERNEL_NT", NT_FULL))
    nc = build_program(NT=nt)
    maps = _in_maps(inputs)
    res = run_bass_kernel_spmd(nc, maps, core_ids=list(range(8)))
    return np.stack([np.asarray(r["out"], dtype=np.float32) for r in res.results], axis=0)
```
